# Optimizing a Trainium2 kernel written in Bass

```python
import math
import jax, jax.numpy as jnp
from jax import lax
import numpy as np

D_MODEL = 1024
BATCH = 8
SEQ = 4096
DEPTH = 4

N_A_LAYERS = DEPTH // 2
N_B_LAYERS = DEPTH - N_A_LAYERS
SSM_GROUP = 16
N_GROUPS = D_MODEL // SSM_GROUP
SSM_STATE = 64
DT_MIN = 1e-3
DT_MAX = 1e-1
N_HEADS = 16
HEAD_DIM = D_MODEL // N_HEADS
Q_BLOCK = 128
ATTN_SCALE = HEAD_DIM ** -0.5
D_FF = ((8 * D_MODEL // 3 + 127) // 128) * 128
CONV_W = 3
EPS = 1e-6

kernel_name = "yoco_s5_fox_convffn_trunk"


def rmsnorm(x, g):
    xf = x.astype(jnp.float32)
    y = xf * lax.rsqrt(jnp.mean(xf * xf, axis=-1, keepdims=True) + EPS) * g.astype(jnp.float32)
    return y.astype(x.dtype)


def causal_dwconv(h, w, b):
    L = h.shape[1]
    hp = jnp.pad(h, ((0, 0), (CONV_W - 1, 0), (0, 0)))
    y = b
    for k in range(CONV_W):
        y = y + hp[:, k:k + L, :] * w[k]
    return y


def conv_ffn(h, w_in, conv_w, conv_b, w_out):
    u = causal_dwconv(h @ w_in, conv_w, conv_b)
    gate, up = jnp.split(u, 2, axis=-1)
    return (jax.nn.silu(gate) * up) @ w_out


def _ssm_combine(e_i, e_j):
    ai_re, ai_im, bi_re, bi_im = e_i
    aj_re, aj_im, bj_re, bj_im = e_j
    a_re = aj_re * ai_re - aj_im * ai_im
    a_im = aj_re * ai_im + aj_im * ai_re
    b_re = aj_re * bi_re - aj_im * bi_im + bj_re
    b_im = aj_re * bi_im + aj_im * bi_re + bj_im
    return (a_re, a_im, b_re, b_im)


def s5_mixer(h, lam_re, lam_im, log_dt, b_re, b_im, c_re, c_im, d_skip, w_glu):
    dtype = h.dtype
    bsz, L, _ = h.shape
    f32 = jnp.float32
    u = h.astype(f32).reshape(bsz, L, N_GROUPS, SSM_GROUP)
    lr = lam_re.astype(f32)
    li = lam_im.astype(f32)
    dt = jnp.exp(log_dt.astype(f32))[:, None]
    mag = jnp.exp(lr * dt)
    lb_re = mag * jnp.cos(li * dt)
    lb_im = mag * jnp.sin(li * dt)
    den = lr * lr + li * li
    nr = lb_re - 1.0
    fr = ((nr * lr + lb_im * li) / den)[..., None]
    fi = ((lb_im * lr - nr * li) / den)[..., None]
    br = b_re.astype(f32)
    bi = b_im.astype(f32)
    bb_re = fr * br - fi * bi
    bb_im = fr * bi + fi * br
    bu_re = jnp.einsum('blgh,gph->blgp', u, bb_re)
    bu_im = jnp.einsum('blgh,gph->blgp', u, bb_im)
    a_re = jnp.broadcast_to(lb_re[None, None], (1, L, N_GROUPS, SSM_STATE))
    a_im = jnp.broadcast_to(lb_im[None, None], (1, L, N_GROUPS, SSM_STATE))
    _, _, s_re, s_im = lax.associative_scan(_ssm_combine, (a_re, a_im, bu_re, bu_im), axis=1)
    y = (jnp.einsum('blgp,ghp->blgh', s_re, c_re.astype(f32))
         - jnp.einsum('blgp,ghp->blgh', s_im, c_im.astype(f32)))
    y = y.reshape(bsz, L, D_MODEL) + d_skip.astype(f32) * u.reshape(bsz, L, D_MODEL)
    y = jax.nn.gelu(y)
    z_a, z_g = jnp.split(y @ w_glu.astype(f32), 2, axis=-1)
    return (z_a * jax.nn.sigmoid(z_g)).astype(dtype)


def fox_shared_kv(h_kv, w_kvf, b_f):
    bsz, L, _ = h_kv.shape
    z = h_kv @ w_kvf
    k = z[..., :D_MODEL].reshape(bsz, L, N_HEADS, HEAD_DIM)
    v = z[..., D_MODEL:2 * D_MODEL].reshape(bsz, L, N_HEADS, HEAD_DIM)
    f_logit = z[..., 2 * D_MODEL:].astype(jnp.float32) + b_f.astype(jnp.float32)
    cum = jnp.cumsum(jax.nn.log_sigmoid(f_logit), axis=1)
    return k, v, cum


def fox_attention(h, w_q, w_o, k, v, cum):
    dtype = h.dtype
    bsz, L, _ = h.shape
    nb = L // Q_BLOCK
    f32 = jnp.float32
    q = (h @ w_q).reshape(bsz, nb, Q_BLOCK, N_HEADS, HEAD_DIM).transpose(1, 0, 2, 3, 4)
    cq = cum.reshape(bsz, nb, Q_BLOCK, N_HEADS).transpose(1, 0, 2, 3)
    kf = k.astype(f32)
    vf = v.astype(f32)
    ck = cum.transpose(0, 2, 1)[:, :, None, :]
    kpos = jnp.arange(L, dtype=jnp.int32)
    starts = jnp.arange(nb, dtype=jnp.int32) * Q_BLOCK

    def one_block(args):
        qb, cqb, start = args
        s = jnp.einsum('bqhd,bkhd->bhqk', qb.astype(f32), kf) * ATTN_SCALE
        s = s + cqb.transpose(0, 2, 1)[..., None] - ck
        qpos = start + jnp.arange(Q_BLOCK, dtype=jnp.int32)
        mask = kpos[None, :] <= qpos[:, None]
        s = jnp.where(mask, s, -jnp.inf)
        p = jax.nn.softmax(s, axis=-1)
        return jnp.einsum('bhqk,bkhd->bqhd', p, vf)

    o = lax.map(one_block, (q, cq, starts))
    o = o.transpose(1, 0, 2, 3, 4).reshape(bsz, L, D_MODEL)
    return o.astype(dtype) @ w_o


def setup_inputs(seed: int = 0) -> dict:
    key = jax.random.key(seed)
    ks = jax.random.split(key, 24)
    nrm = jax.random.normal
    D, G, P, H, F = D_MODEL, N_GROUPS, SSM_STATE, SSM_GROUP, D_FF
    x = nrm(ks[0], (BATCH, SEQ, D), jnp.float32)
    g_mix = 1.0 + 0.02 * nrm(ks[1], (DEPTH, D), jnp.float32)
    g_ffn = 1.0 + 0.02 * nrm(ks[2], (DEPTH, D), jnp.float32)
    lam_re = -0.5 + 0.01 * nrm(ks[3], (N_A_LAYERS, G, P), jnp.float32)
    lam_im = (math.pi * jnp.arange(P, dtype=jnp.float32))[None, None, :] + 0.01 * nrm(ks[4], (N_A_LAYERS, G, P), jnp.float32)
    log_dt = jax.random.uniform(ks[5], (N_A_LAYERS, G), jnp.float32, math.log(DT_MIN), math.log(DT_MAX))
    ssm_b_re = nrm(ks[6], (N_A_LAYERS, G, P, H), jnp.float32) * (2 * H) ** -0.5
    ssm_b_im = nrm(ks[7], (N_A_LAYERS, G, P, H), jnp.float32) * (2 * H) ** -0.5
    ssm_c_re = nrm(ks[8], (N_A_LAYERS, G, H, P), jnp.float32) * P ** -0.5
    ssm_c_im = nrm(ks[9], (N_A_LAYERS, G, H, P), jnp.float32) * P ** -0.5
    ssm_d = nrm(ks[10], (N_A_LAYERS, D), jnp.float32)
    w_glu = nrm(ks[11], (N_A_LAYERS, D, 2 * D), jnp.float32) * D ** -0.5
    g_kv = 1.0 + 0.02 * nrm(ks[12], (D,), jnp.float32)
    w_kvf = nrm(ks[13], (D, 2 * D + N_HEADS), jnp.float32) * D ** -0.5
    b_f = 2.0 + 0.5 * nrm(ks[14], (N_HEADS,), jnp.float32)
    w_q = nrm(ks[15], (N_B_LAYERS, D, D), jnp.float32) * D ** -0.5
    w_o = nrm(ks[16], (N_B_LAYERS, D, D), jnp.float32) * D ** -0.5
    w_ffn_in = nrm(ks[17], (DEPTH, D, 2 * F), jnp.float32) * D ** -0.5
    ffn_conv_w = nrm(ks[18], (DEPTH, CONV_W, 2 * F), jnp.float32) * CONV_W ** -0.5
    ffn_conv_b = 0.01 * nrm(ks[19], (DEPTH, 2 * F), jnp.float32)
    w_ffn_out = nrm(ks[20], (DEPTH, F, D), jnp.float32) * F ** -0.5
    g_final = 1.0 + 0.02 * nrm(ks[21], (D,), jnp.float32)
    return {"x": x, "g_mix": g_mix, "g_ffn": g_ffn, "lam_re": lam_re, "lam_im": lam_im,
            "log_dt": log_dt, "ssm_b_re": ssm_b_re, "ssm_b_im": ssm_b_im, "ssm_c_re": ssm_c_re,
            "ssm_c_im": ssm_c_im, "ssm_d": ssm_d, "w_glu": w_glu, "g_kv": g_kv, "w_kvf": w_kvf,
            "b_f": b_f, "w_q": w_q, "w_o": w_o, "w_ffn_in": w_ffn_in, "ffn_conv_w": ffn_conv_w,
            "ffn_conv_b": ffn_conv_b, "w_ffn_out": w_ffn_out, "g_final": g_final}


def reference(x, g_mix, g_ffn, lam_re, lam_im, log_dt, ssm_b_re, ssm_b_im, ssm_c_re, ssm_c_im,
              ssm_d, w_glu, g_kv, w_kvf, b_f, w_q, w_o, w_ffn_in, ffn_conv_w, ffn_conv_b,
              w_ffn_out, g_final):
    h = x
    k = v = cum = None
    for layer in range(DEPTH):
        if layer < N_A_LAYERS:
            h = h + s5_mixer(rmsnorm(h, g_mix[layer]), lam_re[layer], lam_im[layer], log_dt[layer],
                             ssm_b_re[layer], ssm_b_im[layer], ssm_c_re[layer], ssm_c_im[layer],
                             ssm_d[layer], w_glu[layer])
        else:
            if layer == N_A_LAYERS:
                k, v, cum = fox_shared_kv(rmsnorm(h, g_kv), w_kvf, b_f)
            j = layer - N_A_LAYERS
            h = h + fox_attention(rmsnorm(h, g_mix[layer]), w_q[j], w_o[j], k, v, cum)
        h = h + conv_ffn(rmsnorm(h, g_ffn[layer]), w_ffn_in[layer], ffn_conv_w[layer],
                         ffn_conv_b[layer], w_ffn_out[layer])
    return rmsnorm(h, g_final)
```

```python
import numpy as np
from contextlib import ExitStack
import concourse.bass as bass
import concourse.mybir as mybir
from concourse.bass_utils import run_bass_kernel_spmd

F32 = mybir.dt.float32
BF16 = mybir.dt.bfloat16
ALU = mybir.AluOpType
AF = mybir.ActivationFunctionType

D = 1024
KT = 8
NCORES = 8
EPS = 1e-6
TB = 512


STRICT = True


WC = {}


class NCW:
    def __init__(self, nc):
        self._nc = nc
        self._n = 0

    def __getattr__(self, k):
        return getattr(self._nc, k)

    def _u(self, name):
        self._n += 1
        return f"{name}_{self._n}"

    def sbuf_tensor(self, name, shape, dt):
        return self._nc.sbuf_tensor(self._u(name), shape, dt)

    def psum_tensor(self, name, shape, dt):
        return self._nc.psum_tensor(self._u(name), shape, dt)


class WinPrefetch:
    def __init__(self, P, nc, w_in_d, layer, stack=None):
        self.P, self.nc, self.w_in_d, self.layer, self.stack = P, nc, w_in_d, layer, stack
        self.win = None
        self.todo = [0, 2, 1, 3]

    def alloc(self, stack=None):
        stack = self.stack if self.stack is not None else stack
        self.win = stack.enter_context(self.nc.sbuf_tensor("f_win", [128, KT, 2 * 2816], BF16))

    def load_some(self, n):
        for _ in range(min(n, len(self.todo))):
            c = self.todo.pop(0)
            self.P.dma("sp", self.win[:, :, c * 1408:(c + 1) * 1408],
                       self.w_in_d[self.layer, :, c * 1408:(c + 1) * 1408].rearrange("(k p) f -> p k f", p=128),
                       reads=WC[f"in{self.layer}"], writes=[f"win_{c}"])


class Prog:
    CE = ("act", "pe", "dve", "pool")

    def __init__(self, nc, es, ring=12):
        self.nc = nc
        self.eng = {"sp": nc.sync, "act": nc.scalar, "pe": nc.tensor, "dve": nc.vector, "pool": nc.gpsimd}
        self.sem = {}
        for e in self.CE:
            self.sem["c_" + e] = es.enter_context(nc.semaphore("c_" + e))
        self.K = ring
        self.ringn = {}
        self.KR = {"sp": ring, "act": ring, "pool": ring, "wc": 10}
        for qn in ("sp", "act", "pool", "wc"):
            self.ringn[qn] = 0
            for k in range(self.KR[qn]):
                self.sem[f"d_{qn}_{k}"] = es.enter_context(nc.semaphore(f"d_{qn}_{k}"))
        self.cnt = {e: 0 for e in self.CE}
        self.res = {}
        self.seen = {e: {} for e in self.eng}
        self.last = {}
        self.nins = 0

    def _wait(self, eng, deps):
        best = {}
        for (s, v) in deps:
            if v > best.get(s, 0):
                best[s] = v
        for s, v in best.items():
            if self.seen[eng].get(s, 0) < v:
                self.eng[eng].wait_ge(self.sem[s], v)
                self.seen[eng][s] = v
                self.nins += 1

    def _deps(self, ident, reads, writes):
        deps = []
        for r in reads:
            st = self.res.get(r)
            if st and st[0] is not None:
                tok, who = st[0]
                if who == ident:
                    if ident != "pe":
                        deps.append(tok)
                else:
                    deps.append(tok)
        for w in writes:
            st = self.res.get(w)
            if st:
                if st[0] is not None:
                    tok, who = st[0]
                    if who != ident or (STRICT and ident != "pe"):
                        deps.append(tok)
                for (tok, who) in st[1]:
                    if who != ident or (STRICT and ident != "pe"):
                        deps.append(tok)
        return deps

    def _record(self, tok, ident, reads, writes):
        for r in reads:
            st = self.res.setdefault(r, [None, []])
            st[1].append((tok, ident))
            if len(st[1]) > 64:
                best = {}
                for (t, w) in st[1]:
                    if t[1] >= best.get((t[0], w), (0, 0))[0]:
                        best[(t[0], w)] = (t[1], (t, w))
                st[1] = [v[1] for v in best.values()]
        for w in writes:
            self.res[w] = [(tok, ident), []]
        self.last[tok[0]] = tok[1]

    def op(self, eng, fn, reads=(), writes=()):
        deps = self._deps(eng, reads, writes)
        self._wait(eng, deps)
        self.cnt[eng] += 1
        tok = ("c_" + eng, self.cnt[eng])
        ins = fn(self.eng[eng])
        ins.then_inc(self.sem[tok[0]], 1)
        self.nins += 1
        self._record(tok, eng, reads, writes)

    def dma(self, qn, out, in_, reads=(), writes=(), ring=None, **kw):
        ring = ring or qn
        i = self.ringn[ring]
        self.ringn[ring] += 1
        K = self.KR[ring]
        k, m = i % K, i // K
        s = f"d_{ring}_{k}"
        ident = ("dma", ring, i)
        deps = self._deps(ident, reads, writes)
        if m > 0:
            deps.append((s, 16 * m))
        self._wait(qn, deps)
        ins = self.eng[qn].dma_start(out=out, in_=in_, **kw)
        ins.then_inc(self.sem[s], 16)
        self.nins += 1
        self._record((s, 16 * (m + 1)), ident, reads, writes)

    def barrier(self):
        allt = [(s_, v) for s_, v in self.last.items() if not s_.startswith("d_wc_")]
        for e in self.eng:
            self._wait(e, allt)
        self.res = {k: v for k, v in self.res.items() if k.startswith("WC:")}


def rms_block(P, x, xres, sq, rr, ps, ones, epsb, gs, out, ores, pfx, n):
    P.op("act", lambda e: e.activation(out=sq[:, :, :n], in_=x[:, :, :n], func=AF.Square),
         reads=[xres], writes=[pfx + "sq"])
    for kt in range(KT):
        P.op("pe", lambda e, kt=kt: e.matmul(ps[:, :n], lhsT=ones[:, :], rhs=sq[:, kt, :n],
                                             start=(kt == 0), stop=(kt == KT - 1)),
             reads=[pfx + "sq", "ones"], writes=[pfx + "ps"])
    P.op("act", lambda e: e.activation(out=rr[:, :n], in_=ps[:, :n], func=AF.Sqrt, scale=1.0 / D, bias=epsb[:, 0:1]),
         reads=[pfx + "ps", "epsb"], writes=[pfx + "rr"])
    P.op("dve", lambda e: e.reciprocal(out=rr[:, :n], in_=rr[:, :n]), reads=[pfx + "rr"], writes=[pfx + "rr"])
    for kt in range(KT):
        P.op("dve", lambda e, kt=kt: e.scalar_tensor_tensor(out=out[:, kt, :n], in0=x[:, kt, :n],
                                                           scalar=gs[:, kt:kt + 1], in1=rr[:, :n],
                                                           op0=ALU.mult, op1=ALU.mult),
             reads=[xres, pfx + "rr", "gs"], writes=[ores])


FF = 2816
NFT = 22
TBF = 256


def ffn_phase(P, nc, L, layer, HT, w_in_d, w_out_d, cw_d, ones, epsb, gs, pf=None):
    NBF = L // TBF
    with ExitStack() as ph:
        if pf is None:
            pf = WinPrefetch(P, nc, w_in_d, layer)
        if pf.win is None:
            pf.alloc(ph)
        win = pf.win
        wout = ph.enter_context(nc.sbuf_tensor("f_wout", [128, NFT, D], BF16))
        cw = ph.enter_context(nc.sbuf_tensor("f_cw", [128, 2 * NFT, 4], F32))
        halo = ph.enter_context(nc.sbuf_tensor("f_halo", [128, 2 * NFT, 2], F32))
        xb = [ph.enter_context(nc.sbuf_tensor(f"f_xb{i}", [128, KT, TBF], F32)) for i in range(3)]
        sq = ph.enter_context(nc.sbuf_tensor("f_sq", [128, KT, TBF], BF16))
        rr = ph.enter_context(nc.sbuf_tensor("f_rr", [128, TBF], F32))
        xn = [ph.enter_context(nc.sbuf_tensor(f"f_xn{i}", [128, KT, TBF], BF16)) for i in range(2)]
        act = [ph.enter_context(nc.sbuf_tensor(f"f_act{i}", [128, NFT, TBF], BF16)) for i in range(2)]
        NU = 4
        ue = [ph.enter_context(nc.sbuf_tensor(f"f_ue{i}", [128, TBF + 2], F32)) for i in range(NU)]
        yy = [ph.enter_context(nc.sbuf_tensor(f"f_yy{i}", [128, TBF], F32)) for i in range(NU)]
        sg = [ph.enter_context(nc.sbuf_tensor(f"f_sg{i}", [128, TBF], F32)) for i in range(2)]
        psn = ph.enter_context(nc.psum_tensor("f_psn", [128, 512], F32))
        psu = ph.enter_context(nc.psum_tensor("f_psu", [128, 4, 512], F32))
        pso = ph.enter_context(nc.psum_tensor("f_pso", [128, 2, 512], F32))

        P.dma("sp", cw[:, :, :], cw_d[layer], writes=["cw"])
        P.op("pool", lambda e: e.memset(halo[:, :, :], 0.0), writes=[f"halo{i}" for i in range(2 * NFT)])
        pf.load_some(4)
        P.dma("sp", wout[:, :, :], w_out_d[layer].rearrange("(f p) d -> p f d", p=128), reads=WC[f"out{layer}"],
              writes=[f"wout{ft}" for ft in range(NFT)])

        def load(b):
            s3 = b % 3
            t0 = b * TBF
            P.dma("sp", xb[s3][:, :, :], HT[:, :, t0:t0 + TBF].rearrange("k p t -> p k t"), writes=[f"fb{s3}x"])

        def norm(b):
            rms_block(P, xb[b % 3], f"fb{b % 3}x", sq, rr, psn, ones, epsb, gs[:, 4 + layer, :], xn[b % 2], f"fxn{b % 2}", "fn_", TBF)

        def phase2_items(b):
            s3, s2 = b % 3, b % 2
            t0 = b * TBF
            items = []
            for m in range(KT):
                bk = m % 2
                for ft in range(NFT):
                    items.append(lambda m=m, ft=ft, bk=bk: P.op("pe", lambda e: e.matmul(
                        pso[:, bk, :TBF], lhsT=wout[:, ft, m * 128:(m + 1) * 128], rhs=act[s2][:, ft, :],
                        start=(ft == 0), stop=(ft == NFT - 1)),
                        reads=[f"wout{ft}", f"act{s2}_{ft}"], writes=[f"pso{bk}"]))
                items.append(lambda m=m, bk=bk: P.op("dve", lambda e: e.tensor_tensor(
                    out=xb[s3][:, m, :], in0=xb[s3][:, m, :], in1=pso[:, bk, :TBF], op=ALU.add),
                    reads=[f"pso{bk}", f"fb{s3}x"], writes=[f"fb{s3}o{m}"]))
            items.append(lambda: P.dma("pool", HT[:, :, t0:t0 + TBF].rearrange("k p t -> p k t"), xb[s3][:, :, :],
                                       reads=[f"fb{s3}o{m}" for m in range(KT)] + [f"fb{s3}x"], writes=[f"HTo{b}"]))
            return items

        ui = 0
        load(0)
        norm(0)
        for b in range(NBF):
            s2 = b % 2
            if b + 1 < NBF:
                load(b + 1)
            p2 = phase2_items(b - 1) if b > 0 else []
            per = (len(p2) + NFT - 1) // NFT if p2 else 0
            for ft in range(NFT):
                bufs = []
                for gu in range(2):
                    f = ft + gu * NFT
                    bk = (2 * ft + gu) % 4
                    u = ui % NU
                    ui += 1
                    bufs.append(u)
                    for kt in range(KT):
                        P.op("pe", lambda e, kt=kt, f=f, bk=bk: e.matmul(
                            psu[:, bk, :TBF], lhsT=win[:, kt, f * 128:(f + 1) * 128], rhs=xn[s2][:, kt, :],
                            start=(kt == 0), stop=(kt == KT - 1)),
                            reads=[f"win_{f // 11}", f"fxn{s2}"], writes=[f"psu{bk}"])
                    P.op("pool", lambda e, u=u, f=f: e.tensor_copy(out=ue[u][:, 0:2], in_=halo[:, f, :]),
                         reads=[f"halo{f}"], writes=[f"ueh{u}"])
                    P.op("act", lambda e, u=u, bk=bk: e.copy(out=ue[u][:, 2:TBF + 2], in_=psu[:, bk, :TBF]),
                         reads=[f"psu{bk}"], writes=[f"ue{u}"])
                    P.op("act", lambda e, u=u, bk=bk, f=f: e.activation(
                        out=yy[u][:, :], in_=psu[:, bk, :TBF], func=AF.Identity,
                        scale=cw[:, f, 2:3], bias=cw[:, f, 3:4]),
                        reads=[f"psu{bk}", "cw"], writes=[f"yy{u}"])
                    P.op("pool", lambda e, u=u, f=f: e.tensor_copy(out=halo[:, f, :], in_=ue[u][:, TBF:TBF + 2]),
                         reads=[f"ue{u}"], writes=[f"halo{f}"])
                    P.op("dve", lambda e, u=u, f=f: e.scalar_tensor_tensor(
                        out=yy[u][:, :], in0=ue[u][:, 1:TBF + 1], scalar=cw[:, f, 1:2], in1=yy[u][:, :],
                        op0=ALU.mult, op1=ALU.add), reads=[f"ue{u}", f"ueh{u}", f"yy{u}", "cw"], writes=[f"yy{u}"])
                    P.op("dve", lambda e, u=u, f=f: e.scalar_tensor_tensor(
                        out=yy[u][:, :], in0=ue[u][:, 0:TBF], scalar=cw[:, f, 0:1], in1=yy[u][:, :],
                        op0=ALU.mult, op1=ALU.add), reads=[f"ue{u}", f"ueh{u}", f"yy{u}", "cw"], writes=[f"yy{u}"])
                ug, uu = bufs
                sgi = ft % 2
                P.op("act", lambda e, ug=ug, sgi=sgi: e.activation(out=sg[sgi][:, :], in_=yy[ug][:, :], func=AF.Silu),
                     reads=[f"yy{ug}"], writes=[f"sg{sgi}"])
                P.op("dve", lambda e, uu=uu, sgi=sgi, ft=ft: e.tensor_tensor(
                    out=act[s2][:, ft, :], in0=sg[sgi][:, :], in1=yy[uu][:, :], op=ALU.mult),
                    reads=[f"sg{sgi}", f"yy{uu}"], writes=[f"act{s2}_{ft}"])
                for it in p2[ft * per:(ft + 1) * per]:
                    it()
                if ft == 8 and b + 1 < NBF:
                    norm(b + 1)
        for it in phase2_items(NBF - 1):
            it()
        P.barrier()


NH = 16
HD = 64
KA = 70


def kv_phase(P, nc, L, HT, w_kvf_d, bf_d, KTd, QAd, VTd, ones, epsb, gs):
    NB = L // TB
    with ExitStack() as ph:
        wk = ph.enter_context(nc.sbuf_tensor("k_w", [128, KT, 2064], BF16))
        xb = [ph.enter_context(nc.sbuf_tensor(f"k_xb{i}", [128, KT, TB], F32)) for i in range(2)]
        sq = ph.enter_context(nc.sbuf_tensor("k_sq", [128, KT, TB], BF16))
        rr = ph.enter_context(nc.sbuf_tensor("k_rr", [128, TB], F32))
        xnb = [ph.enter_context(nc.sbuf_tensor(f"k_xn{i}", [128, KT, TB], BF16)) for i in range(2)]
        kst = [ph.enter_context(nc.sbuf_tensor(f"k_kst{i}", [128, 8, TB], BF16)) for i in range(2)]
        vst = [ph.enter_context(nc.sbuf_tensor(f"k_vst{i}", [128, 4, NH, HD + 1], BF16)) for i in range(2)]
        negc = ph.enter_context(nc.sbuf_tensor("k_negc", [NH, L], F32))
        nbf = ph.enter_context(nc.sbuf_tensor("k_nbf", [NH, 1], F32))
        one1 = ph.enter_context(nc.sbuf_tensor("k_one1", [NH, 1], F32))
        onesf = ph.enter_context(nc.sbuf_tensor("k_onesf", [NH, TB], F32))
        e1 = ph.enter_context(nc.sbuf_tensor("k_e1", [NH, TB], F32))
        r1 = ph.enter_context(nc.sbuf_tensor("k_r1", [NH, TB], F32))
        sp3 = [ph.enter_context(nc.sbuf_tensor(f"k_sp3{i}", [NH, 3, TB], BF16)) for i in range(2)]
        ng3 = [ph.enter_context(nc.sbuf_tensor(f"k_ng3{i}", [NH, 3, TB], BF16)) for i in range(2)]
        on3 = ph.enter_context(nc.sbuf_tensor("k_on3", [NH, 3, TB], BF16))
        psn = ph.enter_context(nc.psum_tensor("k_psn", [128, 512], F32))
        psk = ph.enter_context(nc.psum_tensor("k_psk", [128, 3, 512], F32))
        psv = ph.enter_context(nc.psum_tensor("k_psv", [128, 3, 512], F32))
        psf = ph.enter_context(nc.psum_tensor("k_psf", [128, 512], F32))

        for kt in range(KT):
            P.dma("sp", wk[:, kt, :], w_kvf_d[kt * 128:(kt + 1) * 128, :], reads=WC["kvf"], writes=[f"wk{kt}"])
        wkr = [f"wk{kt}" for kt in range(KT)]
        P.dma("sp", nbf[:, :], bf_d[:, :], writes=["nbf"])
        P.op("dve", lambda e: e.tensor_scalar(out=nbf[:, :], in0=nbf[:, :], scalar1=-1.0, scalar2=None, op0=ALU.mult),
             reads=["nbf"], writes=["nbf"])
        P.op("dve", lambda e: e.memset(one1[:, :], 1.0), writes=["one1"])
        P.op("dve", lambda e: e.memset(onesf[:, :], 1.0), writes=["onesf"])
        P.op("dve", lambda e: e.memset(on3[:, :, :], 1.0), writes=["on3"])
        for i in range(2):
            P.op("pool", lambda e, i=i: e.memset(vst[i][:, :, :, :], 1.0), writes=[f"vst{i}"])

        def kv_norm(b):
            s_ = b % 2
            P.dma("sp", xb[s_][:, :, :], HT[:, :, b * TB:(b + 1) * TB].rearrange("k p t -> p k t"), writes=[f"kb{s_}x"])
            rms_block(P, xb[s_], f"kb{s_}x", sq, rr, psn, ones, epsb, gs[:, 8, :], xnb[s_], f"kxn{s_}", "kn_", TB)

        kv_norm(0)
        for b in range(NB):
            s = b % 2
            t0 = b * TB
            xn = xnb[s]
            kxn = f"kxn{s}"
            if b + 1 < NB:
                kv_norm(b + 1)
            for m in range(8):
                bk = (b * 8 + m) % 3
                for kt in range(KT):
                    P.op("pe", lambda e, m=m, kt=kt, bk=bk: e.matmul(
                        psk[:, bk, :], lhsT=wk[:, kt, m * 128:(m + 1) * 128], rhs=xn[:, kt, :],
                        start=(kt == 0), stop=(kt == KT - 1)), reads=wkr + [kxn], writes=[f"psk{bk}"])
                if m % 2 == 0:
                    P.op("act", lambda e, m=m, bk=bk, s=s: e.copy(out=kst[s][:, m, :], in_=psk[:, bk, :]),
                         reads=[f"psk{bk}"], writes=[f"kst{s}_{m}"])
                else:
                    P.op("dve", lambda e, m=m, bk=bk, s=s: e.tensor_copy(out=kst[s][:, m, :], in_=psk[:, bk, :]),
                         reads=[f"psk{bk}"], writes=[f"kst{s}_{m}"])
            for two in range(2):
                P.dma("pool", KTd[two::2, 0:HD, t0:t0 + TB].rearrange("m r t -> r m t"),
                      kst[s][two * 64:(two + 1) * 64, :, :],
                      reads=[f"kst{s}_{m}" for m in range(8)], writes=[f"KTd{b}_{two}"])
            for st in range(4):
                for c in range(2):
                    bk = (b * 8 + st * 2 + c) % 3
                    for kt in range(KT):
                        P.op("pe", lambda e, st=st, c=c, kt=kt, bk=bk: e.matmul(
                            psv[:, bk, :], lhsT=xn[:, kt, st * 128:(st + 1) * 128],
                            rhs=wk[:, kt, 1024 + c * 512:1024 + (c + 1) * 512],
                            start=(kt == 0), stop=(kt == KT - 1)), reads=wkr + [kxn], writes=[f"psv{bk}"])
                    src = psv[:, bk, :].rearrange("p (h d) -> p h d", d=HD)
                    if c == 0:
                        P.op("act", lambda e, st=st, c=c, s=s, src=src: e.copy(out=vst[s][:, st, c * 8:(c + 1) * 8, 0:HD], in_=src),
                             reads=[f"psv{bk}", f"vst{s}"], writes=[f"vst{s}_{st}_{c}"])
                    else:
                        P.op("dve", lambda e, st=st, c=c, s=s, src=src: e.tensor_copy(out=vst[s][:, st, c * 8:(c + 1) * 8, 0:HD], in_=src),
                             reads=[f"psv{bk}", f"vst{s}"], writes=[f"vst{s}_{st}_{c}"])
            P.dma("pool", VTd[b * 4:(b + 1) * 4, :, :, :].rearrange("s p h c -> p s (h c)"),
                  vst[s][:, :, :, :].rearrange("p s h c -> p s (h c)"),
                  reads=[f"vst{s}"] + [f"vst{s}_{st}_{c}" for st in range(4) for c in range(2)], writes=[f"VTd{b}"])
            for kt in range(KT):
                P.op("pe", lambda e, kt=kt: e.matmul(psf[0:NH, :], lhsT=wk[:, kt, 2048:2064], rhs=xn[:, kt, :],
                                                     start=(kt == 0), stop=(kt == KT - 1)),
                     reads=wkr + [kxn], writes=["psf"])
            P.op("act", lambda e: e.activation(out=e1[:, :], in_=psf[0:NH, :], func=AF.Exp, scale=-1.0, bias=nbf[:, 0:1]),
                 reads=["psf", "nbf"], writes=["e1"])
            P.op("act", lambda e: e.activation(out=e1[:, :], in_=e1[:, :], func=AF.Ln, scale=1.0, bias=one1[:, 0:1]),
                 reads=["e1", "one1"], writes=["e1"])
            init = 0.0 if b == 0 else negc[:, t0 - 1:t0]
            P.op("dve", lambda e, t0=t0, init=init: e.tensor_tensor_scan(
                out=negc[:, t0:t0 + TB], data0=onesf[:, :], data1=e1[:, :], initial=init, op0=ALU.mult, op1=ALU.add),
                reads=["e1", "onesf", "negc"], writes=["negc"])
            cb = negc[:, t0:t0 + TB]
            P.op("dve", lambda e, s=s, cb=cb: e.tensor_copy(out=sp3[s][:, 0, :], in_=cb), reads=["negc"], writes=[f"sp3{s}"])
            P.op("dve", lambda e, s=s, cb=cb: e.tensor_tensor(out=r1[:, :], in0=cb, in1=sp3[s][:, 0, :], op=ALU.subtract),
                 reads=["negc", f"sp3{s}"], writes=["r1"])
            P.op("dve", lambda e, s=s: e.tensor_copy(out=sp3[s][:, 1, :], in_=r1[:, :]), reads=["r1", f"sp3{s}"], writes=[f"sp3{s}"])
            P.op("dve", lambda e, s=s: e.tensor_tensor(out=r1[:, :], in0=r1[:, :], in1=sp3[s][:, 1, :], op=ALU.subtract),
                 reads=["r1", f"sp3{s}"], writes=["r1"])
            P.op("dve", lambda e, s=s: e.tensor_copy(out=sp3[s][:, 2, :], in_=r1[:, :]), reads=["r1", f"sp3{s}"], writes=[f"sp3{s}"])
            P.op("dve", lambda e, s=s: e.tensor_scalar(out=ng3[s][:, :, :], in0=sp3[s][:, :, :], scalar1=-1.0, scalar2=None,
                                                       op0=ALU.mult), reads=[f"sp3{s}"], writes=[f"ng3{s}"])
            P.dma("pool", KTd[:, HD:HD + 3, t0:t0 + TB], sp3[s][:, :, :], reads=[f"sp3{s}"], writes=[f"KTa{b}"])
            P.dma("pool", KTd[:, HD + 3:HD + 6, t0:t0 + TB], on3[:, :, :], reads=["on3"], writes=[f"KTb{b}"])
            P.dma("pool", QAd[:, 0:3, t0:t0 + TB], on3[:, :, :], reads=["on3"], writes=[f"QAa{b}"])
            P.dma("pool", QAd[:, 3:6, t0:t0 + TB], ng3[s][:, :, :], reads=[f"ng3{s}"], writes=[f"QAb{b}"])
        P.barrier()


def fox_phase(P, nc, L, layer, HT, w_q_d, w_o_d, KTd, QAd, VTd, OTd, QTd, tri_d, ones, epsb, gs, ident, pf=None):
    j = layer - 2
    NB = L // TB
    NST = L // 128
    with ExitStack() as ph_outer:
        ph = ExitStack()
        xn = ph.enter_context(nc.sbuf_tensor("a_xn", [128, KT, L], BF16))
        wq = ph.enter_context(nc.sbuf_tensor("a_wq", [128, KT, D], BF16))
        trif = ph.enter_context(nc.sbuf_tensor("a_trif", [128, 128], F32))
        tri = ph.enter_context(nc.sbuf_tensor("a_tri", [128, 128], BF16))
        bc1 = ph.enter_context(nc.sbuf_tensor("a_bc1", [128, HD], F32))
        P.dma("sp", wq[:, :, :], w_q_d[j].rearrange("(k p) d -> p k d", p=128), reads=WC[f"q{j}"], writes=["wq"])
        wqr = ["wq"]
        P.dma("sp", trif[:, :], tri_d[:, :], writes=["trif"])
        P.op("dve", lambda e: e.tensor_scalar(out=tri[:, :], in0=trif[:, :], scalar1=-1.0, scalar2=30000.0, op0=ALU.add, op1=ALU.mult),
             reads=["trif"], writes=["tri"])
        identb = ph.enter_context(nc.sbuf_tensor("a_identb", [128, 128], BF16))
        P.op("dve", lambda e: e.tensor_copy(out=identb[:, :], in_=ident[:, :]), reads=["ident"], writes=["identb"])
        P.op("dve", lambda e: e.memset(bc1[:, :], 1.0), writes=["bc1"])

        with ExitStack() as p1:
            xb = [p1.enter_context(nc.sbuf_tensor(f"a_xb{i}", [128, KT, TB], F32)) for i in range(2)]
            sq = p1.enter_context(nc.sbuf_tensor("a_sq", [128, KT, TB], BF16))
            rr = p1.enter_context(nc.sbuf_tensor("a_rr", [128, TB], F32))
            psn = p1.enter_context(nc.psum_tensor("a_psn", [128, 512], F32))
            psqp = p1.enter_context(nc.psum_tensor("a_psqp", [128, 2, 512], F32))
            qst = [p1.enter_context(nc.sbuf_tensor(f"a_qst{i}", [128, 8, TB], BF16)) for i in range(2)]
            def a_norm(b):
                s_ = b % 2
                P.dma("sp", xb[s_][:, :, :], HT[:, :, b * TB:(b + 1) * TB].rearrange("k p t -> p k t"), writes=[f"ab{s_}x"])
                rms_block(P, xb[s_], f"ab{s_}x", sq, rr, psn, ones, epsb, gs[:, layer, :], xn[:, :, b * TB:(b + 1) * TB],
                          f"axn{b}", "an_", TB)

            a_norm(0)
            for b in range(NB):
                s = b % 2
                if b + 1 < NB:
                    a_norm(b + 1)
                for m in range(8):
                    bk = m % 2
                    for kt in range(KT):
                        P.op("pe", lambda e, m=m, kt=kt, bk=bk, b=b: e.matmul(
                            psqp[:, bk, :], lhsT=wq[:, kt, m * 128:(m + 1) * 128], rhs=xn[:, kt, b * TB:(b + 1) * TB],
                            start=(kt == 0), stop=(kt == KT - 1)), reads=wqr + [f"axn{b}"], writes=[f"psqp{bk}"])
                    if m % 2 == 0:
                        P.op("act", lambda e, m=m, bk=bk, s=s: e.activation(out=qst[s][:, m, :], in_=psqp[:, bk, :], func=AF.Copy, scale=float(HD ** -0.5)),
                             reads=[f"psqp{bk}"], writes=[f"qst{s}_{m}"])
                    else:
                        P.op("dve", lambda e, m=m, bk=bk, s=s: e.tensor_scalar(out=qst[s][:, m, :], in0=psqp[:, bk, :], scalar1=float(HD ** -0.5),
                                                                              scalar2=None, op0=ALU.mult),
                             reads=[f"psqp{bk}"], writes=[f"qst{s}_{m}"])
                for two in range(2):
                    P.dma("pool", QTd[two::2, :, b * TB:(b + 1) * TB].rearrange("m r t -> r m t"),
                          qst[s][two * 64:(two + 1) * 64, :, :],
                          reads=[f"qst{s}_{m}" for m in range(8)], writes=[f"QTd{b}_{two}"])
            P.barrier()
        xnr = []

        with ExitStack() as p2:
            qT = [p2.enter_context(nc.sbuf_tensor(f"a_qT{i}", [KA, L], BF16)) for i in range(2)]
            kT = [p2.enter_context(nc.sbuf_tensor(f"a_kT{i}", [KA, L], BF16)) for i in range(2)]
            vh = [p2.enter_context(nc.sbuf_tensor(f"a_vh{i}", [128, NST, 128], BF16)) for i in range(2)]
            NG = 3
            NP = 4
            LA = 2
            pT = [p2.enter_context(nc.sbuf_tensor(f"a_pT{i}", [128, 2, TB], BF16)) for i in range(NP)]
            osb = [p2.enter_context(nc.sbuf_tensor(f"a_osb{i}", [128, TB], F32)) for i in range(2)]
            rc = p2.enter_context(nc.sbuf_tensor("a_rc", [128, TB], F32))
            oT = [p2.enter_context(nc.sbuf_tensor(f"a_oT{i}", [HD, TB], BF16)) for i in range(2)]
            psS = p2.enter_context(nc.psum_tensor("a_psS", [128, NG, 2, 512], F32))
            psO = p2.enter_context(nc.psum_tensor("a_psO", [128, 512], F32))
            psX = p2.enter_context(nc.psum_tensor("a_psX", [128, 512], F32))
            oi = 0
            P.op("pool", lambda e: e.memset(rc[:, :], 0.0), writes=["rc"])
            for i in range(2):
                P.op("pool", lambda e, i=i: e.memset(vh[i][:, :, :], 1.0), writes=[f"vh{i}"])
            gn = 0
            for h in range(NH):
                hs = h % 2
                P.dma("sp", kT[hs][:, :], KTd[h, :, :], writes=[f"kT{hs}"])
                P.dma("sp", vh[hs][:, :, 0:HD + 1], VTd[:, :, h, :].rearrange("s p c -> p s c"), writes=[f"vh{hs}"])
                P.dma("sp", qT[hs][HD:KA, :], QAd[h, :, :], writes=[f"qTa{hs}"])
                P.dma("sp", qT[hs][0:HD, :], QTd[h, :, :], writes=[f"qT{hs}"])
                jobs = [(qb, g) for qb in range(NB) for g in range(2 * (qb + 1))]

                def lo_of(qb, st):
                    return 0 if st < 4 * qb else (st - 4 * qb) * 128

                def emit_qk(n, g_, hs=hs):
                    qb, g = jobs[n]
                    gb = g_ % NG
                    pb = g_ % NP
                    los = []
                    for t in range(2):
                        st = 2 * g + t
                        lo = lo_of(qb, st)
                        los.append(lo)
                        diag = st >= 4 * qb
                        P.op("pe", lambda e, st=st, lo=lo, t=t, diag=diag: e.matmul(
                            psS[:, gb, t, lo:TB], lhsT=kT[hs][:, st * 128:(st + 1) * 128],
                            rhs=qT[hs][:, qb * TB + lo:(qb + 1) * TB], start=True, stop=not diag),
                            reads=[f"kT{hs}", f"qTa{hs}", f"qT{hs}"], writes=[f"psS{gb}"])
                        if diag:
                            P.op("pe", lambda e, lo=lo, t=t: e.matmul(
                                psS[:, gb, t, lo:lo + 128], lhsT=identb[:, :], rhs=tri[:, :], start=False, stop=True),
                                reads=["tri", "identb"], writes=[f"psS{gb}"])
                    if los == [0, 0]:
                        P.op("act", lambda e: e.activation(out=pT[pb][:, :, :], in_=psS[:, gb, :, :], func=AF.Exp),
                             reads=[f"psS{gb}"], writes=[f"pT{pb}"])
                    else:
                        for t in range(2):
                            lo = los[t]
                            P.op("act", lambda e, t=t, lo=lo: e.activation(out=pT[pb][:, t, lo:TB], in_=psS[:, gb, t, lo:TB], func=AF.Exp),
                                 reads=[f"psS{gb}"], writes=[f"pT{pb}"])

                def emit_pv(n, g_, hs=hs):
                    qb, g = jobs[n]
                    pb = g_ % NP
                    nst = 4 * (qb + 1)
                    for t in range(2):
                        st = 2 * g + t
                        lo = lo_of(qb, st)
                        P.op("pe", lambda e, st=st, lo=lo, t=t: e.matmul(
                            psO[:, lo:TB], lhsT=vh[hs][:, st, :], rhs=pT[pb][:, t, lo:TB],
                            start=(st == 0), stop=(st == nst - 1)),
                            reads=[f"vh{hs}", f"pT{pb}"], writes=["psO"])

                pending = []
                for a_ in range(min(LA, len(jobs))):
                    emit_qk(a_, gn + a_)
                for n in range(len(jobs)):
                    qb, g = jobs[n]
                    if n + LA < len(jobs):
                        emit_qk(n + LA, gn + n + LA)
                    emit_pv(n, gn + n)
                    for p_ in pending:
                        p_[0] -= 1
                    while pending and pending[0][0] <= 0:
                        pending.pop(0)[1]()
                    if g == 2 * (qb + 1) - 1:
                        ob = oi % 2
                        oi += 1
                        P.op("dve", lambda e, ob=ob: e.tensor_copy(out=osb[ob][:, :], in_=psO[:, :]),
                             reads=["psO"], writes=[f"osb{ob}"])
                        P.op("dve", lambda e, ob=ob: e.reciprocal(out=rc[HD:128, :], in_=osb[ob][HD:128, :]),
                             reads=[f"osb{ob}"], writes=["rc"])

                        def tail(ob=ob, qb=qb, h=h):
                            P.op("pe", lambda e: e.matmul(psX[0:HD, :], lhsT=ident[:, HD:128], rhs=rc[:, :], start=True, stop=True),
                                 reads=["rc", "ident"], writes=["psX"])
                            P.op("dve", lambda e: e.tensor_tensor(out=oT[ob][:, :], in0=osb[ob][0:HD, :], in1=psX[0:HD, :], op=ALU.mult),
                                 reads=[f"osb{ob}", "psX"], writes=[f"oT{ob}"])
                            P.dma("pool", OTd[h, :, qb * TB:(qb + 1) * TB], oT[ob][:, :], reads=[f"oT{ob}"], writes=[f"OTd{h}_{qb}"])
                        pending.append([4, tail])
                for p_ in pending:
                    p_[1]()
                gn += len(jobs)
            P.barrier()

        ph.close()
        if pf is not None:
            pf.alloc()
        with ExitStack() as p3:
            wo = p3.enter_context(nc.sbuf_tensor("a_wo", [128, KT, D], BF16))
            P.dma("sp", wo[:, :, :], w_o_d[j].rearrange("(k p) d -> p k d", p=128), reads=WC[f"o{j}"], writes=["wo"])
            wor = ["wo"]
            xb = [p3.enter_context(nc.sbuf_tensor(f"a_xc{i}", [128, KT, TB], F32)) for i in range(2)]
            ot = [p3.enter_context(nc.sbuf_tensor(f"a_ot{i}", [128, KT, TB], BF16)) for i in range(2)]
            pso = p3.enter_context(nc.psum_tensor("a_pso", [128, 2, 512], F32))
            for b in range(NB):
                s = b % 2
                t0 = b * TB
                P.dma("sp", xb[s][:, :, :], HT[:, :, t0:t0 + TB].rearrange("k p t -> p k t"), writes=[f"ac{s}x"])
                P.dma("sp", ot[s][:, :, :], OTd[:, :, t0:t0 + TB].rearrange("(k h2) r t -> (h2 r) k t", h2=2), writes=[f"ot{s}"])
                if pf is not None and b % 2 == 1:
                    pf.load_some(1)
                for m in range(KT):
                    bk = m % 2
                    for h in range(KT):
                        P.op("pe", lambda e, m=m, h=h, bk=bk, s=s: e.matmul(
                            pso[:, bk, :], lhsT=wo[:, h, m * 128:(m + 1) * 128], rhs=ot[s][:, h, :],
                            start=(h == 0), stop=(h == KT - 1)), reads=wor + [f"ot{s}"], writes=[f"pso{bk}"])
                    P.op("dve", lambda e, m=m, bk=bk, s=s: e.tensor_tensor(
                        out=xb[s][:, m, :], in0=xb[s][:, m, :], in1=pso[:, bk, :], op=ALU.add),
                        reads=[f"pso{bk}", f"ac{s}x"], writes=[f"ac{s}o{m}"])
                P.dma("pool", HT[:, :, t0:t0 + TB].rearrange("k p t -> p k t"), xb[s][:, :, :],
                      reads=[f"ac{s}o{m}" for m in range(KT)] + [f"ac{s}x"], writes=[f"HTo{b}"])
            P.barrier()


NPAIR = 32
NT96 = 11
TC = 8


def s5_phase(P, nc, L, layer, HT, UTd, Gd, s5p_d, s5b_d, s5c_d, dv_d, w_glu_d, ones, epsb, gs, ident, after_norm=None, pf=None):
    NB = L // TB
    NC = L // TC
    PADC = NC // 2
    NLEV = int(np.ceil(np.log2(NC)))
    assert NC <= 512
    with ExitStack() as ph_outer:
        ph = ExitStack()
        Kb = ph.enter_context(nc.sbuf_tensor("s_Kb", [96, NT96, TC, 96], BF16))
        BE = ph.enter_context(nc.sbuf_tensor("s_BE", [96, NT96, TC, 2, 128], BF16))
        CE = ph.enter_context(nc.sbuf_tensor("s_CE", [128, NPAIR, TC, 2, 32], BF16))
        APW = ph.enter_context(nc.sbuf_tensor("s_APW", [128, 10, 2, NPAIR], F32))
        NAI = ph.enter_context(nc.sbuf_tensor("s_NAI", [128, 10, NPAIR], F32))
        dv = ph.enter_context(nc.sbuf_tensor("s_dv", [96, NT96], F32))
        P.dma("sp", dv[:, :], dv_d[layer], writes=["dv"])

        with ExitStack() as p0:
            T = lambda name, shape, dt=F32: p0.enter_context(nc.sbuf_tensor("s0_" + name, shape, dt))
            prm = T("prm", [128, NPAIR, 3])
            Bm = T("Bm", [128, NPAIR, 2, 16])
            Cm = T("Cm", [128, NPAIR, 2, 16])
            PW = T("PW", [128, 9, 2, NPAIR])
            dt_ = T("dt", [128, NPAIR]); ar = T("ar", [128, NPAIR]); ai = T("ai", [128, NPAIR])
            t1 = T("t1", [128, NPAIR]); t2 = T("t2", [128, NPAIR]); t3 = T("t3", [128, NPAIR])
            zz = T("zz", [128, 5, 2, NPAIR])
            Fr = T("Fr", [128, 2, NPAIR])
            hpi = T("hpi", [128, 1])
            Bb = T("Bb", [128, 2, NPAIR, 16])
            W = [T(f"W{i}", [128, 2, NPAIR, 16]) for i in range(2)]
            u1 = T("u1", [128, NPAIR, 16]); u2 = T("u2", [128, NPAIR, 16])
            Cz = T("Cz", [128, 2, 33 * 32], BF16)
            Zall = [T(f"Z{i}", [128, 2, 33 * 32], BF16) for i in range(2)]
            identb = T("identb", [128, 128], BF16)
            psT = p0.enter_context(nc.psum_tensor("s0_psT", [128, 4, 2, 512], BF16))
            psK = p0.enter_context(nc.psum_tensor("s0_psK", [128, 3, 512], F32))

            n_xb = [T(f"nxb{i}", [128, KT, TB]) for i in range(2)]
            n_sq = T("nsq", [128, KT, TB], BF16)
            n_rr = T("nrr", [128, TB])
            n_xn = [T(f"nxn{i}", [128, KT, TB], BF16) for i in range(2)]
            n_ps = p0.enter_context(nc.psum_tensor("s0_psn", [128, 512], F32))

            def n_load(b):
                if b < NB:
                    P.dma("sp", n_xb[b % 2][:, :, :], HT[:, :, b * TB:(b + 1) * TB].rearrange("k p t -> p k t"), writes=[f"sb{b % 2}x"])

            def n_block(b):
                if b >= NB:
                    return
                s_ = b % 2
                n_load(b + 1)
                rms_block(P, n_xb[s_], f"sb{s_}x", n_sq, n_rr, n_ps, ones, epsb, gs[:, layer, :], n_xn[s_], f"sxn{s_}", "sn_", TB)
                P.dma("pool", UTd[:, b * TB:(b + 1) * TB].rearrange("(k p) t -> p k t", p=128), n_xn[s_][:, :, :],
                      reads=[f"sxn{s_}"], writes=[f"UTd{b}"])

            n_load(0)
            P.dma("sp", prm[:, :, :], s5p_d[layer], writes=["prm"])
            P.dma("sp", Bm[:, :, :, :], s5b_d[layer], writes=["Bm"])
            P.dma("sp", Cm[:, :, :, :], s5c_d[layer], writes=["Cm"])
            P.op("dve", lambda e: e.memset(hpi[:, :], float(np.pi / 2)), writes=["hpi"])
            P.op("dve", lambda e: e.tensor_copy(out=identb[:, :], in_=ident[:, :]), reads=["ident"], writes=["identb"])
            P.op("pool", lambda e: e.memset(Kb[:, :, :, :], 0.0), writes=["Kb"])
            for i in range(3):
                P.op("dve", lambda e, i=i: e.memset(psK[:, i, :], 0.0), writes=[f"psK{i}"])
            P.op("pool", lambda e: e.memset(CE[:, :, :, :, :], 0.0), writes=["CE"])
            P.op("pool", lambda e: e.memset(Cz[:, :, :], 0.0), writes=["Cz"])
            for i in range(2):
                P.op("pool", lambda e, i=i: e.memset(Zall[i][:, :, :], 0.0), writes=[f"Z{i}"])
            lr, li, ldt = prm[:, :, 0], prm[:, :, 1], prm[:, :, 2]

            def dve(fn, reads, writes):
                P.op("dve", fn, reads=reads, writes=writes)

            def tt(out, a, b, op, r, w):
                dve(lambda e: e.tensor_tensor(out=out, in0=a, in1=b, op=op), r, w)

            def cmul(o_re, o_im, a_re, a_im, b_re, b_im, r, w):
                tt(t1[:, :], a_re, b_re, ALU.mult, r, ["t1"])
                tt(t2[:, :], a_im, b_im, ALU.mult, r, ["t2"])
                tt(o_re, t1[:, :], t2[:, :], ALU.subtract, ["t1", "t2"], w)
                tt(t1[:, :], a_re, b_im, ALU.mult, r + w, ["t1"])
                tt(t2[:, :], a_im, b_re, ALU.mult, r + w, ["t2"])
                tt(o_im, t1[:, :], t2[:, :], ALU.add, ["t1", "t2"], w)

            P.op("act", lambda e: e.activation(out=dt_[:, :], in_=ldt, func=AF.Exp), reads=["prm"], writes=["dt"])
            tt(ar[:, :], lr, dt_[:, :], ALU.mult, ["prm", "dt"], ["ar"])
            tt(ai[:, :], li, dt_[:, :], ALU.mult, ["prm", "dt"], ["ai"])
            P.op("act", lambda e: e.activation(out=t3[:, :], in_=ar[:, :], func=AF.Exp, scale=1.0 / 16), reads=["ar"], writes=["t3"])
            P.op("act", lambda e: e.activation(out=zz[:, 0, 1, :], in_=ai[:, :], func=AF.Sin, scale=1.0 / 16), reads=["ai"], writes=["zs"])
            P.op("act", lambda e: e.activation(out=zz[:, 0, 0, :], in_=ai[:, :], func=AF.Sin, scale=-1.0 / 16, bias=hpi[:, 0:1]),
                 reads=["ai", "hpi"], writes=["zc"])
            tt(zz[:, 0, 0, :], zz[:, 0, 0, :], t3[:, :], ALU.mult, ["zc", "t3"], ["zz0"])
            tt(zz[:, 0, 1, :], zz[:, 0, 1, :], t3[:, :], ALU.mult, ["zs", "t3"], ["zz0"])
            for k in range(4):
                cmul(zz[:, k + 1, 0, :], zz[:, k + 1, 1, :], zz[:, k, 0, :], zz[:, k, 1, :], zz[:, k, 0, :], zz[:, k, 1, :],
                     [f"zz{k}"], [f"zz{k + 1}"])
            lb_re, lb_im = zz[:, 4, 0, :], zz[:, 4, 1, :]
            dve(lambda e: e.memset(PW[:, 0, 0, :], 1.0), [], ["pw0"])
            dve(lambda e: e.memset(PW[:, 0, 1, :], 0.0), ["pw0"], ["pw0"])
            dve(lambda e: e.tensor_copy(out=PW[:, 1, :, :], in_=zz[:, 4, :, :]), ["zz4"], ["pw1"])
            for k in range(2, 9):
                cmul(PW[:, k, 0, :], PW[:, k, 1, :], PW[:, k - 1, 0, :], PW[:, k - 1, 1, :], lb_re, lb_im,
                     [f"pw{k - 1}", "zz4"], [f"pw{k}"])
            pwr = [f"pw{k}" for k in range(9)]
            dve(lambda e: e.tensor_copy(out=APW[:, 0, :, :], in_=PW[:, 8, :, :]), ["pw8"], ["apw0"])
            for m in range(1, 10):
                cmul(APW[:, m, 0, :], APW[:, m, 1, :], APW[:, m - 1, 0, :], APW[:, m - 1, 1, :],
                     APW[:, m - 1, 0, :], APW[:, m - 1, 1, :], [f"apw{m - 1}"], [f"apw{m}"])
            apr = [f"apw{m}" for m in range(10)]
            dve(lambda e: e.tensor_scalar(out=NAI[:, :, :], in0=APW[:, :, 1, :], scalar1=-1.0, scalar2=None, op0=ALU.mult), apr, ["nai"])
            tt(t3[:, :], lr, lr, ALU.mult, ["prm"], ["t3"])
            tt(t1[:, :], li, li, ALU.mult, ["prm"], ["t1"])
            tt(t3[:, :], t3[:, :], t1[:, :], ALU.add, ["t3", "t1"], ["t3"])
            dve(lambda e: e.reciprocal(out=t3[:, :], in_=t3[:, :]), ["t3"], ["t3"])
            dve(lambda e: e.tensor_scalar(out=ar[:, :], in0=lb_re, scalar1=-1.0, scalar2=None, op0=ALU.add), ["zz4", "ar"], ["ar"])
            tt(t1[:, :], ar[:, :], lr, ALU.mult, ["ar", "prm"], ["t1"])
            tt(t2[:, :], lb_im, li, ALU.mult, ["zz4", "prm"], ["t2"])
            tt(t1[:, :], t1[:, :], t2[:, :], ALU.add, ["t1", "t2"], ["t1"])
            tt(Fr[:, 0, :], t1[:, :], t3[:, :], ALU.mult, ["t1", "t3"], ["fr"])
            tt(t1[:, :], lb_im, lr, ALU.mult, ["zz4", "prm"], ["t1"])
            tt(t2[:, :], ar[:, :], li, ALU.mult, ["ar", "prm"], ["t2"])
            tt(t1[:, :], t1[:, :], t2[:, :], ALU.subtract, ["t1", "t2"], ["t1"])
            tt(Fr[:, 1, :], t1[:, :], t3[:, :], ALU.mult, ["t1", "t3"], ["fi"])

            def bc(ap2):
                return ap2.unsqueeze(2).to_broadcast([128, NPAIR, 16])

            def cmul_b(o_re, o_im, s_re, s_im, x_re, x_im, r, w):
                tt(u1[:, :, :], x_re, bc(s_re), ALU.mult, r, ["u1"])
                tt(u2[:, :, :], x_im, bc(s_im), ALU.mult, r, ["u2"])
                tt(o_re, u1[:, :, :], u2[:, :, :], ALU.subtract, ["u1", "u2"], w)
                tt(u1[:, :, :], x_im, bc(s_re), ALU.mult, r + w, ["u1"])
                tt(u2[:, :, :], x_re, bc(s_im), ALU.mult, r + w, ["u2"])
                tt(o_im, u1[:, :, :], u2[:, :, :], ALU.add, ["u1", "u2"], w)

            cmul_b(Bb[:, 0, :, :], Bb[:, 1, :, :], Fr[:, 0, :], Fr[:, 1, :], Bm[:, :, 0, :], Bm[:, :, 1, :],
                   ["fr", "fi", "Bm"], ["Bb"])
            for g2 in range(2):
                rows = slice(64 * g2, 64 * g2 + 64)
                czv = Cz[rows, 0, :].rearrange("p (q g h) -> p q g h", g=2, h=16)
                dve(lambda e, czv=czv, rows=rows, g2=g2: e.tensor_copy(out=czv[:, 0:NPAIR, g2, :], in_=Cm[rows, :, 0, :]),
                    ["Cm", "Cz"], ["Cz"])
                czv2 = Cz[rows, 1, :].rearrange("p (q g h) -> p q g h", g=2, h=16)
                dve(lambda e, czv2=czv2, rows=rows, g2=g2: e.tensor_scalar(out=czv2[:, 0:NPAIR, g2, :], in0=Cm[rows, :, 1, :],
                                                                            scalar1=-1.0, scalar2=None, op0=ALU.mult),
                    ["Cm", "Cz"], ["Cz"])
            for i in range(TC):
                cmul_b(W[0][:, 0, :, :], W[0][:, 1, :, :], PW[:, i + 1, 0, :], PW[:, i + 1, 1, :], Cm[:, :, 0, :], Cm[:, :, 1, :],
                       pwr + ["Cm"], ["W0"])
                for g2 in range(2):
                    rows = slice(64 * g2, 64 * g2 + 64)
                    dve(lambda e, rows=rows, g2=g2, i=i: e.tensor_copy(out=CE[rows, :, i, 0, 16 * g2:16 * g2 + 16], in_=W[0][rows, 0, :, :]),
                        ["W0", "CE"], ["CE"])
                    dve(lambda e, rows=rows, g2=g2, i=i: e.tensor_scalar(out=CE[rows, :, i, 1, 16 * g2:16 * g2 + 16], in0=W[0][rows, 1, :, :],
                                                                         scalar1=-1.0, scalar2=None, op0=ALU.mult),
                        ["W0", "CE"], ["CE"])
            for tau in range(TC):
                zb = tau % 2
                cmul_b(W[1][:, 0, :, :], W[1][:, 1, :, :], PW[:, tau, 0, :], PW[:, tau, 1, :], Bb[:, 0, :, :], Bb[:, 1, :, :],
                       pwr + ["Bb"], ["W1"])
                for g2 in range(2):
                    rows = slice(64 * g2, 64 * g2 + 64)
                    for ri in range(2):
                        zv = Zall[zb][rows, ri, :].rearrange("p (q g h) -> p q g h", g=2, h=16)
                        dve(lambda e, zv=zv, rows=rows, g2=g2, ri=ri: e.tensor_copy(out=zv[:, 0:NPAIR, g2, :], in_=W[1][rows, ri, :, :]),
                            ["W1", f"Z{zb}"], [f"Z{zb}"])
                for k in range(NT96):
                    npq = 3 if k < NT96 - 1 else 2
                    R = 32 * npq
                    tb = k % 4
                    kb_ = (tau * NT96 + k) % 3
                    for ri in range(2):
                        P.op("pe", lambda e, k=k, R=R, ri=ri, tb=tb, zb=zb: e.transpose(
                            out=psT[0:R, tb, ri, 0:128], in_=Zall[zb][:, ri, 96 * k:96 * k + R], identity=identb[:, :]),
                            reads=[f"Z{zb}", "identb"], writes=[f"psT{tb}"])
                    P.op("act", lambda e, k=k, R=R, tb=tb, tau=tau: e.copy(out=BE[0:R, k, TC - 1 - tau, :, :], in_=psT[0:R, tb, :, 0:128]),
                         reads=[f"psT{tb}"], writes=["BE"])
                    for q3 in range(npq):
                        c0 = 96 * k + 32 * q3
                        for ri in range(2):
                            P.op("pe", lambda e, q3=q3, c0=c0, ri=ri, kb_=kb_, zb=zb: e.matmul(
                                psK[32 * q3:32 * q3 + 32, kb_, 32 * q3:32 * q3 + 32], lhsT=Zall[zb][:, ri, c0:c0 + 32],
                                rhs=Cz[:, ri, c0:c0 + 32], start=(ri == 0), stop=(ri == 1)),
                                reads=[f"Z{zb}", "Cz"], writes=[f"psK{kb_}"])
                    P.op("dve", lambda e, R=R, k=k, kb_=kb_, tau=tau: e.tensor_copy(
                        out=Kb[0:R, k, tau, 0:R], in_=psK[0:R, kb_, 0:R]),
                        reads=[f"psK{kb_}", "Kb"], writes=["Kb"])
                for bb in range(tau * NB // TC, (tau + 1) * NB // TC):
                    n_block(bb)
            P.barrier()

        with ExitStack() as p2:
            ut = [p2.enter_context(nc.sbuf_tensor(f"s2_ut{i}", [96, L], BF16)) for i in range(2)]
            ut8 = [p2.enter_context(nc.sbuf_tensor(f"s2_ut8{i}", [96, TC, NC], BF16)) for i in range(2)]
            go = [p2.enter_context(nc.sbuf_tensor(f"s2_go{i}", [96, L], BF16)) for i in range(1)]
            NES = 3
            POOL_EVERY = 10 ** 9
            EA = [p2.enter_context(nc.sbuf_tensor(f"s2_E{i}", [128, 2, PADC + NC], F32)) for i in range(2 * NES)]
            ptmp = p2.enter_context(nc.sbuf_tensor("s2_ptmp", [128, NC], F32))
            Sbf = [p2.enter_context(nc.sbuf_tensor(f"s2_S{i}", [128, 3, 2, 1 + NC], BF16)) for i in range(2)]
            tmp = [p2.enter_context(nc.sbuf_tensor(f"s2_tmp{i}", [96, NC], F32)) for i in range(3)]
            psE = p2.enter_context(nc.psum_tensor("s2_psE", [128, 2, 2, 512], F32))
            psY = p2.enter_context(nc.psum_tensor("s2_psY", [128, 4, 512], F32))
            for i in range(2 * NES):
                P.op("pool", lambda e, i=i: e.memset(EA[i][:, :, :], 0.0), writes=[f"E{i}r", f"E{i}i"])
            for i in range(2):
                P.op("pool", lambda e, i=i: e.memset(Sbf[i][:, :, :, :], 0.0), writes=[f"S{i}"])
            tiles = [(k, 3 if k < NT96 - 1 else 2) for k in range(NT96)]
            pairs = [(k, q3) for (k, npq) in tiles for q3 in range(npq)]
            loaded = set()

            def load_tile(k):
                if k in loaded or k >= NT96:
                    return
                loaded.add(k)
                R = 32 * tiles[k][1]
                us = k % 2
                P.dma("sp", ut[us][0:R, :], UTd[96 * k:96 * k + R, :], writes=[f"ut{us}"])
                P.op("pool", lambda e: e.tensor_copy(out=ut8[us][0:R, :, :], in_=ut[us][0:R, :].rearrange("p (c j) -> p j c", j=TC)),
                     reads=[f"ut{us}"], writes=[f"ut8{us}"])

            def emit_B(pi_):
                k, q3 = pairs[pi_]
                load_tile(k)
                us = k % 2
                es_ = pi_ % NES
                pb_ = pi_ % 2
                rows = slice(32 * q3, 32 * q3 + 32)
                for ri in range(2):
                    for j in range(TC):
                        P.op("pe", lambda e, j=j, ri=ri: e.matmul(
                            psE[:, pb_, ri, 0:NC], lhsT=BE[rows, k, j, ri, :], rhs=ut8[us][rows, j, :],
                            start=(j == 0), stop=(j == TC - 1)), reads=["BE", f"ut8{us}"], writes=[f"psE{pb_}_{ri}"])
                P.op("act", lambda e: e.copy(out=EA[2 * es_][:, :, PADC:PADC + NC], in_=psE[:, pb_, :, 0:NC]),
                     reads=[f"psE{pb_}_0", f"psE{pb_}_1"], writes=[f"E{2 * es_}r", f"E{2 * es_}i"])

            def emit_LOG(pi_):
                k, q3 = pairs[pi_]
                q = 3 * k + q3
                us = k % 2
                es_ = pi_ % NES
                src = 0
                for m in range(NLEV):
                    sh = 1 << m
                    X, Y = EA[2 * es_ + src], EA[2 * es_ + 1 - src]
                    xr, yr = f"E{2 * es_ + src}", f"E{2 * es_ + 1 - src}"
                    a_re, a_im, na_im = APW[:, m, 0, q:q + 1], APW[:, m, 1, q:q + 1], NAI[:, m, q:q + 1]
                    cur = slice(PADC, PADC + NC)
                    shf = slice(PADC - sh, PADC - sh + NC)
                    if pi_ % POOL_EVERY == POOL_EVERY - 1:
                        def stt(out, in0, sc, in1, r, w):
                            P.op("pool", lambda e: e.tensor_scalar(out=ptmp[:, :], in0=in0, scalar1=sc, scalar2=None, op0=ALU.mult),
                                 reads=r, writes=["ptmp"])
                            P.op("pool", lambda e: e.tensor_tensor(out=out, in0=ptmp[:, :], in1=in1, op=ALU.add),
                                 reads=r + ["ptmp"], writes=w)
                    else:
                        stt = lambda out, in0, sc, in1, r, w: P.op("dve", lambda e: e.scalar_tensor_tensor(
                            out=out, in0=in0, scalar=sc, in1=in1, op0=ALU.mult, op1=ALU.add), reads=r, writes=w)
                    stt(Y[:, 0, cur], X[:, 0, shf], a_re, X[:, 0, cur], [xr + "r", "apw"], [yr + "r"])
                    stt(Y[:, 1, cur], X[:, 1, shf], a_re, X[:, 1, cur], [xr + "i", "apw"], [yr + "i"])
                    stt(Y[:, 0, cur], X[:, 1, shf], na_im, Y[:, 0, cur], [xr + "i", yr + "r", "apw"], [yr + "r"])
                    stt(Y[:, 1, cur], X[:, 0, shf], a_im, Y[:, 1, cur], [xr + "r", yr + "i", "apw"], [yr + "i"])
                    src = 1 - src
                fin = 2 * es_ + src
                P.op("act", lambda e: e.copy(out=Sbf[us][:, q3, :, 1:1 + NC], in_=EA[fin][:, :, PADC:PADC + NC]),
                     reads=[f"E{fin}r", f"E{fin}i"], writes=[f"S{us}"])

            ti = [0]

            def emit_AD(k):
                npq = tiles[k][1]
                R = 32 * npq
                us = k % 2
                for i in range(TC):
                    bk = i % 4
                    for j in range(i + 1):
                        P.op("pe", lambda e, i=i, j=j, bk=bk: e.matmul(
                            psY[0:R, bk, 0:NC], lhsT=Kb[0:R, k, i - j, 0:R], rhs=ut8[us][0:R, j, :],
                            start=(j == 0), stop=False), reads=["Kb", f"ut8{us}"], writes=[f"psY{bk}"])
                    for q3 in range(npq):
                        q = 3 * k + q3
                        for ri in range(2):
                            last = (q3 == npq - 1 and ri == 1)
                            P.op("pe", lambda e, q3=q3, q=q, i=i, ri=ri, bk=bk, last=last: e.matmul(
                                psY[32 * q3:32 * q3 + 32, bk, 0:NC], lhsT=CE[:, q, i, ri, :], rhs=Sbf[us][:, q3, ri, 0:NC],
                                start=False, stop=(ri == 1)), reads=["CE", f"S{us}"], writes=[f"psY{bk}"])
                    tb = ti[0] % 3
                    ti[0] += 1
                    P.op("dve", lambda e, i=i, bk=bk, tb=tb: e.scalar_tensor_tensor(
                        out=tmp[tb][0:R, :], in0=ut8[us][0:R, i, :], scalar=dv[0:R, k:k + 1], in1=psY[0:R, bk, 0:NC],
                        op0=ALU.mult, op1=ALU.add), reads=[f"psY{bk}", f"ut8{us}", "dv"], writes=[f"tmp{tb}"])
                    P.op("act", lambda e, i=i, tb=tb: e.activation(
                        out=go[0][0:R, i::TC], in_=tmp[tb][0:R, :], func=AF.Gelu_apprx_tanh),
                        reads=[f"tmp{tb}"], writes=[f"go_{i}"])
                P.dma("pool", Gd[96 * k:96 * k + R, :], go[0][0:R, :], reads=[f"go_{i}" for i in range(TC)], writes=[f"Gd{k}"])

            emit_B(0)
            if len(pairs) > 1:
                emit_B(1)
            for pi_ in range(len(pairs)):
                if pi_ + 2 < len(pairs):
                    emit_B(pi_ + 2)
                emit_LOG(pi_)
                if after_norm is not None:
                    after_norm(3)
                k, q3 = pairs[pi_]
                if q3 == tiles[k][1] - 1:
                    emit_AD(k)
            P.barrier()

        ph.close()
        if pf is not None:
            pf.alloc()
        with ExitStack() as p3:
            wg = p3.enter_context(nc.sbuf_tensor("s3_wg", [128, KT, 2 * D], BF16))
            xb = [p3.enter_context(nc.sbuf_tensor(f"s3_xb{i}", [128, KT, TB], F32)) for i in range(2)]
            gb = [p3.enter_context(nc.sbuf_tensor(f"s3_gb{i}", [128, KT, TB], BF16)) for i in range(2)]
            sg = [p3.enter_context(nc.sbuf_tensor(f"s3_sg{i}", [128, TB], F32)) for i in range(2)]
            psA = p3.enter_context(nc.psum_tensor("s3_psA", [128, 2, 512], F32))
            psG = p3.enter_context(nc.psum_tensor("s3_psG", [128, 2, 512], F32))
            P.dma("sp", wg[:, :, :], w_glu_d[layer].rearrange("(k p) c -> p k c", p=128), reads=WC[f"glu{layer}"], writes=["wg"])
            wgr = ["wg"]
            for b in range(NB):
                s = b % 2
                t0 = b * TB
                P.dma("sp", xb[s][:, :, :], HT[:, :, t0:t0 + TB].rearrange("k p t -> p k t"), writes=[f"gx{s}"])
                P.dma("sp", gb[s][:, :, :], Gd[:, t0:t0 + TB].rearrange("(k p) t -> p k t", p=128), writes=[f"gb{s}"])
                if pf is not None and b % 2 == 1:
                    pf.load_some(1)
                for m in range(KT):
                    bk = m % 2
                    for kt in range(KT):
                        P.op("pe", lambda e, m=m, kt=kt, bk=bk, s=s: e.matmul(
                            psA[:, bk, :], lhsT=wg[:, kt, m * 128:(m + 1) * 128], rhs=gb[s][:, kt, :],
                            start=(kt == 0), stop=(kt == KT - 1)), reads=wgr + [f"gb{s}"], writes=[f"psA{bk}"])
                    for kt in range(KT):
                        P.op("pe", lambda e, m=m, kt=kt, bk=bk, s=s: e.matmul(
                            psG[:, bk, :], lhsT=wg[:, kt, D + m * 128:D + (m + 1) * 128], rhs=gb[s][:, kt, :],
                            start=(kt == 0), stop=(kt == KT - 1)), reads=wgr + [f"gb{s}"], writes=[f"psG{bk}"])
                    P.op("act", lambda e, bk=bk: e.activation(out=sg[bk][:, :], in_=psG[:, bk, :], func=AF.Sigmoid),
                         reads=[f"psG{bk}"], writes=[f"sg{bk}"])
                    P.op("dve", lambda e, bk=bk: e.tensor_tensor(out=sg[bk][:, :], in0=sg[bk][:, :], in1=psA[:, bk, :], op=ALU.mult),
                         reads=[f"psA{bk}", f"sg{bk}"], writes=[f"sg{bk}"])
                    P.op("dve", lambda e, m=m, bk=bk, s=s: e.tensor_tensor(out=xb[s][:, m, :], in0=xb[s][:, m, :], in1=sg[bk][:, :], op=ALU.add),
                         reads=[f"sg{bk}", f"gx{s}"], writes=[f"gx{s}o{m}"])
                P.dma("pool", HT[:, :, t0:t0 + TB].rearrange("k p t -> p k t"), xb[s][:, :, :],
                      reads=[f"gx{s}o{m}" for m in range(KT)] + [f"gx{s}"], writes=[f"HTs{b}"])
            P.barrier()


def build(L, phases=("in", "final")):
    nc = bass.Bass("TRN2", target_bir_lowering=False)
    NB = L // TB
    x_d = nc.dram_tensor("x", [L, D], F32, kind="ExternalInput").ap()
    ident_d = nc.dram_tensor("ident", [128, 128], F32, kind="ExternalInput").ap()
    gvec_d = nc.dram_tensor("gvec", [128, 10, KT], F32, kind="ExternalInput").ap()
    out_d = nc.dram_tensor("out", [L, D], F32, kind="ExternalOutput").ap()
    w_in_f = nc.dram_tensor("w_ffn_in", [4, D, 2 * FF], F32, kind="ExternalInput").ap()
    w_out_f = nc.dram_tensor("w_ffn_out", [4, FF, D], F32, kind="ExternalInput").ap()
    w_in_d = nc.dram_tensor("w_in_b", [4, D, 2 * FF], BF16, kind="Internal").ap()
    w_out_d = nc.dram_tensor("w_out_b", [4, FF, D], BF16, kind="Internal").ap()
    cw_d = nc.dram_tensor("ffn_cw", [4, 128, 2 * NFT, 4], F32, kind="ExternalInput").ap()
    w_kvf_f = nc.dram_tensor("w_kvf", [D, 2064], F32, kind="ExternalInput").ap()
    w_kvf_d = nc.dram_tensor("w_kvf_b", [D, 2064], BF16, kind="Internal").ap()
    bf_d = nc.dram_tensor("b_f", [NH, 1], F32, kind="ExternalInput").ap()
    w_q_f = nc.dram_tensor("w_q", [2, D, D], F32, kind="ExternalInput").ap()
    w_o_f = nc.dram_tensor("w_o", [2, D, D], F32, kind="ExternalInput").ap()
    w_q_d = nc.dram_tensor("w_q_b", [2, D, D], BF16, kind="Internal").ap()
    w_o_d = nc.dram_tensor("w_o_b", [2, D, D], BF16, kind="Internal").ap()
    tri_d = nc.dram_tensor("tri", [128, 128], F32, kind="ExternalInput").ap()
    s5p_d = nc.dram_tensor("s5p", [2, 128, NPAIR, 3], F32, kind="ExternalInput").ap()
    s5b_d = nc.dram_tensor("s5b", [2, 128, NPAIR, 2, 16], F32, kind="ExternalInput").ap()
    s5c_d = nc.dram_tensor("s5c", [2, 128, NPAIR, 2, 16], F32, kind="ExternalInput").ap()
    dv_d = nc.dram_tensor("s5dv", [2, 96, NT96], F32, kind="ExternalInput").ap()
    w_glu_f = nc.dram_tensor("w_glu", [2, D, 2 * D], F32, kind="ExternalInput").ap()
    w_glu_d = nc.dram_tensor("w_glu_b", [2, D, 2 * D], BF16, kind="Internal").ap()
    UTd = nc.dram_tensor("UTd", [D, L], BF16, kind="Internal").ap()
    Gd = nc.dram_tensor("Gd", [D, L], BF16, kind="Internal").ap()
    KTd = nc.dram_tensor("KTd", [NH, KA, L], BF16, kind="Internal").ap()
    QAd = nc.dram_tensor("QAd", [NH, 6, L], BF16, kind="Internal").ap()
    VTd = nc.dram_tensor("VTd", [L // 128, 128, NH, HD + 1], BF16, kind="Internal").ap()
    OTd = nc.dram_tensor("OTd", [NH, HD, L], BF16, kind="Internal").ap()
    QTd = nc.dram_tensor("QTd", [NH, HD, L], BF16, kind="Internal").ap()
    HT = nc.dram_tensor("HT", [KT, 128, L], F32, kind="Internal").ap()

    nc_raw = nc
    with ExitStack() as es:
        P = Prog(nc, es)
        nc = NCW(nc_raw)
        ident = es.enter_context(nc.sbuf_tensor("ident_sb", [128, 128], F32))
        ones = es.enter_context(nc.sbuf_tensor("ones_sb", [128, 128], BF16))
        gs = es.enter_context(nc.sbuf_tensor("gs_sb", [128, 10, KT], F32))
        P.dma("sp", ident[:, :], ident_d[:, :], writes=["ident"])
        P.dma("sp", gs[:, :, :], gvec_d[:, :, :], writes=["gs"])
        P.op("dve", lambda e: e.memset(ones[:, :], 1.0), writes=["ones"])
        epsb = es.enter_context(nc.sbuf_tensor("epsb_sb", [128, 1], F32))
        P.op("dve", lambda e: e.memset(epsb[:, :], EPS), writes=["epsb"])

        if "in" in phases:
            with ExitStack() as ph:
                xt = [ph.enter_context(nc.sbuf_tensor(f"in_xt{i}", [128, 4, D], F32)) for i in range(2)]
                ht = [ph.enter_context(nc.sbuf_tensor(f"in_ht{i}", [128, KT, TB], F32)) for i in range(2)]
                ps = ph.enter_context(nc.psum_tensor("in_ps", [128, 4, TB], F32))
                for b in range(NB):
                    s = b % 2
                    P.dma("sp", xt[s][:, :, :], x_d[b * TB:(b + 1) * TB, :].rearrange("(j p) d -> p j d", p=128),
                          writes=[f"xt{s}"])
                    for kt in range(KT):
                        bk = kt % 4
                        for j in range(4):
                            P.op("pe", lambda e, j=j, kt=kt, bk=bk, s=s: e.transpose(
                                out=ps[:, bk, j * 128:(j + 1) * 128], in_=xt[s][:, j, kt * 128:(kt + 1) * 128],
                                identity=ident[:, :]), reads=[f"xt{s}", "ident"], writes=[f"ps{bk}"])
                        ev = "act" if kt % 2 == 0 else "dve"
                        if ev == "act":
                            P.op("act", lambda e, kt=kt, bk=bk, s=s: e.copy(out=ht[s][:, kt, :], in_=ps[:, bk, :]),
                                 reads=[f"ps{bk}"], writes=[f"ht{s}_{kt}"])
                        else:
                            P.op("dve", lambda e, kt=kt, bk=bk, s=s: e.tensor_copy(out=ht[s][:, kt, :], in_=ps[:, bk, :]),
                                 reads=[f"ps{bk}"], writes=[f"ht{s}_{kt}"])
                    P.dma("pool", HT[:, :, b * TB:(b + 1) * TB].rearrange("k p t -> p k t"), ht[s][:, :, :],
                          reads=[f"ht{s}_{kt}" for kt in range(KT)], writes=[f"HT{b}"])
                P.barrier()

        cast_items = []

        def wcast(dst, src, name, nchunk, split=None):
            rows = dst.shape[0]
            step = rows // nchunk
            names = []
            for c in range(nchunk):
                d_, s_ = dst[c * step:(c + 1) * step], src[c * step:(c + 1) * step]
                if split:
                    d_ = d_.rearrange("r (c e) -> r c e", e=split)
                    s_ = s_.rearrange("r (c e) -> r c e", e=split)
                nm = f"WC:{name}:{c}"
                names.append(nm)
                cast_items.append(lambda d_=d_, s_=s_, nm=nm: P.dma("pool", d_, s_, writes=[nm], ring="wc"))
            WC[name] = names
        for l in range(4):
            if l < 2:
                wcast(w_glu_d[l], w_glu_f[l], f"glu{l}", 4, 1024)
            else:
                if l == 2:
                    wcast(w_kvf_d, w_kvf_f, "kvf", 4, 1032)
                wcast(w_q_d[l - 2], w_q_f[l - 2], f"q{l - 2}", 2)
                wcast(w_o_d[l - 2], w_o_f[l - 2], f"o{l - 2}", 2)
            wcast(w_in_d[l], w_in_f[l], f"in{l}", 8, 1408)
            wcast(w_out_d[l], w_out_f[l], f"out{l}", 4)

        def issue_casts(n=None):
            k = len(cast_items) if n is None else min(n, len(cast_items))
            for _ in range(k):
                cast_items.pop(0)()

        for layer in range(4):
            lstack = es.enter_context(ExitStack())
            pf = WinPrefetch(P, nc, w_in_d, layer, lstack) if f"ffn{layer}" in phases else None
            if f"s5{layer}" in phases:
                s5_phase(P, nc, L, layer, HT, UTd, Gd, s5p_d, s5b_d, s5c_d, dv_d, w_glu_d, ones, epsb, gs, ident, issue_casts, pf)
            issue_casts()
            if layer == 2 and "kv" in phases:
                kv_phase(P, nc, L, HT, w_kvf_d, bf_d, KTd, QAd, VTd, ones, epsb, gs)
            if f"fox{layer}" in phases:
                fox_phase(P, nc, L, layer, HT, w_q_d, w_o_d, KTd, QAd, VTd, OTd, QTd, tri_d, ones, epsb, gs, ident, pf)
            if f"ffn{layer}" in phases:
                ffn_phase(P, nc, L, layer, HT, w_in_d, w_out_d, cw_d, ones, epsb, gs, pf)
            lstack.close()

        if "final" in phases:
            with ExitStack() as ph:
                xb = [ph.enter_context(nc.sbuf_tensor(f"fx{i}", [128, KT, TB], F32)) for i in range(2)]
                sq = ph.enter_context(nc.sbuf_tensor("fsq", [128, KT, TB], BF16))
                rr = ph.enter_context(nc.sbuf_tensor("frr", [128, TB], F32))
                xn = ph.enter_context(nc.sbuf_tensor("fxn", [128, KT, TB], F32))
                ot = [ph.enter_context(nc.sbuf_tensor(f"fot{i}", [128, 4, D], F32)) for i in range(2)]
                pss = ph.enter_context(nc.psum_tensor("f_pss", [128, TB], F32))
                ps = ph.enter_context(nc.psum_tensor("f_ps", [128, 4, TB], F32))
                for b in range(NB):
                    s = b % 2
                    P.dma("sp", xb[s][:, :, :], HT[:, :, b * TB:(b + 1) * TB].rearrange("k p t -> p k t"),
                          reads=[f"HT{b}"], writes=[f"fx{s}x"])
                    rms_block(P, xb[s], f"fx{s}x", sq, rr, pss, ones, epsb, gs[:, 9, :], xn, "fxn", "fin_", TB)
                    for j in range(4):
                        for half in range(2):
                            bk = (j * 2 + half) % 4
                            for k4 in range(4):
                                kt = half * 4 + k4
                                P.op("pe", lambda e, j=j, kt=kt, bk=bk, k4=k4: e.transpose(
                                    out=ps[:, bk, k4 * 128:(k4 + 1) * 128], in_=xn[:, kt, j * 128:(j + 1) * 128],
                                    identity=ident[:, :]), reads=["fxn", "ident"], writes=[f"fps{bk}"])
                            if half == 0:
                                P.op("act", lambda e, j=j, bk=bk, s=s: e.copy(out=ot[s][:, j, 0:512], in_=ps[:, bk, :]),
                                     reads=[f"fps{bk}"], writes=[f"fot{s}_{j}_0"])
                            else:
                                P.op("dve", lambda e, j=j, bk=bk, s=s: e.tensor_copy(out=ot[s][:, j, 512:1024], in_=ps[:, bk, :]),
                                     reads=[f"fps{bk}"], writes=[f"fot{s}_{j}_1"])
                    P.dma("pool", out_d[b * TB:(b + 1) * TB, :].rearrange("(j p) d -> p j d", p=128), ot[s][:, :, :],
                          reads=[f"fot{s}_{j}_{h}" for j in range(4) for h in range(2)], writes=[f"out{b}"])
                P.barrier()
        print("instructions:", P.nins)
    return nc_raw


def make_gvec(g_mix, g_ffn, g_kv, g_final):
    allg = np.concatenate([g_mix, g_ffn, g_kv[None], g_final[None]], axis=0).astype(np.float32)
    return np.ascontiguousarray(allg.reshape(10, KT, 128).transpose(2, 0, 1))


def host_inputs(inp):
    f32 = lambda a: np.ascontiguousarray(np.asarray(a, dtype=np.float32))
    cwt = np.concatenate([np.asarray(inp["ffn_conv_w"], np.float32),
                          np.asarray(inp["ffn_conv_b"], np.float32)[:, None, :]], axis=1)
    cw = cwt.reshape(4, 4, 2 * NFT, 128).transpose(0, 3, 2, 1)
    def pairlay(a):
        a = np.asarray(a, np.float32)
        rest = a.shape[3:]
        a = a.reshape((2, NPAIR, 2, 64) + rest)
        a = np.moveaxis(a, 1, 3)
        return a.reshape((2, 128, NPAIR) + rest)
    ldt = np.broadcast_to(np.asarray(inp["log_dt"], np.float32)[:, :, None], (2, 64, 64))
    s5p = np.stack([pairlay(inp["lam_re"]), pairlay(inp["lam_im"]), pairlay(ldt)], axis=-1)
    s5b = np.stack([pairlay(inp["ssm_b_re"]), pairlay(inp["ssm_b_im"])], axis=3)
    ct = lambda a: np.asarray(a, np.float32).transpose(0, 1, 3, 2)
    s5c = np.stack([pairlay(ct(inp["ssm_c_re"])), pairlay(ct(inp["ssm_c_im"]))], axis=3)
    dpad = np.zeros((2, NT96 * 96), np.float32)
    dpad[:, :D] = np.asarray(inp["ssm_d"], np.float32)
    dv96 = dpad.reshape(2, NT96, 96).transpose(0, 2, 1)
    return {
        "ident": np.eye(128, dtype=np.float32),
        "gvec": make_gvec(np.asarray(inp["g_mix"]), np.asarray(inp["g_ffn"]), np.asarray(inp["g_kv"]),
                          np.asarray(inp["g_final"])),
        "w_ffn_in": f32(inp["w_ffn_in"]), "w_ffn_out": f32(inp["w_ffn_out"]), "ffn_cw": f32(cw),
        "w_kvf": f32(inp["w_kvf"]), "b_f": f32(np.asarray(inp["b_f"]).reshape(NH, 1)),
        "w_q": f32(inp["w_q"]), "w_o": f32(inp["w_o"]),
        "tri": np.triu(np.ones((128, 128), np.float32)),
        "s5p": f32(s5p), "s5b": f32(s5b), "s5c": f32(s5c), "s5dv": f32(dv96), "w_glu": f32(inp["w_glu"]),
    }


PHASES = ("in", "s50", "ffn0", "s51", "ffn1", "kv", "fox2", "ffn2", "fox3", "ffn3", "final")


def kernel(**inp):
    x = np.asarray(inp["x"], np.float32)
    B, L, _ = x.shape
    nc = build(L, phases=inp.get("_phases", PHASES))
    shared = host_inputs(inp)
    in_maps = [dict(shared, x=np.ascontiguousarray(x[c])) for c in range(B)]
    res = run_bass_kernel_spmd(nc, in_maps, core_ids=list(range(B)))
    return np.stack([r["out"] for r in res.results], axis=0)
```

```python
import numpy as np
from contextlib import ExitStack
import concourse.bass as bass
import concourse.mybir as mybir
from concourse.bass_utils import run_bass_kernel_spmd

F32 = mybir.dt.float32
BF16 = mybir.dt.bfloat16
ALU = mybir.AluOpType
AF = mybir.ActivationFunctionType

D = 1024
KT = 8
NCORES = 8
EPS = 1e-6
TB = 512


STRICT = True


WC = {}


class NCW:
    def __init__(self, nc):
        self._nc = nc
        self._n = 0

    def __getattr__(self, k):
        return getattr(self._nc, k)

    def _u(self, name):
        self._n += 1
        return f"{name}_{self._n}"

    def sbuf_tensor(self, name, shape, dt):
        return self._nc.sbuf_tensor(self._u(name), shape, dt)

    def psum_tensor(self, name, shape, dt):
        return self._nc.psum_tensor(self._u(name), shape, dt)


class WinPrefetch:
    def __init__(self, P, nc, w_in_d, layer, stack=None):
        self.P, self.nc, self.w_in_d, self.layer, self.stack = P, nc, w_in_d, layer, stack
        self.win = None
        self.todo = [0, 2, 1, 3]

    def alloc(self, stack=None):
        stack = self.stack if self.stack is not None else stack
        self.win = stack.enter_context(self.nc.sbuf_tensor("f_win", [128, KT, 2 * 2816], BF16))

    def load_some(self, n):
        for _ in range(min(n, len(self.todo))):
            c = self.todo.pop(0)
            self.P.dma("sp", self.win[:, :, c * 1408:(c + 1) * 1408],
                       self.w_in_d[self.layer, :, c * 1408:(c + 1) * 1408].rearrange("(k p) f -> p k f", p=128),
                       reads=WC[f"in{self.layer}"], writes=[f"win_{c}"])


class Prog:
    CE = ("act", "pe", "dve", "pool")

    def __init__(self, nc, es, ring=12):
        self.nc = nc
        self.eng = {"sp": nc.sync, "act": nc.scalar, "pe": nc.tensor, "dve": nc.vector, "pool": nc.gpsimd}
        self.sem = {}
        for e in self.CE:
            self.sem["c_" + e] = es.enter_context(nc.semaphore("c_" + e))
        self.K = ring
        self.ringn = {}
        self.KR = {"sp": ring, "act": ring, "pool": ring, "wc": 10}
        for qn in ("sp", "act", "pool", "wc"):
            self.ringn[qn] = 0
            for k in range(self.KR[qn]):
                self.sem[f"d_{qn}_{k}"] = es.enter_context(nc.semaphore(f"d_{qn}_{k}"))
        self.cnt = {e: 0 for e in self.CE}
        self.res = {}
        self.seen = {e: {} for e in self.eng}
        self.last = {}
        self.nins = 0

    def _wait(self, eng, deps):
        best = {}
        for (s, v) in deps:
            if v > best.get(s, 0):
                best[s] = v
        for s, v in best.items():
            if self.seen[eng].get(s, 0) < v:
                self.eng[eng].wait_ge(self.sem[s], v)
                self.seen[eng][s] = v
                self.nins += 1

    def _deps(self, ident, reads, writes):
        deps = []
        for r in reads:
            st = self.res.get(r)
            if st and st[0] is not None:
                tok, who = st[0]
                if who == ident:
                    if ident != "pe":
                        deps.append(tok)
                else:
                    deps.append(tok)
        for w in writes:
            st = self.res.get(w)
            if st:
                if st[0] is not None:
                    tok, who = st[0]
                    if who != ident or (STRICT and ident != "pe"):
                        deps.append(tok)
                for (tok, who) in st[1]:
                    if who != ident or (STRICT and ident != "pe"):
                        deps.append(tok)
        return deps

    def _record(self, tok, ident, reads, writes):
        for r in reads:
            st = self.res.setdefault(r, [None, []])
            st[1].append((tok, ident))
            if len(st[1]) > 64:
                best = {}
                for (t, w) in st[1]:
                    if t[1] >= best.get((t[0], w), (0, 0))[0]:
                        best[(t[0], w)] = (t[1], (t, w))
                st[1] = [v[1] for v in best.values()]
        for w in writes:
            self.res[w] = [(tok, ident), []]
        self.last[tok[0]] = tok[1]

    def op(self, eng, fn, reads=(), writes=()):
        deps = self._deps(eng, reads, writes)
        self._wait(eng, deps)
        self.cnt[eng] += 1
        tok = ("c_" + eng, self.cnt[eng])
        ins = fn(self.eng[eng])
        ins.then_inc(self.sem[tok[0]], 1)
        self.nins += 1
        self._record(tok, eng, reads, writes)

    def dma(self, qn, out, in_, reads=(), writes=(), ring=None, **kw):
        ring = ring or qn
        i = self.ringn[ring]
        self.ringn[ring] += 1
        K = self.KR[ring]
        k, m = i % K, i // K
        s = f"d_{ring}_{k}"
        ident = ("dma", ring, i)
        deps = self._deps(ident, reads, writes)
        if m > 0:
            deps.append((s, 16 * m))
        self._wait(qn, deps)
        ins = self.eng[qn].dma_start(out=out, in_=in_, **kw)
        ins.then_inc(self.sem[s], 16)
        self.nins += 1
        self._record((s, 16 * (m + 1)), ident, reads, writes)

    def barrier(self):
        allt = [(s_, v) for s_, v in self.last.items() if not s_.startswith("d_wc_")]
        for e in self.eng:
            self._wait(e, allt)
        self.res = {k: v for k, v in self.res.items() if k.startswith("WC:")}


def rms_block(P, x, xres, sq, rr, ps, ones, epsb, gs, out, ores, pfx, n):
    P.op("act", lambda e: e.activation(out=sq[:, :, :n], in_=x[:, :, :n], func=AF.Square),
         reads=[xres], writes=[pfx + "sq"])
    for kt in range(KT):
        P.op("pe", lambda e, kt=kt: e.matmul(ps[:, :n], lhsT=ones[:, :], rhs=sq[:, kt, :n],
                                             start=(kt == 0), stop=(kt == KT - 1)),
             reads=[pfx + "sq", "ones"], writes=[pfx + "ps"])
    P.op("act", lambda e: e.activation(out=rr[:, :n], in_=ps[:, :n], func=AF.Sqrt, scale=1.0 / D, bias=epsb[:, 0:1]),
         reads=[pfx + "ps", "epsb"], writes=[pfx + "rr"])
    P.op("dve", lambda e: e.reciprocal(out=rr[:, :n], in_=rr[:, :n]), reads=[pfx + "rr"], writes=[pfx + "rr"])
    for kt in range(KT):
        P.op("dve", lambda e, kt=kt: e.scalar_tensor_tensor(out=out[:, kt, :n], in0=x[:, kt, :n],
                                                           scalar=gs[:, kt:kt + 1], in1=rr[:, :n],
                                                           op0=ALU.mult, op1=ALU.mult),
             reads=[xres, pfx + "rr", "gs"], writes=[ores])


FF = 2816
NFT = 22
TBF = 256


def ffn_phase(P, nc, L, layer, HT, w_in_d, w_out_d, cw_d, ones, epsb, gs, pf=None):
    NBF = L // TBF
    with ExitStack() as ph:
        if pf is None:
            pf = WinPrefetch(P, nc, w_in_d, layer)
        if pf.win is None:
            pf.alloc(ph)
        win = pf.win
        wout = ph.enter_context(nc.sbuf_tensor("f_wout", [128, NFT, D], BF16))
        cw = ph.enter_context(nc.sbuf_tensor("f_cw", [128, 2 * NFT, 4], F32))
        halo = ph.enter_context(nc.sbuf_tensor("f_halo", [128, 2 * NFT, 2], F32))
        xb = [ph.enter_context(nc.sbuf_tensor(f"f_xb{i}", [128, KT, TBF], F32)) for i in range(3)]
        sq = ph.enter_context(nc.sbuf_tensor("f_sq", [128, KT, TBF], BF16))
        rr = ph.enter_context(nc.sbuf_tensor("f_rr", [128, TBF], F32))
        xn = [ph.enter_context(nc.sbuf_tensor(f"f_xn{i}", [128, KT, TBF], BF16)) for i in range(2)]
        act = [ph.enter_context(nc.sbuf_tensor(f"f_act{i}", [128, NFT, TBF], BF16)) for i in range(2)]
        NU = 4
        ue = [ph.enter_context(nc.sbuf_tensor(f"f_ue{i}", [128, TBF + 2], F32)) for i in range(NU)]
        yy = [ph.enter_context(nc.sbuf_tensor(f"f_yy{i}", [128, TBF], F32)) for i in range(NU)]
        sg = [ph.enter_context(nc.sbuf_tensor(f"f_sg{i}", [128, TBF], F32)) for i in range(2)]
        psn = ph.enter_context(nc.psum_tensor("f_psn", [128, 512], F32))
        psu = ph.enter_context(nc.psum_tensor("f_psu", [128, 4, 512], F32))
        pso = ph.enter_context(nc.psum_tensor("f_pso", [128, 2, 512], F32))

        P.dma("sp", cw[:, :, :], cw_d[layer], writes=["cw"])
        P.op("pool", lambda e: e.memset(halo[:, :, :], 0.0), writes=[f"halo{i}" for i in range(2 * NFT)])
        pf.load_some(4)
        P.dma("sp", wout[:, :, :], w_out_d[layer].rearrange("(f p) d -> p f d", p=128), reads=WC[f"out{layer}"],
              writes=[f"wout{ft}" for ft in range(NFT)])

        def load(b):
            s3 = b % 3
            t0 = b * TBF
            P.dma("sp", xb[s3][:, :, :], HT[:, :, t0:t0 + TBF].rearrange("k p t -> p k t"), writes=[f"fb{s3}x"])

        def norm(b):
            rms_block(P, xb[b % 3], f"fb{b % 3}x", sq, rr, psn, ones, epsb, gs[:, 4 + layer, :], xn[b % 2], f"fxn{b % 2}", "fn_", TBF)

        def phase2_items(b):
            s3, s2 = b % 3, b % 2
            t0 = b * TBF
            items = []
            for m in range(KT):
                bk = m % 2
                for ft in range(NFT):
                    items.append(lambda m=m, ft=ft, bk=bk: P.op("pe", lambda e: e.matmul(
                        pso[:, bk, :TBF], lhsT=wout[:, ft, m * 128:(m + 1) * 128], rhs=act[s2][:, ft, :],
                        start=(ft == 0), stop=(ft == NFT - 1)),
                        reads=[f"wout{ft}", f"act{s2}_{ft}"], writes=[f"pso{bk}"]))
                items.append(lambda m=m, bk=bk: P.op("dve", lambda e: e.tensor_tensor(
                    out=xb[s3][:, m, :], in0=xb[s3][:, m, :], in1=pso[:, bk, :TBF], op=ALU.add),
                    reads=[f"pso{bk}", f"fb{s3}x"], writes=[f"fb{s3}o{m}"]))
            items.append(lambda: P.dma("pool", HT[:, :, t0:t0 + TBF].rearrange("k p t -> p k t"), xb[s3][:, :, :],
                                       reads=[f"fb{s3}o{m}" for m in range(KT)] + [f"fb{s3}x"], writes=[f"HTo{b}"]))
            return items

        ui = 0
        load(0)
        norm(0)
        for b in range(NBF):
            s2 = b % 2
            if b + 1 < NBF:
                load(b + 1)
            p2 = phase2_items(b - 1) if b > 0 else []
            per = (len(p2) + NFT - 1) // NFT if p2 else 0
            for ft in range(NFT):
                bufs = []
                for gu in range(2):
                    f = ft + gu * NFT
                    bk = (2 * ft + gu) % 4
                    u = ui % NU
                    ui += 1
                    bufs.append(u)
                    for kt in range(KT):
                        P.op("pe", lambda e, kt=kt, f=f, bk=bk: e.matmul(
                            psu[:, bk, :TBF], lhsT=win[:, kt, f * 128:(f + 1) * 128], rhs=xn[s2][:, kt, :],
                            start=(kt == 0), stop=(kt == KT - 1)),
                            reads=[f"win_{f // 11}", f"fxn{s2}"], writes=[f"psu{bk}"])
                    P.op("pool", lambda e, u=u, f=f: e.tensor_copy(out=ue[u][:, 0:2], in_=halo[:, f, :]),
                         reads=[f"halo{f}"], writes=[f"ueh{u}"])
                    P.op("act", lambda e, u=u, bk=bk: e.copy(out=ue[u][:, 2:TBF + 2], in_=psu[:, bk, :TBF]),
                         reads=[f"psu{bk}"], writes=[f"ue{u}"])
                    P.op("act", lambda e, u=u, bk=bk, f=f: e.activation(
                        out=yy[u][:, :], in_=psu[:, bk, :TBF], func=AF.Identity,
                        scale=cw[:, f, 2:3], bias=cw[:, f, 3:4]),
                        reads=[f"psu{bk}", "cw"], writes=[f"yy{u}"])
                    P.op("pool", lambda e, u=u, f=f: e.tensor_copy(out=halo[:, f, :], in_=ue[u][:, TBF:TBF + 2]),
                         reads=[f"ue{u}"], writes=[f"halo{f}"])
                    P.op("dve", lambda e, u=u, f=f: e.scalar_tensor_tensor(
                        out=yy[u][:, :], in0=ue[u][:, 1:TBF + 1], scalar=cw[:, f, 1:2], in1=yy[u][:, :],
                        op0=ALU.mult, op1=ALU.add), reads=[f"ue{u}", f"ueh{u}", f"yy{u}", "cw"], writes=[f"yy{u}"])
                    P.op("dve", lambda e, u=u, f=f: e.scalar_tensor_tensor(
                        out=yy[u][:, :], in0=ue[u][:, 0:TBF], scalar=cw[:, f, 0:1], in1=yy[u][:, :],
                        op0=ALU.mult, op1=ALU.add), reads=[f"ue{u}", f"ueh{u}", f"yy{u}", "cw"], writes=[f"yy{u}"])
                ug, uu = bufs
                sgi = ft % 2
                P.op("act", lambda e, ug=ug, sgi=sgi: e.activation(out=sg[sgi][:, :], in_=yy[ug][:, :], func=AF.Silu),
                     reads=[f"yy{ug}"], writes=[f"sg{sgi}"])
                P.op("dve", lambda e, uu=uu, sgi=sgi, ft=ft: e.tensor_tensor(
                    out=act[s2][:, ft, :], in0=sg[sgi][:, :], in1=yy[uu][:, :], op=ALU.mult),
                    reads=[f"sg{sgi}", f"yy{uu}"], writes=[f"act{s2}_{ft}"])
                for it in p2[ft * per:(ft + 1) * per]:
                    it()
                if ft == 8 and b + 1 < NBF:
                    norm(b + 1)
        for it in phase2_items(NBF - 1):
            it()
        P.barrier()


NH = 16
HD = 64
KA = 70


def kv_phase(P, nc, L, HT, w_kvf_d, bf_d, KTd, QAd, VTd, ones, epsb, gs):
    NB = L // TB
    with ExitStack() as ph:
        wk = ph.enter_context(nc.sbuf_tensor("k_w", [128, KT, 2064], BF16))
        xb = [ph.enter_context(nc.sbuf_tensor(f"k_xb{i}", [128, KT, TB], F32)) for i in range(2)]
        sq = ph.enter_context(nc.sbuf_tensor("k_sq", [128, KT, TB], BF16))
        rr = ph.enter_context(nc.sbuf_tensor("k_rr", [128, TB], F32))
        xnb = [ph.enter_context(nc.sbuf_tensor(f"k_xn{i}", [128, KT, TB], BF16)) for i in range(2)]
        kst = [ph.enter_context(nc.sbuf_tensor(f"k_kst{i}", [128, 8, TB], BF16)) for i in range(2)]
        vst = [ph.enter_context(nc.sbuf_tensor(f"k_vst{i}", [128, 4, NH, HD + 1], BF16)) for i in range(2)]
        negc = ph.enter_context(nc.sbuf_tensor("k_negc", [NH, L], F32))
        nbf = ph.enter_context(nc.sbuf_tensor("k_nbf", [NH, 1], F32))
        one1 = ph.enter_context(nc.sbuf_tensor("k_one1", [NH, 1], F32))
        onesf = ph.enter_context(nc.sbuf_tensor("k_onesf", [NH, TB], F32))
        e1 = ph.enter_context(nc.sbuf_tensor("k_e1", [NH, TB], F32))
        r1 = ph.enter_context(nc.sbuf_tensor("k_r1", [NH, TB], F32))
        sp3 = [ph.enter_context(nc.sbuf_tensor(f"k_sp3{i}", [NH, 3, TB], BF16)) for i in range(2)]
        ng3 = [ph.enter_context(nc.sbuf_tensor(f"k_ng3{i}", [NH, 3, TB], BF16)) for i in range(2)]
        on3 = ph.enter_context(nc.sbuf_tensor("k_on3", [NH, 3, TB], BF16))
        psn = ph.enter_context(nc.psum_tensor("k_psn", [128, 512], F32))
        psk = ph.enter_context(nc.psum_tensor("k_psk", [128, 3, 512], F32))
        psv = ph.enter_context(nc.psum_tensor("k_psv", [128, 3, 512], F32))
        psf = ph.enter_context(nc.psum_tensor("k_psf", [128, 512], F32))

        for kt in range(KT):
            P.dma("sp", wk[:, kt, :], w_kvf_d[kt * 128:(kt + 1) * 128, :], reads=WC["kvf"], writes=[f"wk{kt}"])
        wkr = [f"wk{kt}" for kt in range(KT)]
        P.dma("sp", nbf[:, :], bf_d[:, :], writes=["nbf"])
        P.op("dve", lambda e: e.tensor_scalar(out=nbf[:, :], in0=nbf[:, :], scalar1=-1.0, scalar2=None, op0=ALU.mult),
             reads=["nbf"], writes=["nbf"])
        P.op("dve", lambda e: e.memset(one1[:, :], 1.0), writes=["one1"])
        P.op("dve", lambda e: e.memset(onesf[:, :], 1.0), writes=["onesf"])
        P.op("dve", lambda e: e.memset(on3[:, :, :], 1.0), writes=["on3"])
        for i in range(2):
            P.op("pool", lambda e, i=i: e.memset(vst[i][:, :, :, :], 1.0), writes=[f"vst{i}"])

        def kv_norm(b):
            s_ = b % 2
            P.dma("sp", xb[s_][:, :, :], HT[:, :, b * TB:(b + 1) * TB].rearrange("k p t -> p k t"), writes=[f"kb{s_}x"])
            rms_block(P, xb[s_], f"kb{s_}x", sq, rr, psn, ones, epsb, gs[:, 8, :], xnb[s_], f"kxn{s_}", "kn_", TB)

        kv_norm(0)
        for b in range(NB):
            s = b % 2
            t0 = b * TB
            xn = xnb[s]
            kxn = f"kxn{s}"
            if b + 1 < NB:
                kv_norm(b + 1)
            for m in range(8):
                bk = (b * 8 + m) % 3
                for kt in range(KT):
                    P.op("pe", lambda e, m=m, kt=kt, bk=bk: e.matmul(
                        psk[:, bk, :], lhsT=wk[:, kt, m * 128:(m + 1) * 128], rhs=xn[:, kt, :],
                        start=(kt == 0), stop=(kt == KT - 1)), reads=wkr + [kxn], writes=[f"psk{bk}"])
                if m % 2 == 0:
                    P.op("act", lambda e, m=m, bk=bk, s=s: e.copy(out=kst[s][:, m, :], in_=psk[:, bk, :]),
                         reads=[f"psk{bk}"], writes=[f"kst{s}_{m}"])
                else:
                    P.op("dve", lambda e, m=m, bk=bk, s=s: e.tensor_copy(out=kst[s][:, m, :], in_=psk[:, bk, :]),
                         reads=[f"psk{bk}"], writes=[f"kst{s}_{m}"])
            for two in range(2):
                P.dma("pool", KTd[two::2, 0:HD, t0:t0 + TB].rearrange("m r t -> r m t"),
                      kst[s][two * 64:(two + 1) * 64, :, :],
                      reads=[f"kst{s}_{m}" for m in range(8)], writes=[f"KTd{b}_{two}"])
            for st in range(4):
                for c in range(2):
                    bk = (b * 8 + st * 2 + c) % 3
                    for kt in range(KT):
                        P.op("pe", lambda e, st=st, c=c, kt=kt, bk=bk: e.matmul(
                            psv[:, bk, :], lhsT=xn[:, kt, st * 128:(st + 1) * 128],
                            rhs=wk[:, kt, 1024 + c * 512:1024 + (c + 1) * 512],
                            start=(kt == 0), stop=(kt == KT - 1)), reads=wkr + [kxn], writes=[f"psv{bk}"])
                    src = psv[:, bk, :].rearrange("p (h d) -> p h d", d=HD)
                    if c == 0:
                        P.op("act", lambda e, st=st, c=c, s=s, src=src: e.copy(out=vst[s][:, st, c * 8:(c + 1) * 8, 0:HD], in_=src),
                             reads=[f"psv{bk}", f"vst{s}"], writes=[f"vst{s}_{st}_{c}"])
                    else:
                        P.op("dve", lambda e, st=st, c=c, s=s, src=src: e.tensor_copy(out=vst[s][:, st, c * 8:(c + 1) * 8, 0:HD], in_=src),
                             reads=[f"psv{bk}", f"vst{s}"], writes=[f"vst{s}_{st}_{c}"])
            P.dma("pool", VTd[b * 4:(b + 1) * 4, :, :, :].rearrange("s p h c -> p s (h c)"),
                  vst[s][:, :, :, :].rearrange("p s h c -> p s (h c)"),
                  reads=[f"vst{s}"] + [f"vst{s}_{st}_{c}" for st in range(4) for c in range(2)], writes=[f"VTd{b}"])
            for kt in range(KT):
                P.op("pe", lambda e, kt=kt: e.matmul(psf[0:NH, :], lhsT=wk[:, kt, 2048:2064], rhs=xn[:, kt, :],
                                                     start=(kt == 0), stop=(kt == KT - 1)),
                     reads=wkr + [kxn], writes=["psf"])
            P.op("act", lambda e: e.activation(out=e1[:, :], in_=psf[0:NH, :], func=AF.Exp, scale=-1.0, bias=nbf[:, 0:1]),
                 reads=["psf", "nbf"], writes=["e1"])
            P.op("act", lambda e: e.activation(out=e1[:, :], in_=e1[:, :], func=AF.Ln, scale=1.0, bias=one1[:, 0:1]),
                 reads=["e1", "one1"], writes=["e1"])
            init = 0.0 if b == 0 else negc[:, t0 - 1:t0]
            P.op("dve", lambda e, t0=t0, init=init: e.tensor_tensor_scan(
                out=negc[:, t0:t0 + TB], data0=onesf[:, :], data1=e1[:, :], initial=init, op0=ALU.mult, op1=ALU.add),
                reads=["e1", "onesf", "negc"], writes=["negc"])
            cb = negc[:, t0:t0 + TB]
            P.op("dve", lambda e, s=s, cb=cb: e.tensor_copy(out=sp3[s][:, 0, :], in_=cb), reads=["negc"], writes=[f"sp3{s}"])
            P.op("dve", lambda e, s=s, cb=cb: e.tensor_tensor(out=r1[:, :], in0=cb, in1=sp3[s][:, 0, :], op=ALU.subtract),
                 reads=["negc", f"sp3{s}"], writes=["r1"])
            P.op("dve", lambda e, s=s: e.tensor_copy(out=sp3[s][:, 1, :], in_=r1[:, :]), reads=["r1", f"sp3{s}"], writes=[f"sp3{s}"])
            P.op("dve", lambda e, s=s: e.tensor_tensor(out=r1[:, :], in0=r1[:, :], in1=sp3[s][:, 1, :], op=ALU.subtract),
                 reads=["r1", f"sp3{s}"], writes=["r1"])
            P.op("dve", lambda e, s=s: e.tensor_copy(out=sp3[s][:, 2, :], in_=r1[:, :]), reads=["r1", f"sp3{s}"], writes=[f"sp3{s}"])
            P.op("dve", lambda e, s=s: e.tensor_scalar(out=ng3[s][:, :, :], in0=sp3[s][:, :, :], scalar1=-1.0, scalar2=None,
                                                       op0=ALU.mult), reads=[f"sp3{s}"], writes=[f"ng3{s}"])
            P.dma("pool", KTd[:, HD:HD + 3, t0:t0 + TB], sp3[s][:, :, :], reads=[f"sp3{s}"], writes=[f"KTa{b}"])
            P.dma("pool", KTd[:, HD + 3:HD + 6, t0:t0 + TB], on3[:, :, :], reads=["on3"], writes=[f"KTb{b}"])
            P.dma("pool", QAd[:, 0:3, t0:t0 + TB], on3[:, :, :], reads=["on3"], writes=[f"QAa{b}"])
            P.dma("pool", QAd[:, 3:6, t0:t0 + TB], ng3[s][:, :, :], reads=[f"ng3{s}"], writes=[f"QAb{b}"])
        P.barrier()


def fox_phase(P, nc, L, layer, HT, w_q_d, w_o_d, KTd, QAd, VTd, OTd, QTd, tri_d, ones, epsb, gs, ident, pf=None):
    j = layer - 2
    NB = L // TB
    NST = L // 128
    with ExitStack() as ph_outer:
        ph = ExitStack()
        xn = ph.enter_context(nc.sbuf_tensor("a_xn", [128, KT, L], BF16))
        wq = ph.enter_context(nc.sbuf_tensor("a_wq", [128, KT, D], BF16))
        trif = ph.enter_context(nc.sbuf_tensor("a_trif", [128, 128], F32))
        tri = ph.enter_context(nc.sbuf_tensor("a_tri", [128, 128], BF16))
        bc1 = ph.enter_context(nc.sbuf_tensor("a_bc1", [128, HD], F32))
        P.dma("sp", wq[:, :, :], w_q_d[j].rearrange("(k p) d -> p k d", p=128), reads=WC[f"q{j}"], writes=["wq"])
        wqr = ["wq"]
        P.dma("sp", trif[:, :], tri_d[:, :], writes=["trif"])
        P.op("dve", lambda e: e.tensor_scalar(out=tri[:, :], in0=trif[:, :], scalar1=-1.0, scalar2=30000.0, op0=ALU.add, op1=ALU.mult),
             reads=["trif"], writes=["tri"])
        identb = ph.enter_context(nc.sbuf_tensor("a_identb", [128, 128], BF16))
        P.op("dve", lambda e: e.tensor_copy(out=identb[:, :], in_=ident[:, :]), reads=["ident"], writes=["identb"])
        P.op("dve", lambda e: e.memset(bc1[:, :], 1.0), writes=["bc1"])

        with ExitStack() as p1:
            xb = [p1.enter_context(nc.sbuf_tensor(f"a_xb{i}", [128, KT, TB], F32)) for i in range(2)]
            sq = p1.enter_context(nc.sbuf_tensor("a_sq", [128, KT, TB], BF16))
            rr = p1.enter_context(nc.sbuf_tensor("a_rr", [128, TB], F32))
            psn = p1.enter_context(nc.psum_tensor("a_psn", [128, 512], F32))
            psqp = p1.enter_context(nc.psum_tensor("a_psqp", [128, 2, 512], F32))
            qst = [p1.enter_context(nc.sbuf_tensor(f"a_qst{i}", [128, 8, TB], BF16)) for i in range(2)]
            def a_norm(b):
                s_ = b % 2
                P.dma("sp", xb[s_][:, :, :], HT[:, :, b * TB:(b + 1) * TB].rearrange("k p t -> p k t"), writes=[f"ab{s_}x"])
                rms_block(P, xb[s_], f"ab{s_}x", sq, rr, psn, ones, epsb, gs[:, layer, :], xn[:, :, b * TB:(b + 1) * TB],
                          f"axn{b}", "an_", TB)

            a_norm(0)
            for b in range(NB):
                s = b % 2
                if b + 1 < NB:
                    a_norm(b + 1)
                for m in range(8):
                    bk = m % 2
                    for kt in range(KT):
                        P.op("pe", lambda e, m=m, kt=kt, bk=bk, b=b: e.matmul(
                            psqp[:, bk, :], lhsT=wq[:, kt, m * 128:(m + 1) * 128], rhs=xn[:, kt, b * TB:(b + 1) * TB],
                            start=(kt == 0), stop=(kt == KT - 1)), reads=wqr + [f"axn{b}"], writes=[f"psqp{bk}"])
                    if m % 2 == 0:
                        P.op("act", lambda e, m=m, bk=bk, s=s: e.activation(out=qst[s][:, m, :], in_=psqp[:, bk, :], func=AF.Copy, scale=float(HD ** -0.5)),
                             reads=[f"psqp{bk}"], writes=[f"qst{s}_{m}"])
                    else:
                        P.op("dve", lambda e, m=m, bk=bk, s=s: e.tensor_scalar(out=qst[s][:, m, :], in0=psqp[:, bk, :], scalar1=float(HD ** -0.5),
                                                                              scalar2=None, op0=ALU.mult),
                             reads=[f"psqp{bk}"], writes=[f"qst{s}_{m}"])
                for two in range(2):
                    P.dma("pool", QTd[two::2, :, b * TB:(b + 1) * TB].rearrange("m r t -> r m t"),
                          qst[s][two * 64:(two + 1) * 64, :, :],
                          reads=[f"qst{s}_{m}" for m in range(8)], writes=[f"QTd{b}_{two}"])
            P.barrier()
        xnr = []

        with ExitStack() as p2:
            qT = [p2.enter_context(nc.sbuf_tensor(f"a_qT{i}", [KA, L], BF16)) for i in range(2)]
            kT = [p2.enter_context(nc.sbuf_tensor(f"a_kT{i}", [KA, L], BF16)) for i in range(2)]
            vh = [p2.enter_context(nc.sbuf_tensor(f"a_vh{i}", [128, NST, 128], BF16)) for i in range(2)]
            NG = 3
            NP = 4
            LA = 2
            pT = [p2.enter_context(nc.sbuf_tensor(f"a_pT{i}", [128, 2, TB], BF16)) for i in range(NP)]
            osb = [p2.enter_context(nc.sbuf_tensor(f"a_osb{i}", [128, TB], F32)) for i in range(2)]
            rc = p2.enter_context(nc.sbuf_tensor("a_rc", [128, TB], F32))
            oT = [p2.enter_context(nc.sbuf_tensor(f"a_oT{i}", [HD, TB], BF16)) for i in range(2)]
            psS = p2.enter_context(nc.psum_tensor("a_psS", [128, NG, 2, 512], F32))
            psO = p2.enter_context(nc.psum_tensor("a_psO", [128, 512], F32))
            psX = p2.enter_context(nc.psum_tensor("a_psX", [128, 512], F32))
            oi = 0
            P.op("pool", lambda e: e.memset(rc[:, :], 0.0), writes=["rc"])
            for i in range(2):
                P.op("pool", lambda e, i=i: e.memset(vh[i][:, :, :], 1.0), writes=[f"vh{i}"])
            gn = 0
            for h in range(NH):
                hs = h % 2
                P.dma("sp", kT[hs][:, :], KTd[h, :, :], writes=[f"kT{hs}"])
                P.dma("sp", vh[hs][:, :, 0:HD + 1], VTd[:, :, h, :].rearrange("s p c -> p s c"), writes=[f"vh{hs}"])
                P.dma("sp", qT[hs][HD:KA, :], QAd[h, :, :], writes=[f"qTa{hs}"])
                P.dma("sp", qT[hs][0:HD, :], QTd[h, :, :], writes=[f"qT{hs}"])
                jobs = [(qb, g) for qb in range(NB) for g in range(2 * (qb + 1))]

                def lo_of(qb, st):
                    return 0 if st < 4 * qb else (st - 4 * qb) * 128

                def emit_qk(n, g_, hs=hs):
                    qb, g = jobs[n]
                    gb = g_ % NG
                    pb = g_ % NP
                    los = []
                    for t in range(2):
                        st = 2 * g + t
                        lo = lo_of(qb, st)
                        los.append(lo)
                        diag = st >= 4 * qb
                        P.op("pe", lambda e, st=st, lo=lo, t=t, diag=diag: e.matmul(
                            psS[:, gb, t, lo:TB], lhsT=kT[hs][:, st * 128:(st + 1) * 128],
                            rhs=qT[hs][:, qb * TB + lo:(qb + 1) * TB], start=True, stop=not diag),
                            reads=[f"kT{hs}", f"qTa{hs}", f"qT{hs}"], writes=[f"psS{gb}"])
                        if diag:
                            P.op("pe", lambda e, lo=lo, t=t: e.matmul(
                                psS[:, gb, t, lo:lo + 128], lhsT=identb[:, :], rhs=tri[:, :], start=False, stop=True),
                                reads=["tri", "identb"], writes=[f"psS{gb}"])
                    if los == [0, 0]:
                        P.op("act", lambda e: e.activation(out=pT[pb][:, :, :], in_=psS[:, gb, :, :], func=AF.Exp),
                             reads=[f"psS{gb}"], writes=[f"pT{pb}"])
                    else:
                        for t in range(2):
                            lo = los[t]
                            P.op("act", lambda e, t=t, lo=lo: e.activation(out=pT[pb][:, t, lo:TB], in_=psS[:, gb, t, lo:TB], func=AF.Exp),
                                 reads=[f"psS{gb}"], writes=[f"pT{pb}"])

                def emit_pv(n, g_, hs=hs):
                    qb, g = jobs[n]
                    pb = g_ % NP
                    nst = 4 * (qb + 1)
                    for t in range(2):
                        st = 2 * g + t
                        lo = lo_of(qb, st)
                        P.op("pe", lambda e, st=st, lo=lo, t=t: e.matmul(
                            psO[:, lo:TB], lhsT=vh[hs][:, st, :], rhs=pT[pb][:, t, lo:TB],
                            start=(st == 0), stop=(st == nst - 1)),
                            reads=[f"vh{hs}", f"pT{pb}"], writes=["psO"])

                pending = []
                for a_ in range(min(LA, len(jobs))):
                    emit_qk(a_, gn + a_)
                for n in range(len(jobs)):
                    qb, g = jobs[n]
                    if n + LA < len(jobs):
                        emit_qk(n + LA, gn + n + LA)
                    emit_pv(n, gn + n)
                    for p_ in pending:
                        p_[0] -= 1
                    while pending and pending[0][0] <= 0:
                        pending.pop(0)[1]()
                    if g == 2 * (qb + 1) - 1:
                        ob = oi % 2
                        oi += 1
                        P.op("dve", lambda e, ob=ob: e.tensor_copy(out=osb[ob][:, :], in_=psO[:, :]),
                             reads=["psO"], writes=[f"osb{ob}"])
                        P.op("dve", lambda e, ob=ob: e.reciprocal(out=rc[HD:128, :], in_=osb[ob][HD:128, :]),
                             reads=[f"osb{ob}"], writes=["rc"])

                        def tail(ob=ob, qb=qb, h=h):
                            P.op("pe", lambda e: e.matmul(psX[0:HD, :], lhsT=ident[:, HD:128], rhs=rc[:, :], start=True, stop=True),
                                 reads=["rc", "ident"], writes=["psX"])
                            P.op("dve", lambda e: e.tensor_tensor(out=oT[ob][:, :], in0=osb[ob][0:HD, :], in1=psX[0:HD, :], op=ALU.mult),
                                 reads=[f"osb{ob}", "psX"], writes=[f"oT{ob}"])
                            P.dma("pool", OTd[h, :, qb * TB:(qb + 1) * TB], oT[ob][:, :], reads=[f"oT{ob}"], writes=[f"OTd{h}_{qb}"])
                        pending.append([4, tail])
                for p_ in pending:
                    p_[1]()
                gn += len(jobs)
            P.barrier()

        ph.close()
        if pf is not None:
            pf.alloc()
        with ExitStack() as p3:
            wo = p3.enter_context(nc.sbuf_tensor("a_wo", [128, KT, D], BF16))
            P.dma("sp", wo[:, :, :], w_o_d[j].rearrange("(k p) d -> p k d", p=128), reads=WC[f"o{j}"], writes=["wo"])
            wor = ["wo"]
            xb = [p3.enter_context(nc.sbuf_tensor(f"a_xc{i}", [128, KT, TB], F32)) for i in range(2)]
            ot = [p3.enter_context(nc.sbuf_tensor(f"a_ot{i}", [128, KT, TB], BF16)) for i in range(2)]
            pso = p3.enter_context(nc.psum_tensor("a_pso", [128, 2, 512], F32))
            for b in range(NB):
                s = b % 2
                t0 = b * TB
                P.dma("sp", xb[s][:, :, :], HT[:, :, t0:t0 + TB].rearrange("k p t -> p k t"), writes=[f"ac{s}x"])
                P.dma("sp", ot[s][:, :, :], OTd[:, :, t0:t0 + TB].rearrange("(k h2) r t -> (h2 r) k t", h2=2), writes=[f"ot{s}"])
                if pf is not None and b % 2 == 1:
                    pf.load_some(1)
                for m in range(KT):
                    bk = m % 2
                    for h in range(KT):
                        P.op("pe", lambda e, m=m, h=h, bk=bk, s=s: e.matmul(
                            pso[:, bk, :], lhsT=wo[:, h, m * 128:(m + 1) * 128], rhs=ot[s][:, h, :],
                            start=(h == 0), stop=(h == KT - 1)), reads=wor + [f"ot{s}"], writes=[f"pso{bk}"])
                    P.op("dve", lambda e, m=m, bk=bk, s=s: e.tensor_tensor(
                        out=xb[s][:, m, :], in0=xb[s][:, m, :], in1=pso[:, bk, :], op=ALU.add),
                        reads=[f"pso{bk}", f"ac{s}x"], writes=[f"ac{s}o{m}"])
                P.dma("pool", HT[:, :, t0:t0 + TB].rearrange("k p t -> p k t"), xb[s][:, :, :],
                      reads=[f"ac{s}o{m}" for m in range(KT)] + [f"ac{s}x"], writes=[f"HTo{b}"])
            P.barrier()


NPAIR = 32
NT96 = 11
TC = 8


def s5_phase(P, nc, L, layer, HT, UTd, Gd, s5p_d, s5b_d, s5c_d, dv_d, w_glu_d, ones, epsb, gs, ident, after_norm=None, pf=None):
    NB = L // TB
    NC = L // TC
    PADC = NC // 2
    NLEV = int(np.ceil(np.log2(NC)))
    assert NC <= 512
    with ExitStack() as ph_outer:
        ph = ExitStack()
        Kb = ph.enter_context(nc.sbuf_tensor("s_Kb", [96, NT96, TC, 96], BF16))
        BE = ph.enter_context(nc.sbuf_tensor("s_BE", [96, NT96, TC, 2, 128], BF16))
        CE = ph.enter_context(nc.sbuf_tensor("s_CE", [128, NPAIR, TC, 2, 32], BF16))
        APW = ph.enter_context(nc.sbuf_tensor("s_APW", [128, 10, 2, NPAIR], F32))
        NAI = ph.enter_context(nc.sbuf_tensor("s_NAI", [128, 10, NPAIR], F32))
        dv = ph.enter_context(nc.sbuf_tensor("s_dv", [96, NT96], F32))
        P.dma("sp", dv[:, :], dv_d[layer], writes=["dv"])

        with ExitStack() as p0:
            T = lambda name, shape, dt=F32: p0.enter_context(nc.sbuf_tensor("s0_" + name, shape, dt))
            prm = T("prm", [128, NPAIR, 3])
            Bm = T("Bm", [128, NPAIR, 2, 16])
            Cm = T("Cm", [128, NPAIR, 2, 16])
            PW = T("PW", [128, 9, 2, NPAIR])
            dt_ = T("dt", [128, NPAIR]); ar = T("ar", [128, NPAIR]); ai = T("ai", [128, NPAIR])
            t1 = T("t1", [128, NPAIR]); t2 = T("t2", [128, NPAIR]); t3 = T("t3", [128, NPAIR])
            zz = T("zz", [128, 5, 2, NPAIR])
            Fr = T("Fr", [128, 2, NPAIR])
            hpi = T("hpi", [128, 1])
            Bb = T("Bb", [128, 2, NPAIR, 16])
            W = [T(f"W{i}", [128, 2, NPAIR, 16]) for i in range(2)]
            u1 = T("u1", [128, NPAIR, 16]); u2 = T("u2", [128, NPAIR, 16])
            Cz = T("Cz", [128, 2, 33 * 32], BF16)
            Zall = [T(f"Z{i}", [128, 2, 33 * 32], BF16) for i in range(2)]
            identb = T("identb", [128, 128], BF16)
            psT = p0.enter_context(nc.psum_tensor("s0_psT", [128, 4, 2, 512], BF16))
            psK = p0.enter_context(nc.psum_tensor("s0_psK", [128, 3, 512], F32))

            n_xb = [T(f"nxb{i}", [128, KT, TB]) for i in range(2)]
            n_sq = T("nsq", [128, KT, TB], BF16)
            n_rr = T("nrr", [128, TB])
            n_xn = [T(f"nxn{i}", [128, KT, TB], BF16) for i in range(2)]
            n_ps = p0.enter_context(nc.psum_tensor("s0_psn", [128, 512], F32))

            def n_load(b):
                if b < NB:
                    P.dma("sp", n_xb[b % 2][:, :, :], HT[:, :, b * TB:(b + 1) * TB].rearrange("k p t -> p k t"), writes=[f"sb{b % 2}x"])

            def n_block(b):
                if b >= NB:
                    return
                s_ = b % 2
                n_load(b + 1)
                rms_block(P, n_xb[s_], f"sb{s_}x", n_sq, n_rr, n_ps, ones, epsb, gs[:, layer, :], n_xn[s_], f"sxn{s_}", "sn_", TB)
                P.dma("pool", UTd[:, b * TB:(b + 1) * TB].rearrange("(k p) t -> p k t", p=128), n_xn[s_][:, :, :],
                      reads=[f"sxn{s_}"], writes=[f"UTd{b}"])

            n_load(0)
            P.dma("sp", prm[:, :, :], s5p_d[layer], writes=["prm"])
            P.dma("sp", Bm[:, :, :, :], s5b_d[layer], writes=["Bm"])
            P.dma("sp", Cm[:, :, :, :], s5c_d[layer], writes=["Cm"])
            P.op("dve", lambda e: e.memset(hpi[:, :], float(np.pi / 2)), writes=["hpi"])
            P.op("dve", lambda e: e.tensor_copy(out=identb[:, :], in_=ident[:, :]), reads=["ident"], writes=["identb"])
            P.op("pool", lambda e: e.memset(Kb[:, :, :, :], 0.0), writes=["Kb"])
            for i in range(3):
                P.op("dve", lambda e, i=i: e.memset(psK[:, i, :], 0.0), writes=[f"psK{i}"])
            P.op("pool", lambda e: e.memset(CE[:, :, :, :, :], 0.0), writes=["CE"])
            P.op("pool", lambda e: e.memset(Cz[:, :, :], 0.0), writes=["Cz"])
            for i in range(2):
                P.op("pool", lambda e, i=i: e.memset(Zall[i][:, :, :], 0.0), writes=[f"Z{i}"])
            lr, li, ldt = prm[:, :, 0], prm[:, :, 1], prm[:, :, 2]

            def dve(fn, reads, writes):
                P.op("dve", fn, reads=reads, writes=writes)

            def tt(out, a, b, op, r, w):
                dve(lambda e: e.tensor_tensor(out=out, in0=a, in1=b, op=op), r, w)

            def cmul(o_re, o_im, a_re, a_im, b_re, b_im, r, w):
                tt(t1[:, :], a_re, b_re, ALU.mult, r, ["t1"])
                tt(t2[:, :], a_im, b_im, ALU.mult, r, ["t2"])
                tt(o_re, t1[:, :], t2[:, :], ALU.subtract, ["t1", "t2"], w)
                tt(t1[:, :], a_re, b_im, ALU.mult, r + w, ["t1"])
                tt(t2[:, :], a_im, b_re, ALU.mult, r + w, ["t2"])
                tt(o_im, t1[:, :], t2[:, :], ALU.add, ["t1", "t2"], w)

            P.op("act", lambda e: e.activation(out=dt_[:, :], in_=ldt, func=AF.Exp), reads=["prm"], writes=["dt"])
            tt(ar[:, :], lr, dt_[:, :], ALU.mult, ["prm", "dt"], ["ar"])
            tt(ai[:, :], li, dt_[:, :], ALU.mult, ["prm", "dt"], ["ai"])
            P.op("act", lambda e: e.activation(out=t3[:, :], in_=ar[:, :], func=AF.Exp, scale=1.0 / 16), reads=["ar"], writes=["t3"])
            P.op("act", lambda e: e.activation(out=zz[:, 0, 1, :], in_=ai[:, :], func=AF.Sin, scale=1.0 / 16), reads=["ai"], writes=["zs"])
            P.op("act", lambda e: e.activation(out=zz[:, 0, 0, :], in_=ai[:, :], func=AF.Sin, scale=-1.0 / 16, bias=hpi[:, 0:1]),
                 reads=["ai", "hpi"], writes=["zc"])
            tt(zz[:, 0, 0, :], zz[:, 0, 0, :], t3[:, :], ALU.mult, ["zc", "t3"], ["zz0"])
            tt(zz[:, 0, 1, :], zz[:, 0, 1, :], t3[:, :], ALU.mult, ["zs", "t3"], ["zz0"])
            for k in range(4):
                cmul(zz[:, k + 1, 0, :], zz[:, k + 1, 1, :], zz[:, k, 0, :], zz[:, k, 1, :], zz[:, k, 0, :], zz[:, k, 1, :],
                     [f"zz{k}"], [f"zz{k + 1}"])
            lb_re, lb_im = zz[:, 4, 0, :], zz[:, 4, 1, :]
            dve(lambda e: e.memset(PW[:, 0, 0, :], 1.0), [], ["pw0"])
            dve(lambda e: e.memset(PW[:, 0, 1, :], 0.0), ["pw0"], ["pw0"])
            dve(lambda e: e.tensor_copy(out=PW[:, 1, :, :], in_=zz[:, 4, :, :]), ["zz4"], ["pw1"])
            for k in range(2, 9):
                cmul(PW[:, k, 0, :], PW[:, k, 1, :], PW[:, k - 1, 0, :], PW[:, k - 1, 1, :], lb_re, lb_im,
                     [f"pw{k - 1}", "zz4"], [f"pw{k}"])
            pwr = [f"pw{k}" for k in range(9)]
            dve(lambda e: e.tensor_copy(out=APW[:, 0, :, :], in_=PW[:, 8, :, :]), ["pw8"], ["apw0"])
            for m in range(1, 10):
                cmul(APW[:, m, 0, :], APW[:, m, 1, :], APW[:, m - 1, 0, :], APW[:, m - 1, 1, :],
                     APW[:, m - 1, 0, :], APW[:, m - 1, 1, :], [f"apw{m - 1}"], [f"apw{m}"])
            apr = [f"apw{m}" for m in range(10)]
            dve(lambda e: e.tensor_scalar(out=NAI[:, :, :], in0=APW[:, :, 1, :], scalar1=-1.0, scalar2=None, op0=ALU.mult), apr, ["nai"])
            tt(t3[:, :], lr, lr, ALU.mult, ["prm"], ["t3"])
            tt(t1[:, :], li, li, ALU.mult, ["prm"], ["t1"])
            tt(t3[:, :], t3[:, :], t1[:, :], ALU.add, ["t3", "t1"], ["t3"])
            dve(lambda e: e.reciprocal(out=t3[:, :], in_=t3[:, :]), ["t3"], ["t3"])
            dve(lambda e: e.tensor_scalar(out=ar[:, :], in0=lb_re, scalar1=-1.0, scalar2=None, op0=ALU.add), ["zz4", "ar"], ["ar"])
            tt(t1[:, :], ar[:, :], lr, ALU.mult, ["ar", "prm"], ["t1"])
            tt(t2[:, :], lb_im, li, ALU.mult, ["zz4", "prm"], ["t2"])
            tt(t1[:, :], t1[:, :], t2[:, :], ALU.add, ["t1", "t2"], ["t1"])
            tt(Fr[:, 0, :], t1[:, :], t3[:, :], ALU.mult, ["t1", "t3"], ["fr"])
            tt(t1[:, :], lb_im, lr, ALU.mult, ["zz4", "prm"], ["t1"])
            tt(t2[:, :], ar[:, :], li, ALU.mult, ["ar", "prm"], ["t2"])
            tt(t1[:, :], t1[:, :], t2[:, :], ALU.subtract, ["t1", "t2"], ["t1"])
            tt(Fr[:, 1, :], t1[:, :], t3[:, :], ALU.mult, ["t1", "t3"], ["fi"])

            def bc(ap2):
                return ap2.unsqueeze(2).to_broadcast([128, NPAIR, 16])

            def cmul_b(o_re, o_im, s_re, s_im, x_re, x_im, r, w):
                tt(u1[:, :, :], x_re, bc(s_re), ALU.mult, r, ["u1"])
                tt(u2[:, :, :], x_im, bc(s_im), ALU.mult, r, ["u2"])
                tt(o_re, u1[:, :, :], u2[:, :, :], ALU.subtract, ["u1", "u2"], w)
                tt(u1[:, :, :], x_im, bc(s_re), ALU.mult, r + w, ["u1"])
                tt(u2[:, :, :], x_re, bc(s_im), ALU.mult, r + w, ["u2"])
                tt(o_im, u1[:, :, :], u2[:, :, :], ALU.add, ["u1", "u2"], w)

            cmul_b(Bb[:, 0, :, :], Bb[:, 1, :, :], Fr[:, 0, :], Fr[:, 1, :], Bm[:, :, 0, :], Bm[:, :, 1, :],
                   ["fr", "fi", "Bm"], ["Bb"])
            for g2 in range(2):
                rows = slice(64 * g2, 64 * g2 + 64)
                czv = Cz[rows, 0, :].rearrange("p (q g h) -> p q g h", g=2, h=16)
                dve(lambda e, czv=czv, rows=rows, g2=g2: e.tensor_copy(out=czv[:, 0:NPAIR, g2, :], in_=Cm[rows, :, 0, :]),
                    ["Cm", "Cz"], ["Cz"])
                czv2 = Cz[rows, 1, :].rearrange("p (q g h) -> p q g h", g=2, h=16)
                dve(lambda e, czv2=czv2, rows=rows, g2=g2: e.tensor_scalar(out=czv2[:, 0:NPAIR, g2, :], in0=Cm[rows, :, 1, :],
                                                                            scalar1=-1.0, scalar2=None, op0=ALU.mult),
                    ["Cm", "Cz"], ["Cz"])
            for i in range(TC):
                cmul_b(W[0][:, 0, :, :], W[0][:, 1, :, :], PW[:, i + 1, 0, :], PW[:, i + 1, 1, :], Cm[:, :, 0, :], Cm[:, :, 1, :],
                       pwr + ["Cm"], ["W0"])
                for g2 in range(2):
                    rows = slice(64 * g2, 64 * g2 + 64)
                    dve(lambda e, rows=rows, g2=g2, i=i: e.tensor_copy(out=CE[rows, :, i, 0, 16 * g2:16 * g2 + 16], in_=W[0][rows, 0, :, :]),
                        ["W0", "CE"], ["CE"])
                    dve(lambda e, rows=rows, g2=g2, i=i: e.tensor_scalar(out=CE[rows, :, i, 1, 16 * g2:16 * g2 + 16], in0=W[0][rows, 1, :, :],
                                                                         scalar1=-1.0, scalar2=None, op0=ALU.mult),
                        ["W0", "CE"], ["CE"])
            for tau in range(TC):
                zb = tau % 2
                cmul_b(W[1][:, 0, :, :], W[1][:, 1, :, :], PW[:, tau, 0, :], PW[:, tau, 1, :], Bb[:, 0, :, :], Bb[:, 1, :, :],
                       pwr + ["Bb"], ["W1"])
                for g2 in range(2):
                    rows = slice(64 * g2, 64 * g2 + 64)
                    for ri in range(2):
                        zv = Zall[zb][rows, ri, :].rearrange("p (q g h) -> p q g h", g=2, h=16)
                        dve(lambda e, zv=zv, rows=rows, g2=g2, ri=ri: e.tensor_copy(out=zv[:, 0:NPAIR, g2, :], in_=W[1][rows, ri, :, :]),
                            ["W1", f"Z{zb}"], [f"Z{zb}"])
                for k in range(NT96):
                    npq = 3 if k < NT96 - 1 else 2
                    R = 32 * npq
                    tb = k % 4
                    kb_ = (tau * NT96 + k) % 3
                    for ri in range(2):
                        P.op("pe", lambda e, k=k, R=R, ri=ri, tb=tb, zb=zb: e.transpose(
                            out=psT[0:R, tb, ri, 0:128], in_=Zall[zb][:, ri, 96 * k:96 * k + R], identity=identb[:, :]),
                            reads=[f"Z{zb}", "identb"], writes=[f"psT{tb}"])
                    P.op("act", lambda e, k=k, R=R, tb=tb, tau=tau: e.copy(out=BE[0:R, k, TC - 1 - tau, :, :], in_=psT[0:R, tb, :, 0:128]),
                         reads=[f"psT{tb}"], writes=["BE"])
                    for q3 in range(npq):
                        c0 = 96 * k + 32 * q3
                        for ri in range(2):
                            P.op("pe", lambda e, q3=q3, c0=c0, ri=ri, kb_=kb_, zb=zb: e.matmul(
                                psK[32 * q3:32 * q3 + 32, kb_, 32 * q3:32 * q3 + 32], lhsT=Zall[zb][:, ri, c0:c0 + 32],
                                rhs=Cz[:, ri, c0:c0 + 32], start=(ri == 0), stop=(ri == 1)),
                                reads=[f"Z{zb}", "Cz"], writes=[f"psK{kb_}"])
                    P.op("dve", lambda e, R=R, k=k, kb_=kb_, tau=tau: e.tensor_copy(
                        out=Kb[0:R, k, tau, 0:R], in_=psK[0:R, kb_, 0:R]),
                        reads=[f"psK{kb_}", "Kb"], writes=["Kb"])
                for bb in range(tau * NB // TC, (tau + 1) * NB // TC):
                    n_block(bb)
            P.barrier()

        with ExitStack() as p2:
            ut = [p2.enter_context(nc.sbuf_tensor(f"s2_ut{i}", [96, L], BF16)) for i in range(2)]
            ut8 = [p2.enter_context(nc.sbuf_tensor(f"s2_ut8{i}", [96, TC, NC], BF16)) for i in range(2)]
            go = [p2.enter_context(nc.sbuf_tensor(f"s2_go{i}", [96, L], BF16)) for i in range(1)]
            NES = 3
            POOL_EVERY = 10 ** 9
            EA = [p2.enter_context(nc.sbuf_tensor(f"s2_E{i}", [128, 2, PADC + NC], F32)) for i in range(2 * NES)]
            ptmp = p2.enter_context(nc.sbuf_tensor("s2_ptmp", [128, NC], F32))
            Sbf = [p2.enter_context(nc.sbuf_tensor(f"s2_S{i}", [128, 3, 2, 1 + NC], BF16)) for i in range(2)]
            tmp = [p2.enter_context(nc.sbuf_tensor(f"s2_tmp{i}", [96, NC], F32)) for i in range(3)]
            psE = p2.enter_context(nc.psum_tensor("s2_psE", [128, 2, 2, 512], F32))
            psY = p2.enter_context(nc.psum_tensor("s2_psY", [128, 4, 512], F32))
            for i in range(2 * NES):
                P.op("pool", lambda e, i=i: e.memset(EA[i][:, :, :], 0.0), writes=[f"E{i}r", f"E{i}i"])
            for i in range(2):
                P.op("pool", lambda e, i=i: e.memset(Sbf[i][:, :, :, :], 0.0), writes=[f"S{i}"])
            tiles = [(k, 3 if k < NT96 - 1 else 2) for k in range(NT96)]
            pairs = [(k, q3) for (k, npq) in tiles for q3 in range(npq)]
            loaded = set()

            def load_tile(k):
                if k in loaded or k >= NT96:
                    return
                loaded.add(k)
                R = 32 * tiles[k][1]
                us = k % 2
                P.dma("sp", ut[us][0:R, :], UTd[96 * k:96 * k + R, :], writes=[f"ut{us}"])
                P.op("pool", lambda e: e.tensor_copy(out=ut8[us][0:R, :, :], in_=ut[us][0:R, :].rearrange("p (c j) -> p j c", j=TC)),
                     reads=[f"ut{us}"], writes=[f"ut8{us}"])

            def emit_B(pi_):
                k, q3 = pairs[pi_]
                load_tile(k)
                us = k % 2
                es_ = pi_ % NES
                pb_ = pi_ % 2
                rows = slice(32 * q3, 32 * q3 + 32)
                for ri in range(2):
                    for j in range(TC):
                        P.op("pe", lambda e, j=j, ri=ri: e.matmul(
                            psE[:, pb_, ri, 0:NC], lhsT=BE[rows, k, j, ri, :], rhs=ut8[us][rows, j, :],
                            start=(j == 0), stop=(j == TC - 1)), reads=["BE", f"ut8{us}"], writes=[f"psE{pb_}_{ri}"])
                P.op("act", lambda e: e.copy(out=EA[2 * es_][:, :, PADC:PADC + NC], in_=psE[:, pb_, :, 0:NC]),
                     reads=[f"psE{pb_}_0", f"psE{pb_}_1"], writes=[f"E{2 * es_}r", f"E{2 * es_}i"])

            def emit_LOG(pi_):
                k, q3 = pairs[pi_]
                q = 3 * k + q3
                us = k % 2
                es_ = pi_ % NES
                src = 0
                for m in range(NLEV):
                    sh = 1 << m
                    X, Y = EA[2 * es_ + src], EA[2 * es_ + 1 - src]
                    xr, yr = f"E{2 * es_ + src}", f"E{2 * es_ + 1 - src}"
                    a_re, a_im, na_im = APW[:, m, 0, q:q + 1], APW[:, m, 1, q:q + 1], NAI[:, m, q:q + 1]
                    cur = slice(PADC, PADC + NC)
                    shf = slice(PADC - sh, PADC - sh + NC)
                    if pi_ % POOL_EVERY == POOL_EVERY - 1:
                        def stt(out, in0, sc, in1, r, w):
                            P.op("pool", lambda e: e.tensor_scalar(out=ptmp[:, :], in0=in0, scalar1=sc, scalar2=None, op0=ALU.mult),
                                 reads=r, writes=["ptmp"])
                            P.op("pool", lambda e: e.tensor_tensor(out=out, in0=ptmp[:, :], in1=in1, op=ALU.add),
                                 reads=r + ["ptmp"], writes=w)
                    else:
                        stt = lambda out, in0, sc, in1, r, w: P.op("dve", lambda e: e.scalar_tensor_tensor(
                            out=out, in0=in0, scalar=sc, in1=in1, op0=ALU.mult, op1=ALU.add), reads=r, writes=w)
                    stt(Y[:, 0, cur], X[:, 0, shf], a_re, X[:, 0, cur], [xr + "r", "apw"], [yr + "r"])
                    stt(Y[:, 1, cur], X[:, 1, shf], a_re, X[:, 1, cur], [xr + "i", "apw"], [yr + "i"])
                    cur2 = slice(PADC + sh, PADC + NC)
                    shf2 = slice(PADC, PADC + NC - sh)
                    stt(Y[:, 0, cur2], X[:, 1, shf2], na_im, Y[:, 0, cur2], [xr + "i", yr + "r", "apw"], [yr + "r"])
                    stt(Y[:, 1, cur2], X[:, 0, shf2], a_im, Y[:, 1, cur2], [xr + "r", yr + "i", "apw"], [yr + "i"])
                    src = 1 - src
                fin = 2 * es_ + src
                P.op("act", lambda e: e.copy(out=Sbf[us][:, q3, :, 1:1 + NC], in_=EA[fin][:, :, PADC:PADC + NC]),
                     reads=[f"E{fin}r", f"E{fin}i"], writes=[f"S{us}"])

            ti = [0]

            def emit_AD(k):
                npq = tiles[k][1]
                R = 32 * npq
                us = k % 2
                for i in range(TC):
                    bk = i % 4
                    for j in range(i + 1):
                        P.op("pe", lambda e, i=i, j=j, bk=bk: e.matmul(
                            psY[0:R, bk, 0:NC], lhsT=Kb[0:R, k, i - j, 0:R], rhs=ut8[us][0:R, j, :],
                            start=(j == 0), stop=False), reads=["Kb", f"ut8{us}"], writes=[f"psY{bk}"])
                    for q3 in range(npq):
                        q = 3 * k + q3
                        for ri in range(2):
                            last = (q3 == npq - 1 and ri == 1)
                            P.op("pe", lambda e, q3=q3, q=q, i=i, ri=ri, bk=bk, last=last: e.matmul(
                                psY[32 * q3:32 * q3 + 32, bk, 0:NC], lhsT=CE[:, q, i, ri, :], rhs=Sbf[us][:, q3, ri, 0:NC],
                                start=False, stop=(ri == 1)), reads=["CE", f"S{us}"], writes=[f"psY{bk}"])
                    tb = ti[0] % 3
                    ti[0] += 1
                    P.op("dve", lambda e, i=i, bk=bk, tb=tb: e.scalar_tensor_tensor(
                        out=tmp[tb][0:R, :], in0=ut8[us][0:R, i, :], scalar=dv[0:R, k:k + 1], in1=psY[0:R, bk, 0:NC],
                        op0=ALU.mult, op1=ALU.add), reads=[f"psY{bk}", f"ut8{us}", "dv"], writes=[f"tmp{tb}"])
                    P.op("act", lambda e, i=i, tb=tb: e.activation(
                        out=go[0][0:R, i::TC], in_=tmp[tb][0:R, :], func=AF.Gelu_apprx_tanh),
                        reads=[f"tmp{tb}"], writes=[f"go_{i}"])
                P.dma("pool", Gd[96 * k:96 * k + R, :], go[0][0:R, :], reads=[f"go_{i}" for i in range(TC)], writes=[f"Gd{k}"])

            emit_B(0)
            if len(pairs) > 1:
                emit_B(1)
            for pi_ in range(len(pairs)):
                if pi_ + 2 < len(pairs):
                    emit_B(pi_ + 2)
                emit_LOG(pi_)
                if after_norm is not None:
                    after_norm(3)
                k, q3 = pairs[pi_]
                if q3 == tiles[k][1] - 1:
                    emit_AD(k)
            P.barrier()

        ph.close()
        if pf is not None:
            pf.alloc()
        with ExitStack() as p3:
            wg = p3.enter_context(nc.sbuf_tensor("s3_wg", [128, KT, 2 * D], BF16))
            xb = [p3.enter_context(nc.sbuf_tensor(f"s3_xb{i}", [128, KT, TB], F32)) for i in range(2)]
            gb = [p3.enter_context(nc.sbuf_tensor(f"s3_gb{i}", [128, KT, TB], BF16)) for i in range(2)]
            sg = [p3.enter_context(nc.sbuf_tensor(f"s3_sg{i}", [128, TB], F32)) for i in range(2)]
            psA = p3.enter_context(nc.psum_tensor("s3_psA", [128, 2, 512], F32))
            psG = p3.enter_context(nc.psum_tensor("s3_psG", [128, 2, 512], F32))
            P.dma("sp", wg[:, :, :], w_glu_d[layer].rearrange("(k p) c -> p k c", p=128), reads=WC[f"glu{layer}"], writes=["wg"])
            wgr = ["wg"]
            for b in range(NB):
                s = b % 2
                t0 = b * TB
                P.dma("sp", xb[s][:, :, :], HT[:, :, t0:t0 + TB].rearrange("k p t -> p k t"), writes=[f"gx{s}"])
                P.dma("sp", gb[s][:, :, :], Gd[:, t0:t0 + TB].rearrange("(k p) t -> p k t", p=128), writes=[f"gb{s}"])
                if pf is not None and b % 2 == 1:
                    pf.load_some(1)
                for m in range(KT):
                    bk = m % 2
                    for kt in range(KT):
                        P.op("pe", lambda e, m=m, kt=kt, bk=bk, s=s: e.matmul(
                            psA[:, bk, :], lhsT=wg[:, kt, m * 128:(m + 1) * 128], rhs=gb[s][:, kt, :],
                            start=(kt == 0), stop=(kt == KT - 1)), reads=wgr + [f"gb{s}"], writes=[f"psA{bk}"])
                    for kt in range(KT):
                        P.op("pe", lambda e, m=m, kt=kt, bk=bk, s=s: e.matmul(
                            psG[:, bk, :], lhsT=wg[:, kt, D + m * 128:D + (m + 1) * 128], rhs=gb[s][:, kt, :],
                            start=(kt == 0), stop=(kt == KT - 1)), reads=wgr + [f"gb{s}"], writes=[f"psG{bk}"])
                    P.op("act", lambda e, bk=bk: e.activation(out=sg[bk][:, :], in_=psG[:, bk, :], func=AF.Sigmoid),
                         reads=[f"psG{bk}"], writes=[f"sg{bk}"])
                    P.op("dve", lambda e, bk=bk: e.tensor_tensor(out=sg[bk][:, :], in0=sg[bk][:, :], in1=psA[:, bk, :], op=ALU.mult),
                         reads=[f"psA{bk}", f"sg{bk}"], writes=[f"sg{bk}"])
                    P.op("dve", lambda e, m=m, bk=bk, s=s: e.tensor_tensor(out=xb[s][:, m, :], in0=xb[s][:, m, :], in1=sg[bk][:, :], op=ALU.add),
                         reads=[f"sg{bk}", f"gx{s}"], writes=[f"gx{s}o{m}"])
                P.dma("pool", HT[:, :, t0:t0 + TB].rearrange("k p t -> p k t"), xb[s][:, :, :],
                      reads=[f"gx{s}o{m}" for m in range(KT)] + [f"gx{s}"], writes=[f"HTs{b}"])
            P.barrier()


def build(L, phases=("in", "final")):
    nc = bass.Bass("TRN2", target_bir_lowering=False)
    NB = L // TB
    x_d = nc.dram_tensor("x", [L, D], F32, kind="ExternalInput").ap()
    ident_d = nc.dram_tensor("ident", [128, 128], F32, kind="ExternalInput").ap()
    gvec_d = nc.dram_tensor("gvec", [128, 10, KT], F32, kind="ExternalInput").ap()
    out_d = nc.dram_tensor("out", [L, D], F32, kind="ExternalOutput").ap()
    w_in_f = nc.dram_tensor("w_ffn_in", [4, D, 2 * FF], F32, kind="ExternalInput").ap()
    w_out_f = nc.dram_tensor("w_ffn_out", [4, FF, D], F32, kind="ExternalInput").ap()
    w_in_d = nc.dram_tensor("w_in_b", [4, D, 2 * FF], BF16, kind="Internal").ap()
    w_out_d = nc.dram_tensor("w_out_b", [4, FF, D], BF16, kind="Internal").ap()
    cw_d = nc.dram_tensor("ffn_cw", [4, 128, 2 * NFT, 4], F32, kind="ExternalInput").ap()
    w_kvf_f = nc.dram_tensor("w_kvf", [D, 2064], F32, kind="ExternalInput").ap()
    w_kvf_d = nc.dram_tensor("w_kvf_b", [D, 2064], BF16, kind="Internal").ap()
    bf_d = nc.dram_tensor("b_f", [NH, 1], F32, kind="ExternalInput").ap()
    w_q_f = nc.dram_tensor("w_q", [2, D, D], F32, kind="ExternalInput").ap()
    w_o_f = nc.dram_tensor("w_o", [2, D, D], F32, kind="ExternalInput").ap()
    w_q_d = nc.dram_tensor("w_q_b", [2, D, D], BF16, kind="Internal").ap()
    w_o_d = nc.dram_tensor("w_o_b", [2, D, D], BF16, kind="Internal").ap()
    tri_d = nc.dram_tensor("tri", [128, 128], F32, kind="ExternalInput").ap()
    s5p_d = nc.dram_tensor("s5p", [2, 128, NPAIR, 3], F32, kind="ExternalInput").ap()
    s5b_d = nc.dram_tensor("s5b", [2, 128, NPAIR, 2, 16], F32, kind="ExternalInput").ap()
    s5c_d = nc.dram_tensor("s5c", [2, 128, NPAIR, 2, 16], F32, kind="ExternalInput").ap()
    dv_d = nc.dram_tensor("s5dv", [2, 96, NT96], F32, kind="ExternalInput").ap()
    w_glu_f = nc.dram_tensor("w_glu", [2, D, 2 * D], F32, kind="ExternalInput").ap()
    w_glu_d = nc.dram_tensor("w_glu_b", [2, D, 2 * D], BF16, kind="Internal").ap()
    UTd = nc.dram_tensor("UTd", [D, L], BF16, kind="Internal").ap()
    Gd = nc.dram_tensor("Gd", [D, L], BF16, kind="Internal").ap()
    KTd = nc.dram_tensor("KTd", [NH, KA, L], BF16, kind="Internal").ap()
    QAd = nc.dram_tensor("QAd", [NH, 6, L], BF16, kind="Internal").ap()
    VTd = nc.dram_tensor("VTd", [L // 128, 128, NH, HD + 1], BF16, kind="Internal").ap()
    OTd = nc.dram_tensor("OTd", [NH, HD, L], BF16, kind="Internal").ap()
    QTd = nc.dram_tensor("QTd", [NH, HD, L], BF16, kind="Internal").ap()
    HT = nc.dram_tensor("HT", [KT, 128, L], F32, kind="Internal").ap()

    nc_raw = nc
    with ExitStack() as es:
        P = Prog(nc, es)
        nc = NCW(nc_raw)
        ident = es.enter_context(nc.sbuf_tensor("ident_sb", [128, 128], F32))
        ones = es.enter_context(nc.sbuf_tensor("ones_sb", [128, 128], BF16))
        gs = es.enter_context(nc.sbuf_tensor("gs_sb", [128, 10, KT], F32))
        P.dma("sp", ident[:, :], ident_d[:, :], writes=["ident"])
        P.dma("sp", gs[:, :, :], gvec_d[:, :, :], writes=["gs"])
        P.op("dve", lambda e: e.memset(ones[:, :], 1.0), writes=["ones"])
        epsb = es.enter_context(nc.sbuf_tensor("epsb_sb", [128, 1], F32))
        P.op("dve", lambda e: e.memset(epsb[:, :], EPS), writes=["epsb"])

        if "in" in phases:
            with ExitStack() as ph:
                xt = [ph.enter_context(nc.sbuf_tensor(f"in_xt{i}", [128, 4, D], F32)) for i in range(2)]
                ht = [ph.enter_context(nc.sbuf_tensor(f"in_ht{i}", [128, KT, TB], F32)) for i in range(2)]
                ps = ph.enter_context(nc.psum_tensor("in_ps", [128, 4, TB], F32))
                for b in range(NB):
                    s = b % 2
                    P.dma("sp", xt[s][:, :, :], x_d[b * TB:(b + 1) * TB, :].rearrange("(j p) d -> p j d", p=128),
                          writes=[f"xt{s}"])
                    for kt in range(KT):
                        bk = kt % 4
                        for j in range(4):
                            P.op("pe", lambda e, j=j, kt=kt, bk=bk, s=s: e.transpose(
                                out=ps[:, bk, j * 128:(j + 1) * 128], in_=xt[s][:, j, kt * 128:(kt + 1) * 128],
                                identity=ident[:, :]), reads=[f"xt{s}", "ident"], writes=[f"ps{bk}"])
                        ev = "act" if kt % 2 == 0 else "dve"
                        if ev == "act":
                            P.op("act", lambda e, kt=kt, bk=bk, s=s: e.copy(out=ht[s][:, kt, :], in_=ps[:, bk, :]),
                                 reads=[f"ps{bk}"], writes=[f"ht{s}_{kt}"])
                        else:
                            P.op("dve", lambda e, kt=kt, bk=bk, s=s: e.tensor_copy(out=ht[s][:, kt, :], in_=ps[:, bk, :]),
                                 reads=[f"ps{bk}"], writes=[f"ht{s}_{kt}"])
                    P.dma("pool", HT[:, :, b * TB:(b + 1) * TB].rearrange("k p t -> p k t"), ht[s][:, :, :],
                          reads=[f"ht{s}_{kt}" for kt in range(KT)], writes=[f"HT{b}"])
                P.barrier()

        cast_items = []

        def wcast(dst, src, name, nchunk, split=None):
            rows = dst.shape[0]
            step = rows // nchunk
            names = []
            for c in range(nchunk):
                d_, s_ = dst[c * step:(c + 1) * step], src[c * step:(c + 1) * step]
                if split:
                    d_ = d_.rearrange("r (c e) -> r c e", e=split)
                    s_ = s_.rearrange("r (c e) -> r c e", e=split)
                nm = f"WC:{name}:{c}"
                names.append(nm)
                cast_items.append(lambda d_=d_, s_=s_, nm=nm: P.dma("pool", d_, s_, writes=[nm], ring="wc"))
            WC[name] = names
        for l in range(4):
            if l < 2:
                wcast(w_glu_d[l], w_glu_f[l], f"glu{l}", 4, 1024)
            else:
                if l == 2:
                    wcast(w_kvf_d, w_kvf_f, "kvf", 4, 1032)
                wcast(w_q_d[l - 2], w_q_f[l - 2], f"q{l - 2}", 2)
                wcast(w_o_d[l - 2], w_o_f[l - 2], f"o{l - 2}", 2)
            wcast(w_in_d[l], w_in_f[l], f"in{l}", 8, 1408)
            wcast(w_out_d[l], w_out_f[l], f"out{l}", 4)

        def issue_casts(n=None):
            k = len(cast_items) if n is None else min(n, len(cast_items))
            for _ in range(k):
                cast_items.pop(0)()

        for layer in range(4):
            lstack = es.enter_context(ExitStack())
            pf = WinPrefetch(P, nc, w_in_d, layer, lstack) if f"ffn{layer}" in phases else None
            if f"s5{layer}" in phases:
                s5_phase(P, nc, L, layer, HT, UTd, Gd, s5p_d, s5b_d, s5c_d, dv_d, w_glu_d, ones, epsb, gs, ident, issue_casts, pf)
            issue_casts()
            if layer == 2 and "kv" in phases:
                kv_phase(P, nc, L, HT, w_kvf_d, bf_d, KTd, QAd, VTd, ones, epsb, gs)
            if f"fox{layer}" in phases:
                fox_phase(P, nc, L, layer, HT, w_q_d, w_o_d, KTd, QAd, VTd, OTd, QTd, tri_d, ones, epsb, gs, ident, pf)
            if f"ffn{layer}" in phases:
                ffn_phase(P, nc, L, layer, HT, w_in_d, w_out_d, cw_d, ones, epsb, gs, pf)
            lstack.close()

        if "final" in phases:
            with ExitStack() as ph:
                xb = [ph.enter_context(nc.sbuf_tensor(f"fx{i}", [128, KT, TB], F32)) for i in range(2)]
                sq = ph.enter_context(nc.sbuf_tensor("fsq", [128, KT, TB], BF16))
                rr = ph.enter_context(nc.sbuf_tensor("frr", [128, TB], F32))
                xn = ph.enter_context(nc.sbuf_tensor("fxn", [128, KT, TB], F32))
                ot = [ph.enter_context(nc.sbuf_tensor(f"fot{i}", [128, 4, D], F32)) for i in range(2)]
                pss = ph.enter_context(nc.psum_tensor("f_pss", [128, TB], F32))
                ps = ph.enter_context(nc.psum_tensor("f_ps", [128, 4, TB], F32))
                for b in range(NB):
                    s = b % 2
                    P.dma("sp", xb[s][:, :, :], HT[:, :, b * TB:(b + 1) * TB].rearrange("k p t -> p k t"),
                          reads=[f"HT{b}"], writes=[f"fx{s}x"])
                    rms_block(P, xb[s], f"fx{s}x", sq, rr, pss, ones, epsb, gs[:, 9, :], xn, "fxn", "fin_", TB)
                    for j in range(4):
                        for half in range(2):
                            bk = (j * 2 + half) % 4
                            for k4 in range(4):
                                kt = half * 4 + k4
                                P.op("pe", lambda e, j=j, kt=kt, bk=bk, k4=k4: e.transpose(
                                    out=ps[:, bk, k4 * 128:(k4 + 1) * 128], in_=xn[:, kt, j * 128:(j + 1) * 128],
                                    identity=ident[:, :]), reads=["fxn", "ident"], writes=[f"fps{bk}"])
                            if half == 0:
                                P.op("act", lambda e, j=j, bk=bk, s=s: e.copy(out=ot[s][:, j, 0:512], in_=ps[:, bk, :]),
                                     reads=[f"fps{bk}"], writes=[f"fot{s}_{j}_0"])
                            else:
                                P.op("dve", lambda e, j=j, bk=bk, s=s: e.tensor_copy(out=ot[s][:, j, 512:1024], in_=ps[:, bk, :]),
                                     reads=[f"fps{bk}"], writes=[f"fot{s}_{j}_1"])
                    P.dma("pool", out_d[b * TB:(b + 1) * TB, :].rearrange("(j p) d -> p j d", p=128), ot[s][:, :, :],
                          reads=[f"fot{s}_{j}_{h}" for j in range(4) for h in range(2)], writes=[f"out{b}"])
                P.barrier()
        print("instructions:", P.nins)
    return nc_raw


def make_gvec(g_mix, g_ffn, g_kv, g_final):
    allg = np.concatenate([g_mix, g_ffn, g_kv[None], g_final[None]], axis=0).astype(np.float32)
    return np.ascontiguousarray(allg.reshape(10, KT, 128).transpose(2, 0, 1))


def host_inputs(inp):
    f32 = lambda a: np.ascontiguousarray(np.asarray(a, dtype=np.float32))
    cwt = np.concatenate([np.asarray(inp["ffn_conv_w"], np.float32),
                          np.asarray(inp["ffn_conv_b"], np.float32)[:, None, :]], axis=1)
    cw = cwt.reshape(4, 4, 2 * NFT, 128).transpose(0, 3, 2, 1)
    def pairlay(a):
        a = np.asarray(a, np.float32)
        rest = a.shape[3:]
        a = a.reshape((2, NPAIR, 2, 64) + rest)
        a = np.moveaxis(a, 1, 3)
        return a.reshape((2, 128, NPAIR) + rest)
    ldt = np.broadcast_to(np.asarray(inp["log_dt"], np.float32)[:, :, None], (2, 64, 64))
    s5p = np.stack([pairlay(inp["lam_re"]), pairlay(inp["lam_im"]), pairlay(ldt)], axis=-1)
    s5b = np.stack([pairlay(inp["ssm_b_re"]), pairlay(inp["ssm_b_im"])], axis=3)
    ct = lambda a: np.asarray(a, np.float32).transpose(0, 1, 3, 2)
    s5c = np.stack([pairlay(ct(inp["ssm_c_re"])), pairlay(ct(inp["ssm_c_im"]))], axis=3)
    dpad = np.zeros((2, NT96 * 96), np.float32)
    dpad[:, :D] = np.asarray(inp["ssm_d"], np.float32)
    dv96 = dpad.reshape(2, NT96, 96).transpose(0, 2, 1)
    return {
        "ident": np.eye(128, dtype=np.float32),
        "gvec": make_gvec(np.asarray(inp["g_mix"]), np.asarray(inp["g_ffn"]), np.asarray(inp["g_kv"]),
                          np.asarray(inp["g_final"])),
        "w_ffn_in": f32(inp["w_ffn_in"]), "w_ffn_out": f32(inp["w_ffn_out"]), "ffn_cw": f32(cw),
        "w_kvf": f32(inp["w_kvf"]), "b_f": f32(np.asarray(inp["b_f"]).reshape(NH, 1)),
        "w_q": f32(inp["w_q"]), "w_o": f32(inp["w_o"]),
        "tri": np.triu(np.ones((128, 128), np.float32)),
        "s5p": f32(s5p), "s5b": f32(s5b), "s5c": f32(s5c), "s5dv": f32(dv96), "w_glu": f32(inp["w_glu"]),
    }


PHASES = ("in", "s50", "ffn0", "s51", "ffn1", "kv", "fox2", "ffn2", "fox3", "ffn3", "final")


def kernel(**inp):
    x = np.asarray(inp["x"], np.float32)
    B, L, _ = x.shape
    nc = build(L, phases=inp.get("_phases", PHASES))
    shared = host_inputs(inp)
    in_maps = [dict(shared, x=np.ascontiguousarray(x[c])) for c in range(B)]
    res = run_bass_kernel_spmd(nc, in_maps, core_ids=list(range(B)))
    return np.stack([r["out"] for r in res.results], axis=0)
```

```python
import numpy as np
from contextlib import ExitStack
import concourse.bass as bass
import concourse.mybir as mybir
from concourse.bass_utils import run_bass_kernel_spmd

F32 = mybir.dt.float32
BF16 = mybir.dt.bfloat16
ALU = mybir.AluOpType
AF = mybir.ActivationFunctionType

D = 1024
KT = 8
NCORES = 8
EPS = 1e-6
TB = 512


STRICT = True


WC = {}


class NCW:
    def __init__(self, nc):
        self._nc = nc
        self._n = 0

    def __getattr__(self, k):
        return getattr(self._nc, k)

    def _u(self, name):
        self._n += 1
        return f"{name}_{self._n}"

    def sbuf_tensor(self, name, shape, dt):
        return self._nc.sbuf_tensor(self._u(name), shape, dt)

    def psum_tensor(self, name, shape, dt):
        return self._nc.psum_tensor(self._u(name), shape, dt)


class WinPrefetch:
    def __init__(self, P, nc, w_in_d, layer, stack=None):
        self.P, self.nc, self.w_in_d, self.layer, self.stack = P, nc, w_in_d, layer, stack
        self.win = None
        self.todo = [0, 2, 1, 3]

    def alloc(self, stack=None):
        stack = self.stack if self.stack is not None else stack
        self.win = stack.enter_context(self.nc.sbuf_tensor("f_win", [128, KT, 2 * 2816], BF16))

    def load_some(self, n):
        for _ in range(min(n, len(self.todo))):
            c = self.todo.pop(0)
            self.P.dma("sp", self.win[:, :, c * 1408:(c + 1) * 1408],
                       self.w_in_d[self.layer, :, c * 1408:(c + 1) * 1408].rearrange("(k p) f -> p k f", p=128),
                       reads=WC[f"in{self.layer}"], writes=[f"win_{c}"])


class Prog:
    CE = ("act", "pe", "dve", "pool")

    def __init__(self, nc, es, ring=12):
        self.nc = nc
        self.eng = {"sp": nc.sync, "act": nc.scalar, "pe": nc.tensor, "dve": nc.vector, "pool": nc.gpsimd}
        self.sem = {}
        for e in self.CE:
            self.sem["c_" + e] = es.enter_context(nc.semaphore("c_" + e))
        self.K = ring
        self.ringn = {}
        self.KR = {"sp": ring, "act": ring, "pool": ring, "wc": 10}
        for qn in ("sp", "act", "pool", "wc"):
            self.ringn[qn] = 0
            for k in range(self.KR[qn]):
                self.sem[f"d_{qn}_{k}"] = es.enter_context(nc.semaphore(f"d_{qn}_{k}"))
        self.cnt = {e: 0 for e in self.CE}
        self.res = {}
        self.seen = {e: {} for e in self.eng}
        self.last = {}
        self.nins = 0

    def _wait(self, eng, deps):
        best = {}
        for (s, v) in deps:
            if v > best.get(s, 0):
                best[s] = v
        for s, v in best.items():
            if self.seen[eng].get(s, 0) < v:
                self.eng[eng].wait_ge(self.sem[s], v)
                self.seen[eng][s] = v
                self.nins += 1

    def _deps(self, ident, reads, writes):
        deps = []
        for r in reads:
            st = self.res.get(r)
            if st and st[0] is not None:
                tok, who = st[0]
                if who == ident:
                    if ident != "pe":
                        deps.append(tok)
                else:
                    deps.append(tok)
        for w in writes:
            st = self.res.get(w)
            if st:
                if st[0] is not None:
                    tok, who = st[0]
                    if who != ident or (STRICT and ident != "pe"):
                        deps.append(tok)
                for (tok, who) in st[1]:
                    if who != ident or (STRICT and ident != "pe"):
                        deps.append(tok)
        return deps

    def _record(self, tok, ident, reads, writes):
        for r in reads:
            st = self.res.setdefault(r, [None, []])
            st[1].append((tok, ident))
            if len(st[1]) > 64:
                best = {}
                for (t, w) in st[1]:
                    if t[1] >= best.get((t[0], w), (0, 0))[0]:
                        best[(t[0], w)] = (t[1], (t, w))
                st[1] = [v[1] for v in best.values()]
        for w in writes:
            self.res[w] = [(tok, ident), []]
        self.last[tok[0]] = tok[1]

    def op(self, eng, fn, reads=(), writes=()):
        deps = self._deps(eng, reads, writes)
        self._wait(eng, deps)
        self.cnt[eng] += 1
        tok = ("c_" + eng, self.cnt[eng])
        ins = fn(self.eng[eng])
        ins.then_inc(self.sem[tok[0]], 1)
        self.nins += 1
        self._record(tok, eng, reads, writes)

    def dma(self, qn, out, in_, reads=(), writes=(), ring=None, **kw):
        ring = ring or qn
        i = self.ringn[ring]
        self.ringn[ring] += 1
        K = self.KR[ring]
        k, m = i % K, i // K
        s = f"d_{ring}_{k}"
        ident = ("dma", ring, i)
        deps = self._deps(ident, reads, writes)
        if m > 0:
            deps.append((s, 16 * m))
        self._wait(qn, deps)
        ins = self.eng[qn].dma_start(out=out, in_=in_, **kw)
        ins.then_inc(self.sem[s], 16)
        self.nins += 1
        self._record((s, 16 * (m + 1)), ident, reads, writes)

    def barrier(self):
        allt = [(s_, v) for s_, v in self.last.items() if not s_.startswith("d_wc_")]
        for e in self.eng:
            self._wait(e, allt)
        self.res = {k: v for k, v in self.res.items() if k.startswith("WC:")}


def rms_block(P, x, xres, sq, rr, ps, ones, epsb, gs, out, ores, pfx, n):
    P.op("act", lambda e: e.activation(out=sq[:, :, :n], in_=x[:, :, :n], func=AF.Square),
         reads=[xres], writes=[pfx + "sq"])
    for kt in range(KT):
        P.op("pe", lambda e, kt=kt: e.matmul(ps[:, :n], lhsT=ones[:, :], rhs=sq[:, kt, :n],
                                             start=(kt == 0), stop=(kt == KT - 1)),
             reads=[pfx + "sq", "ones"], writes=[pfx + "ps"])
    P.op("act", lambda e: e.activation(out=rr[:, :n], in_=ps[:, :n], func=AF.Sqrt, scale=1.0 / D, bias=epsb[:, 0:1]),
         reads=[pfx + "ps", "epsb"], writes=[pfx + "rr"])
    P.op("dve", lambda e: e.reciprocal(out=rr[:, :n], in_=rr[:, :n]), reads=[pfx + "rr"], writes=[pfx + "rr"])
    for kt in range(KT):
        P.op("dve", lambda e, kt=kt: e.scalar_tensor_tensor(out=out[:, kt, :n], in0=x[:, kt, :n],
                                                           scalar=gs[:, kt:kt + 1], in1=rr[:, :n],
                                                           op0=ALU.mult, op1=ALU.mult),
             reads=[xres, pfx + "rr", "gs"], writes=[ores])


FF = 2816
NFT = 22
TBF = 256


def ffn_phase(P, nc, L, layer, HT, w_in_d, w_out_d, cw_d, ones, epsb, gs, pf=None):
    NBF = L // TBF
    with ExitStack() as ph:
        if pf is None:
            pf = WinPrefetch(P, nc, w_in_d, layer)
        if pf.win is None:
            pf.alloc(ph)
        win = pf.win
        wout = ph.enter_context(nc.sbuf_tensor("f_wout", [128, NFT, D], BF16))
        cw = ph.enter_context(nc.sbuf_tensor("f_cw", [128, 2 * NFT, 4], F32))
        halo = ph.enter_context(nc.sbuf_tensor("f_halo", [128, 2 * NFT, 2], F32))
        xb = [ph.enter_context(nc.sbuf_tensor(f"f_xb{i}", [128, KT, TBF], F32)) for i in range(3)]
        sq = ph.enter_context(nc.sbuf_tensor("f_sq", [128, KT, TBF], BF16))
        rr = ph.enter_context(nc.sbuf_tensor("f_rr", [128, TBF], F32))
        xn = [ph.enter_context(nc.sbuf_tensor(f"f_xn{i}", [128, KT, TBF], BF16)) for i in range(2)]
        act = [ph.enter_context(nc.sbuf_tensor(f"f_act{i}", [128, NFT, TBF], BF16)) for i in range(2)]
        NU = 4
        ue = [ph.enter_context(nc.sbuf_tensor(f"f_ue{i}", [128, TBF + 2], F32)) for i in range(NU)]
        yy = [ph.enter_context(nc.sbuf_tensor(f"f_yy{i}", [128, TBF], F32)) for i in range(NU)]
        sg = [ph.enter_context(nc.sbuf_tensor(f"f_sg{i}", [128, TBF], F32)) for i in range(2)]
        psn = ph.enter_context(nc.psum_tensor("f_psn", [128, 512], F32))
        psu = ph.enter_context(nc.psum_tensor("f_psu", [128, 4, 512], F32))
        pso = ph.enter_context(nc.psum_tensor("f_pso", [128, 2, 512], F32))

        P.dma("sp", cw[:, :, :], cw_d[layer], writes=["cw"])
        P.op("pool", lambda e: e.memset(halo[:, :, :], 0.0), writes=[f"halo{i}" for i in range(2 * NFT)])
        pf.load_some(4)
        P.dma("sp", wout[:, :, :], w_out_d[layer].rearrange("(f p) d -> p f d", p=128), reads=WC[f"out{layer}"],
              writes=[f"wout{ft}" for ft in range(NFT)])

        def load(b):
            s3 = b % 3
            t0 = b * TBF
            P.dma("sp", xb[s3][:, :, :], HT[:, :, t0:t0 + TBF].rearrange("k p t -> p k t"), writes=[f"fb{s3}x"])

        def norm(b):
            rms_block(P, xb[b % 3], f"fb{b % 3}x", sq, rr, psn, ones, epsb, gs[:, 4 + layer, :], xn[b % 2], f"fxn{b % 2}", "fn_", TBF)

        def phase2_items(b):
            s3, s2 = b % 3, b % 2
            t0 = b * TBF
            items = []
            for m in range(KT):
                bk = m % 2
                for ft in range(NFT):
                    items.append(lambda m=m, ft=ft, bk=bk: P.op("pe", lambda e: e.matmul(
                        pso[:, bk, :TBF], lhsT=wout[:, ft, m * 128:(m + 1) * 128], rhs=act[s2][:, ft, :],
                        start=(ft == 0), stop=(ft == NFT - 1)),
                        reads=[f"wout{ft}", f"act{s2}_{ft}"], writes=[f"pso{bk}"]))
                items.append(lambda m=m, bk=bk: P.op("dve", lambda e: e.tensor_tensor(
                    out=xb[s3][:, m, :], in0=xb[s3][:, m, :], in1=pso[:, bk, :TBF], op=ALU.add),
                    reads=[f"pso{bk}", f"fb{s3}x"], writes=[f"fb{s3}o{m}"]))
            items.append(lambda: P.dma("pool", HT[:, :, t0:t0 + TBF].rearrange("k p t -> p k t"), xb[s3][:, :, :],
                                       reads=[f"fb{s3}o{m}" for m in range(KT)] + [f"fb{s3}x"], writes=[f"HTo{b}"]))
            return items

        ui = 0
        load(0)
        norm(0)
        for b in range(NBF):
            s2 = b % 2
            if b + 1 < NBF:
                load(b + 1)
            p2 = phase2_items(b - 1) if b > 0 else []
            per = (len(p2) + NFT - 1) // NFT if p2 else 0
            for ft in range(NFT):
                bufs = []
                for gu in range(2):
                    f = ft + gu * NFT
                    bk = (2 * ft + gu) % 4
                    u = ui % NU
                    ui += 1
                    bufs.append(u)
                    for kt in range(KT):
                        P.op("pe", lambda e, kt=kt, f=f, bk=bk: e.matmul(
                            psu[:, bk, :TBF], lhsT=win[:, kt, f * 128:(f + 1) * 128], rhs=xn[s2][:, kt, :],
                            start=(kt == 0), stop=(kt == KT - 1)),
                            reads=[f"win_{f // 11}", f"fxn{s2}"], writes=[f"psu{bk}"])
                    P.op("pool", lambda e, u=u, f=f: e.tensor_copy(out=ue[u][:, 0:2], in_=halo[:, f, :]),
                         reads=[f"halo{f}"], writes=[f"ueh{u}"])
                    P.op("act", lambda e, u=u, bk=bk: e.copy(out=ue[u][:, 2:TBF + 2], in_=psu[:, bk, :TBF]),
                         reads=[f"psu{bk}"], writes=[f"ue{u}"])
                    P.op("act", lambda e, u=u, bk=bk, f=f: e.activation(
                        out=yy[u][:, :], in_=psu[:, bk, :TBF], func=AF.Identity,
                        scale=cw[:, f, 2:3], bias=cw[:, f, 3:4]),
                        reads=[f"psu{bk}", "cw"], writes=[f"yy{u}"])
                    P.op("pool", lambda e, u=u, f=f: e.tensor_copy(out=halo[:, f, :], in_=ue[u][:, TBF:TBF + 2]),
                         reads=[f"ue{u}"], writes=[f"halo{f}"])
                    P.op("dve", lambda e, u=u, f=f: e.scalar_tensor_tensor(
                        out=yy[u][:, :], in0=ue[u][:, 1:TBF + 1], scalar=cw[:, f, 1:2], in1=yy[u][:, :],
                        op0=ALU.mult, op1=ALU.add), reads=[f"ue{u}", f"ueh{u}", f"yy{u}", "cw"], writes=[f"yy{u}"])
                    P.op("dve", lambda e, u=u, f=f: e.scalar_tensor_tensor(
                        out=yy[u][:, :], in0=ue[u][:, 0:TBF], scalar=cw[:, f, 0:1], in1=yy[u][:, :],
                        op0=ALU.mult, op1=ALU.add), reads=[f"ue{u}", f"ueh{u}", f"yy{u}", "cw"], writes=[f"yy{u}"])
                ug, uu = bufs
                sgi = ft % 2
                P.op("act", lambda e, ug=ug, sgi=sgi: e.activation(out=sg[sgi][:, :], in_=yy[ug][:, :], func=AF.Silu),
                     reads=[f"yy{ug}"], writes=[f"sg{sgi}"])
                P.op("dve", lambda e, uu=uu, sgi=sgi, ft=ft: e.tensor_tensor(
                    out=act[s2][:, ft, :], in0=sg[sgi][:, :], in1=yy[uu][:, :], op=ALU.mult),
                    reads=[f"sg{sgi}", f"yy{uu}"], writes=[f"act{s2}_{ft}"])
                for it in p2[ft * per:(ft + 1) * per]:
                    it()
                if ft == 8 and b + 1 < NBF:
                    norm(b + 1)
        for it in phase2_items(NBF - 1):
            it()
        P.barrier()


NH = 16
HD = 64
KA = 70


def kv_phase(P, nc, L, HT, w_kvf_d, bf_d, KTd, QAd, VTd, ones, epsb, gs):
    NB = L // TB
    with ExitStack() as ph:
        wk = ph.enter_context(nc.sbuf_tensor("k_w", [128, KT, 2064], BF16))
        xb = [ph.enter_context(nc.sbuf_tensor(f"k_xb{i}", [128, KT, TB], F32)) for i in range(2)]
        sq = ph.enter_context(nc.sbuf_tensor("k_sq", [128, KT, TB], BF16))
        rr = ph.enter_context(nc.sbuf_tensor("k_rr", [128, TB], F32))
        xnb = [ph.enter_context(nc.sbuf_tensor(f"k_xn{i}", [128, KT, TB], BF16)) for i in range(2)]
        kst = [ph.enter_context(nc.sbuf_tensor(f"k_kst{i}", [128, 8, TB], BF16)) for i in range(2)]
        vst = [ph.enter_context(nc.sbuf_tensor(f"k_vst{i}", [128, 4, NH, HD + 1], BF16)) for i in range(2)]
        negc = ph.enter_context(nc.sbuf_tensor("k_negc", [NH, L], F32))
        nbf = ph.enter_context(nc.sbuf_tensor("k_nbf", [NH, 1], F32))
        one1 = ph.enter_context(nc.sbuf_tensor("k_one1", [NH, 1], F32))
        onesf = ph.enter_context(nc.sbuf_tensor("k_onesf", [NH, TB], F32))
        e1 = ph.enter_context(nc.sbuf_tensor("k_e1", [NH, TB], F32))
        r1 = ph.enter_context(nc.sbuf_tensor("k_r1", [NH, TB], F32))
        sp3 = [ph.enter_context(nc.sbuf_tensor(f"k_sp3{i}", [NH, 3, TB], BF16)) for i in range(2)]
        ng3 = [ph.enter_context(nc.sbuf_tensor(f"k_ng3{i}", [NH, 3, TB], BF16)) for i in range(2)]
        on3 = ph.enter_context(nc.sbuf_tensor("k_on3", [NH, 3, TB], BF16))
        psn = ph.enter_context(nc.psum_tensor("k_psn", [128, 512], F32))
        psk = ph.enter_context(nc.psum_tensor("k_psk", [128, 3, 512], F32))
        psv = ph.enter_context(nc.psum_tensor("k_psv", [128, 3, 512], F32))
        psf = ph.enter_context(nc.psum_tensor("k_psf", [128, 512], F32))

        for kt in range(KT):
            P.dma("sp", wk[:, kt, :], w_kvf_d[kt * 128:(kt + 1) * 128, :], reads=WC["kvf"], writes=[f"wk{kt}"])
        wkr = [f"wk{kt}" for kt in range(KT)]
        P.dma("sp", nbf[:, :], bf_d[:, :], writes=["nbf"])
        P.op("dve", lambda e: e.tensor_scalar(out=nbf[:, :], in0=nbf[:, :], scalar1=-1.0, scalar2=None, op0=ALU.mult),
             reads=["nbf"], writes=["nbf"])
        P.op("dve", lambda e: e.memset(one1[:, :], 1.0), writes=["one1"])
        P.op("dve", lambda e: e.memset(onesf[:, :], 1.0), writes=["onesf"])
        P.op("dve", lambda e: e.memset(on3[:, :, :], 1.0), writes=["on3"])
        for i in range(2):
            P.op("pool", lambda e, i=i: e.memset(vst[i][:, :, :, :], 1.0), writes=[f"vst{i}"])

        def kv_norm(b):
            s_ = b % 2
            P.dma("sp", xb[s_][:, :, :], HT[:, :, b * TB:(b + 1) * TB].rearrange("k p t -> p k t"), writes=[f"kb{s_}x"])
            rms_block(P, xb[s_], f"kb{s_}x", sq, rr, psn, ones, epsb, gs[:, 8, :], xnb[s_], f"kxn{s_}", "kn_", TB)

        kv_norm(0)
        for b in range(NB):
            s = b % 2
            t0 = b * TB
            xn = xnb[s]
            kxn = f"kxn{s}"
            if b + 1 < NB:
                kv_norm(b + 1)
            for m in range(8):
                bk = (b * 8 + m) % 3
                for kt in range(KT):
                    P.op("pe", lambda e, m=m, kt=kt, bk=bk: e.matmul(
                        psk[:, bk, :], lhsT=wk[:, kt, m * 128:(m + 1) * 128], rhs=xn[:, kt, :],
                        start=(kt == 0), stop=(kt == KT - 1)), reads=wkr + [kxn], writes=[f"psk{bk}"])
                if m % 2 == 0:
                    P.op("act", lambda e, m=m, bk=bk, s=s: e.copy(out=kst[s][:, m, :], in_=psk[:, bk, :]),
                         reads=[f"psk{bk}"], writes=[f"kst{s}_{m}"])
                else:
                    P.op("dve", lambda e, m=m, bk=bk, s=s: e.tensor_copy(out=kst[s][:, m, :], in_=psk[:, bk, :]),
                         reads=[f"psk{bk}"], writes=[f"kst{s}_{m}"])
            for two in range(2):
                P.dma("pool", KTd[two::2, 0:HD, t0:t0 + TB].rearrange("m r t -> r m t"),
                      kst[s][two * 64:(two + 1) * 64, :, :],
                      reads=[f"kst{s}_{m}" for m in range(8)], writes=[f"KTd{b}_{two}"])
            for st in range(4):
                for c in range(2):
                    bk = (b * 8 + st * 2 + c) % 3
                    for kt in range(KT):
                        P.op("pe", lambda e, st=st, c=c, kt=kt, bk=bk: e.matmul(
                            psv[:, bk, :], lhsT=xn[:, kt, st * 128:(st + 1) * 128],
                            rhs=wk[:, kt, 1024 + c * 512:1024 + (c + 1) * 512],
                            start=(kt == 0), stop=(kt == KT - 1)), reads=wkr + [kxn], writes=[f"psv{bk}"])
                    src = psv[:, bk, :].rearrange("p (h d) -> p h d", d=HD)
                    if c == 0:
                        P.op("act", lambda e, st=st, c=c, s=s, src=src: e.copy(out=vst[s][:, st, c * 8:(c + 1) * 8, 0:HD], in_=src),
                             reads=[f"psv{bk}", f"vst{s}"], writes=[f"vst{s}_{st}_{c}"])
                    else:
                        P.op("dve", lambda e, st=st, c=c, s=s, src=src: e.tensor_copy(out=vst[s][:, st, c * 8:(c + 1) * 8, 0:HD], in_=src),
                             reads=[f"psv{bk}", f"vst{s}"], writes=[f"vst{s}_{st}_{c}"])
            P.dma("pool", VTd[b * 4:(b + 1) * 4, :, :, :].rearrange("s p h c -> p s (h c)"),
                  vst[s][:, :, :, :].rearrange("p s h c -> p s (h c)"),
                  reads=[f"vst{s}"] + [f"vst{s}_{st}_{c}" for st in range(4) for c in range(2)], writes=[f"VTd{b}"])
            for kt in range(KT):
                P.op("pe", lambda e, kt=kt: e.matmul(psf[0:NH, :], lhsT=wk[:, kt, 2048:2064], rhs=xn[:, kt, :],
                                                     start=(kt == 0), stop=(kt == KT - 1)),
                     reads=wkr + [kxn], writes=["psf"])
            P.op("act", lambda e: e.activation(out=e1[:, :], in_=psf[0:NH, :], func=AF.Exp, scale=-1.0, bias=nbf[:, 0:1]),
                 reads=["psf", "nbf"], writes=["e1"])
            P.op("act", lambda e: e.activation(out=e1[:, :], in_=e1[:, :], func=AF.Ln, scale=1.0, bias=one1[:, 0:1]),
                 reads=["e1", "one1"], writes=["e1"])
            init = 0.0 if b == 0 else negc[:, t0 - 1:t0]
            P.op("dve", lambda e, t0=t0, init=init: e.tensor_tensor_scan(
                out=negc[:, t0:t0 + TB], data0=onesf[:, :], data1=e1[:, :], initial=init, op0=ALU.mult, op1=ALU.add),
                reads=["e1", "onesf", "negc"], writes=["negc"])
            cb = negc[:, t0:t0 + TB]
            P.op("dve", lambda e, s=s, cb=cb: e.tensor_copy(out=sp3[s][:, 0, :], in_=cb), reads=["negc"], writes=[f"sp3{s}"])
            P.op("dve", lambda e, s=s, cb=cb: e.tensor_tensor(out=r1[:, :], in0=cb, in1=sp3[s][:, 0, :], op=ALU.subtract),
                 reads=["negc", f"sp3{s}"], writes=["r1"])
            P.op("dve", lambda e, s=s: e.tensor_copy(out=sp3[s][:, 1, :], in_=r1[:, :]), reads=["r1", f"sp3{s}"], writes=[f"sp3{s}"])
            P.op("dve", lambda e, s=s: e.tensor_tensor(out=r1[:, :], in0=r1[:, :], in1=sp3[s][:, 1, :], op=ALU.subtract),
                 reads=["r1", f"sp3{s}"], writes=["r1"])
            P.op("dve", lambda e, s=s: e.tensor_copy(out=sp3[s][:, 2, :], in_=r1[:, :]), reads=["r1", f"sp3{s}"], writes=[f"sp3{s}"])
            P.op("dve", lambda e, s=s: e.tensor_scalar(out=ng3[s][:, :, :], in0=sp3[s][:, :, :], scalar1=-1.0, scalar2=None,
                                                       op0=ALU.mult), reads=[f"sp3{s}"], writes=[f"ng3{s}"])
            P.dma("pool", KTd[:, HD:HD + 3, t0:t0 + TB], sp3[s][:, :, :], reads=[f"sp3{s}"], writes=[f"KTa{b}"])
            P.dma("pool", KTd[:, HD + 3:HD + 6, t0:t0 + TB], on3[:, :, :], reads=["on3"], writes=[f"KTb{b}"])
            P.dma("pool", QAd[:, 0:3, t0:t0 + TB], on3[:, :, :], reads=["on3"], writes=[f"QAa{b}"])
            P.dma("pool", QAd[:, 3:6, t0:t0 + TB], ng3[s][:, :, :], reads=[f"ng3{s}"], writes=[f"QAb{b}"])
        P.barrier()


def fox_phase(P, nc, L, layer, HT, w_q_d, w_o_d, KTd, QAd, VTd, OTd, QTd, tri_d, ones, epsb, gs, ident, pf=None):
    j = layer - 2
    NB = L // TB
    NST = L // 128
    with ExitStack() as ph_outer:
        ph = ExitStack()
        xn = ph.enter_context(nc.sbuf_tensor("a_xn", [128, KT, L], BF16))
        wq = ph.enter_context(nc.sbuf_tensor("a_wq", [128, KT, D], BF16))
        trif = ph.enter_context(nc.sbuf_tensor("a_trif", [128, 128], F32))
        tri = ph.enter_context(nc.sbuf_tensor("a_tri", [128, 128], BF16))
        bc1 = ph.enter_context(nc.sbuf_tensor("a_bc1", [128, HD], F32))
        P.dma("sp", wq[:, :, :], w_q_d[j].rearrange("(k p) d -> p k d", p=128), reads=WC[f"q{j}"], writes=["wq"])
        wqr = ["wq"]
        P.dma("sp", trif[:, :], tri_d[:, :], writes=["trif"])
        P.op("dve", lambda e: e.tensor_scalar(out=tri[:, :], in0=trif[:, :], scalar1=-1.0, scalar2=30000.0, op0=ALU.add, op1=ALU.mult),
             reads=["trif"], writes=["tri"])
        identb = ph.enter_context(nc.sbuf_tensor("a_identb", [128, 128], BF16))
        P.op("dve", lambda e: e.tensor_copy(out=identb[:, :], in_=ident[:, :]), reads=["ident"], writes=["identb"])
        P.op("dve", lambda e: e.memset(bc1[:, :], 1.0), writes=["bc1"])

        with ExitStack() as p1:
            xb = [p1.enter_context(nc.sbuf_tensor(f"a_xb{i}", [128, KT, TB], F32)) for i in range(2)]
            sq = p1.enter_context(nc.sbuf_tensor("a_sq", [128, KT, TB], BF16))
            rr = p1.enter_context(nc.sbuf_tensor("a_rr", [128, TB], F32))
            psn = p1.enter_context(nc.psum_tensor("a_psn", [128, 512], F32))
            psqp = p1.enter_context(nc.psum_tensor("a_psqp", [128, 2, 512], F32))
            qst = [p1.enter_context(nc.sbuf_tensor(f"a_qst{i}", [128, 8, TB], BF16)) for i in range(2)]
            def a_norm(b):
                s_ = b % 2
                P.dma("sp", xb[s_][:, :, :], HT[:, :, b * TB:(b + 1) * TB].rearrange("k p t -> p k t"), writes=[f"ab{s_}x"])
                rms_block(P, xb[s_], f"ab{s_}x", sq, rr, psn, ones, epsb, gs[:, layer, :], xn[:, :, b * TB:(b + 1) * TB],
                          f"axn{b}", "an_", TB)

            a_norm(0)
            for b in range(NB):
                s = b % 2
                if b + 1 < NB:
                    a_norm(b + 1)
                for m in range(8):
                    bk = m % 2
                    for kt in range(KT):
                        P.op("pe", lambda e, m=m, kt=kt, bk=bk, b=b: e.matmul(
                            psqp[:, bk, :], lhsT=wq[:, kt, m * 128:(m + 1) * 128], rhs=xn[:, kt, b * TB:(b + 1) * TB],
                            start=(kt == 0), stop=(kt == KT - 1)), reads=wqr + [f"axn{b}"], writes=[f"psqp{bk}"])
                    if m % 2 == 0:
                        P.op("act", lambda e, m=m, bk=bk, s=s: e.activation(out=qst[s][:, m, :], in_=psqp[:, bk, :], func=AF.Copy, scale=float(HD ** -0.5)),
                             reads=[f"psqp{bk}"], writes=[f"qst{s}_{m}"])
                    else:
                        P.op("dve", lambda e, m=m, bk=bk, s=s: e.tensor_scalar(out=qst[s][:, m, :], in0=psqp[:, bk, :], scalar1=float(HD ** -0.5),
                                                                              scalar2=None, op0=ALU.mult),
                             reads=[f"psqp{bk}"], writes=[f"qst{s}_{m}"])
                for two in range(2):
                    P.dma("pool", QTd[two::2, :, b * TB:(b + 1) * TB].rearrange("m r t -> r m t"),
                          qst[s][two * 64:(two + 1) * 64, :, :],
                          reads=[f"qst{s}_{m}" for m in range(8)], writes=[f"QTd{b}_{two}"])
            P.barrier()
        xnr = []

        with ExitStack() as p2:
            qT = [p2.enter_context(nc.sbuf_tensor(f"a_qT{i}", [KA, L], BF16)) for i in range(2)]
            kT = [p2.enter_context(nc.sbuf_tensor(f"a_kT{i}", [KA, L], BF16)) for i in range(2)]
            vh = [p2.enter_context(nc.sbuf_tensor(f"a_vh{i}", [128, NST, 128], BF16)) for i in range(2)]
            NG = 3
            NP = 4
            LA = 2
            pT = [p2.enter_context(nc.sbuf_tensor(f"a_pT{i}", [128, 2, TB], BF16)) for i in range(NP)]
            osb = [p2.enter_context(nc.sbuf_tensor(f"a_osb{i}", [128, TB], F32)) for i in range(2)]
            rc = p2.enter_context(nc.sbuf_tensor("a_rc", [128, TB], F32))
            oT = [p2.enter_context(nc.sbuf_tensor(f"a_oT{i}", [HD, TB], BF16)) for i in range(2)]
            psS = p2.enter_context(nc.psum_tensor("a_psS", [128, NG, 2, 512], F32))
            psO = p2.enter_context(nc.psum_tensor("a_psO", [128, 512], F32))
            psX = p2.enter_context(nc.psum_tensor("a_psX", [128, 512], F32))
            oi = 0
            P.op("pool", lambda e: e.memset(rc[:, :], 0.0), writes=["rc"])
            for i in range(2):
                P.op("pool", lambda e, i=i: e.memset(vh[i][:, :, :], 1.0), writes=[f"vh{i}"])
            gn = 0
            for h in range(NH):
                hs = h % 2
                P.dma("sp", kT[hs][:, :], KTd[h, :, :], writes=[f"kT{hs}"])
                P.dma("sp", vh[hs][:, :, 0:HD + 1], VTd[:, :, h, :].rearrange("s p c -> p s c"), writes=[f"vh{hs}"])
                P.dma("sp", qT[hs][HD:KA, :], QAd[h, :, :], writes=[f"qTa{hs}"])
                P.dma("sp", qT[hs][0:HD, :], QTd[h, :, :], writes=[f"qT{hs}"])
                jobs = [(qb, g) for qb in range(NB) for g in range(2 * (qb + 1))]

                def lo_of(qb, st):
                    return 0 if st < 4 * qb else (st - 4 * qb) * 128

                def emit_qk(n, g_, hs=hs):
                    qb, g = jobs[n]
                    gb = g_ % NG
                    pb = g_ % NP
                    los = []
                    for t in range(2):
                        st = 2 * g + t
                        lo = lo_of(qb, st)
                        los.append(lo)
                        diag = st >= 4 * qb
                        P.op("pe", lambda e, st=st, lo=lo, t=t, diag=diag: e.matmul(
                            psS[:, gb, t, lo:TB], lhsT=kT[hs][:, st * 128:(st + 1) * 128],
                            rhs=qT[hs][:, qb * TB + lo:(qb + 1) * TB], start=True, stop=not diag),
                            reads=[f"kT{hs}", f"qTa{hs}", f"qT{hs}"], writes=[f"psS{gb}"])
                        if diag:
                            P.op("pe", lambda e, lo=lo, t=t: e.matmul(
                                psS[:, gb, t, lo:lo + 128], lhsT=identb[:, :], rhs=tri[:, :], start=False, stop=True),
                                reads=["tri", "identb"], writes=[f"psS{gb}"])
                    if los == [0, 0]:
                        P.op("act", lambda e: e.activation(out=pT[pb][:, :, :], in_=psS[:, gb, :, :], func=AF.Exp),
                             reads=[f"psS{gb}"], writes=[f"pT{pb}"])
                    else:
                        for t in range(2):
                            lo = los[t]
                            P.op("act", lambda e, t=t, lo=lo: e.activation(out=pT[pb][:, t, lo:TB], in_=psS[:, gb, t, lo:TB], func=AF.Exp),
                                 reads=[f"psS{gb}"], writes=[f"pT{pb}"])

                def emit_pv(n, g_, hs=hs):
                    qb, g = jobs[n]
                    pb = g_ % NP
                    nst = 4 * (qb + 1)
                    for t in range(2):
                        st = 2 * g + t
                        lo = lo_of(qb, st)
                        P.op("pe", lambda e, st=st, lo=lo, t=t: e.matmul(
                            psO[:, lo:TB], lhsT=vh[hs][:, st, :], rhs=pT[pb][:, t, lo:TB],
                            start=(st == 0), stop=(st == nst - 1)),
                            reads=[f"vh{hs}", f"pT{pb}"], writes=["psO"])

                pending = []
                for a_ in range(min(LA, len(jobs))):
                    emit_qk(a_, gn + a_)
                for n in range(len(jobs)):
                    qb, g = jobs[n]
                    if n + LA < len(jobs):
                        emit_qk(n + LA, gn + n + LA)
                    emit_pv(n, gn + n)
                    for p_ in pending:
                        p_[0] -= 1
                    while pending and pending[0][0] <= 0:
                        pending.pop(0)[1]()
                    if g == 2 * (qb + 1) - 1:
                        ob = oi % 2
                        oi += 1
                        P.op("dve", lambda e, ob=ob: e.tensor_copy(out=osb[ob][:, :], in_=psO[:, :]),
                             reads=["psO"], writes=[f"osb{ob}"])
                        P.op("dve", lambda e, ob=ob: e.reciprocal(out=rc[HD:128, :], in_=osb[ob][HD:128, :]),
                             reads=[f"osb{ob}"], writes=["rc"])

                        def tail(ob=ob, qb=qb, h=h):
                            P.op("pe", lambda e: e.matmul(psX[0:HD, :], lhsT=ident[:, HD:128], rhs=rc[:, :], start=True, stop=True),
                                 reads=["rc", "ident"], writes=["psX"])
                            P.op("dve", lambda e: e.tensor_tensor(out=oT[ob][:, :], in0=osb[ob][0:HD, :], in1=psX[0:HD, :], op=ALU.mult),
                                 reads=[f"osb{ob}", "psX"], writes=[f"oT{ob}"])
                            P.dma("pool", OTd[h, :, qb * TB:(qb + 1) * TB], oT[ob][:, :], reads=[f"oT{ob}"], writes=[f"OTd{h}_{qb}"])
                        pending.append([4, tail])
                for p_ in pending:
                    p_[1]()
                gn += len(jobs)
            P.barrier()

        ph.close()
        if pf is not None:
            pf.alloc()
        with ExitStack() as p3:
            wo = p3.enter_context(nc.sbuf_tensor("a_wo", [128, KT, D], BF16))
            P.dma("sp", wo[:, :, :], w_o_d[j].rearrange("(k p) d -> p k d", p=128), reads=WC[f"o{j}"], writes=["wo"])
            wor = ["wo"]
            xb = [p3.enter_context(nc.sbuf_tensor(f"a_xc{i}", [128, KT, TB], F32)) for i in range(2)]
            ot = [p3.enter_context(nc.sbuf_tensor(f"a_ot{i}", [128, KT, TB], BF16)) for i in range(2)]
            pso = p3.enter_context(nc.psum_tensor("a_pso", [128, 2, 512], F32))
            for b in range(NB):
                s = b % 2
                t0 = b * TB
                P.dma("sp", xb[s][:, :, :], HT[:, :, t0:t0 + TB].rearrange("k p t -> p k t"), writes=[f"ac{s}x"])
                P.dma("sp", ot[s][:, :, :], OTd[:, :, t0:t0 + TB].rearrange("(k h2) r t -> (h2 r) k t", h2=2), writes=[f"ot{s}"])
                if pf is not None and b % 2 == 1:
                    pf.load_some(1)
                for m in range(KT):
                    bk = m % 2
                    for h in range(KT):
                        P.op("pe", lambda e, m=m, h=h, bk=bk, s=s: e.matmul(
                            pso[:, bk, :], lhsT=wo[:, h, m * 128:(m + 1) * 128], rhs=ot[s][:, h, :],
                            start=(h == 0), stop=(h == KT - 1)), reads=wor + [f"ot{s}"], writes=[f"pso{bk}"])
                    P.op("dve", lambda e, m=m, bk=bk, s=s: e.tensor_tensor(
                        out=xb[s][:, m, :], in0=xb[s][:, m, :], in1=pso[:, bk, :], op=ALU.add),
                        reads=[f"pso{bk}", f"ac{s}x"], writes=[f"ac{s}o{m}"])
                P.dma("pool", HT[:, :, t0:t0 + TB].rearrange("k p t -> p k t"), xb[s][:, :, :],
                      reads=[f"ac{s}o{m}" for m in range(KT)] + [f"ac{s}x"], writes=[f"HTo{b}"])
            P.barrier()


NPAIR = 32
NT96 = 11
TC = 8


def s5_phase(P, nc, L, layer, HT, UTd, Gd, s5p_d, s5b_d, s5c_d, dv_d, w_glu_d, ones, epsb, gs, ident, after_norm=None, pf=None):
    NB = L // TB
    NC = L // TC
    PADC = NC // 2
    NLEV = int(np.ceil(np.log2(NC)))
    assert NC <= 512
    with ExitStack() as ph_outer:
        ph = ExitStack()
        Kb = ph.enter_context(nc.sbuf_tensor("s_Kb", [96, NT96, TC, 96], BF16))
        BE = ph.enter_context(nc.sbuf_tensor("s_BE", [96, NT96, TC, 2, 128], BF16))
        CE = ph.enter_context(nc.sbuf_tensor("s_CE", [128, NPAIR, TC, 2, 32], BF16))
        APW = ph.enter_context(nc.sbuf_tensor("s_APW", [128, 10, 2, NPAIR], F32))
        NAI = ph.enter_context(nc.sbuf_tensor("s_NAI", [128, 10, NPAIR], F32))
        dv = ph.enter_context(nc.sbuf_tensor("s_dv", [96, NT96], F32))
        P.dma("sp", dv[:, :], dv_d[layer], writes=["dv"])

        with ExitStack() as p0:
            T = lambda name, shape, dt=F32: p0.enter_context(nc.sbuf_tensor("s0_" + name, shape, dt))
            prm = T("prm", [128, NPAIR, 3])
            Bm = T("Bm", [128, NPAIR, 2, 16])
            Cm = T("Cm", [128, NPAIR, 2, 16])
            PW = T("PW", [128, 9, 2, NPAIR])
            dt_ = T("dt", [128, NPAIR]); ar = T("ar", [128, NPAIR]); ai = T("ai", [128, NPAIR])
            t1 = T("t1", [128, NPAIR]); t2 = T("t2", [128, NPAIR]); t3 = T("t3", [128, NPAIR])
            zz = T("zz", [128, 5, 2, NPAIR])
            Fr = T("Fr", [128, 2, NPAIR])
            hpi = T("hpi", [128, 1])
            Bb = T("Bb", [128, 2, NPAIR, 16])
            W = [T(f"W{i}", [128, 2, NPAIR, 16]) for i in range(2)]
            u1 = T("u1", [128, NPAIR, 16]); u2 = T("u2", [128, NPAIR, 16])
            Cz = T("Cz", [128, 2, 33 * 32], BF16)
            Zall = [T(f"Z{i}", [128, 2, 33 * 32], BF16) for i in range(2)]
            identb = T("identb", [128, 128], BF16)
            psT = p0.enter_context(nc.psum_tensor("s0_psT", [128, 4, 2, 512], BF16))
            psK = p0.enter_context(nc.psum_tensor("s0_psK", [128, 3, 512], F32))

            n_xb = [T(f"nxb{i}", [128, KT, TB]) for i in range(2)]
            n_sq = T("nsq", [128, KT, TB], BF16)
            n_rr = T("nrr", [128, TB])
            n_xn = [T(f"nxn{i}", [128, KT, TB], BF16) for i in range(2)]
            n_ps = p0.enter_context(nc.psum_tensor("s0_psn", [128, 512], F32))

            def n_load(b):
                if b < NB:
                    P.dma("sp", n_xb[b % 2][:, :, :], HT[:, :, b * TB:(b + 1) * TB].rearrange("k p t -> p k t"), writes=[f"sb{b % 2}x"])

            def n_block(b):
                if b >= NB:
                    return
                s_ = b % 2
                n_load(b + 1)
                rms_block(P, n_xb[s_], f"sb{s_}x", n_sq, n_rr, n_ps, ones, epsb, gs[:, layer, :], n_xn[s_], f"sxn{s_}", "sn_", TB)
                P.dma("pool", UTd[:, b * TB:(b + 1) * TB].rearrange("(k p) t -> p k t", p=128), n_xn[s_][:, :, :],
                      reads=[f"sxn{s_}"], writes=[f"UTd{b}"])

            n_load(0)
            P.dma("sp", prm[:, :, :], s5p_d[layer], writes=["prm"])
            P.dma("sp", Bm[:, :, :, :], s5b_d[layer], writes=["Bm"])
            P.dma("sp", Cm[:, :, :, :], s5c_d[layer], writes=["Cm"])
            P.op("dve", lambda e: e.memset(hpi[:, :], float(np.pi / 2)), writes=["hpi"])
            P.op("dve", lambda e: e.tensor_copy(out=identb[:, :], in_=ident[:, :]), reads=["ident"], writes=["identb"])
            P.op("pool", lambda e: e.memset(Kb[:, :, :, :], 0.0), writes=["Kb"])
            for i in range(3):
                P.op("dve", lambda e, i=i: e.memset(psK[:, i, :], 0.0), writes=[f"psK{i}"])
            P.op("pool", lambda e: e.memset(CE[:, :, :, :, :], 0.0), writes=["CE"])
            P.op("pool", lambda e: e.memset(Cz[:, :, :], 0.0), writes=["Cz"])
            for i in range(2):
                P.op("pool", lambda e, i=i: e.memset(Zall[i][:, :, :], 0.0), writes=[f"Z{i}"])
            lr, li, ldt = prm[:, :, 0], prm[:, :, 1], prm[:, :, 2]

            def dve(fn, reads, writes):
                P.op("dve", fn, reads=reads, writes=writes)

            def tt(out, a, b, op, r, w):
                dve(lambda e: e.tensor_tensor(out=out, in0=a, in1=b, op=op), r, w)

            def cmul(o_re, o_im, a_re, a_im, b_re, b_im, r, w):
                tt(t1[:, :], a_re, b_re, ALU.mult, r, ["t1"])
                tt(t2[:, :], a_im, b_im, ALU.mult, r, ["t2"])
                tt(o_re, t1[:, :], t2[:, :], ALU.subtract, ["t1", "t2"], w)
                tt(t1[:, :], a_re, b_im, ALU.mult, r + w, ["t1"])
                tt(t2[:, :], a_im, b_re, ALU.mult, r + w, ["t2"])
                tt(o_im, t1[:, :], t2[:, :], ALU.add, ["t1", "t2"], w)

            P.op("act", lambda e: e.activation(out=dt_[:, :], in_=ldt, func=AF.Exp), reads=["prm"], writes=["dt"])
            tt(ar[:, :], lr, dt_[:, :], ALU.mult, ["prm", "dt"], ["ar"])
            tt(ai[:, :], li, dt_[:, :], ALU.mult, ["prm", "dt"], ["ai"])
            P.op("act", lambda e: e.activation(out=t3[:, :], in_=ar[:, :], func=AF.Exp, scale=1.0 / 16), reads=["ar"], writes=["t3"])
            P.op("act", lambda e: e.activation(out=zz[:, 0, 1, :], in_=ai[:, :], func=AF.Sin, scale=1.0 / 16), reads=["ai"], writes=["zs"])
            P.op("act", lambda e: e.activation(out=zz[:, 0, 0, :], in_=ai[:, :], func=AF.Sin, scale=-1.0 / 16, bias=hpi[:, 0:1]),
                 reads=["ai", "hpi"], writes=["zc"])
            tt(zz[:, 0, 0, :], zz[:, 0, 0, :], t3[:, :], ALU.mult, ["zc", "t3"], ["zz0"])
            tt(zz[:, 0, 1, :], zz[:, 0, 1, :], t3[:, :], ALU.mult, ["zs", "t3"], ["zz0"])
            for k in range(4):
                cmul(zz[:, k + 1, 0, :], zz[:, k + 1, 1, :], zz[:, k, 0, :], zz[:, k, 1, :], zz[:, k, 0, :], zz[:, k, 1, :],
                     [f"zz{k}"], [f"zz{k + 1}"])
            lb_re, lb_im = zz[:, 4, 0, :], zz[:, 4, 1, :]
            dve(lambda e: e.memset(PW[:, 0, 0, :], 1.0), [], ["pw0"])
            dve(lambda e: e.memset(PW[:, 0, 1, :], 0.0), ["pw0"], ["pw0"])
            dve(lambda e: e.tensor_copy(out=PW[:, 1, :, :], in_=zz[:, 4, :, :]), ["zz4"], ["pw1"])
            for k in range(2, 9):
                cmul(PW[:, k, 0, :], PW[:, k, 1, :], PW[:, k - 1, 0, :], PW[:, k - 1, 1, :], lb_re, lb_im,
                     [f"pw{k - 1}", "zz4"], [f"pw{k}"])
            pwr = [f"pw{k}" for k in range(9)]
            dve(lambda e: e.tensor_copy(out=APW[:, 0, :, :], in_=PW[:, 8, :, :]), ["pw8"], ["apw0"])
            for m in range(1, 10):
                cmul(APW[:, m, 0, :], APW[:, m, 1, :], APW[:, m - 1, 0, :], APW[:, m - 1, 1, :],
                     APW[:, m - 1, 0, :], APW[:, m - 1, 1, :], [f"apw{m - 1}"], [f"apw{m}"])
            apr = [f"apw{m}" for m in range(10)]
            dve(lambda e: e.tensor_scalar(out=NAI[:, :, :], in0=APW[:, :, 1, :], scalar1=-1.0, scalar2=None, op0=ALU.mult), apr, ["nai"])
            tt(t3[:, :], lr, lr, ALU.mult, ["prm"], ["t3"])
            tt(t1[:, :], li, li, ALU.mult, ["prm"], ["t1"])
            tt(t3[:, :], t3[:, :], t1[:, :], ALU.add, ["t3", "t1"], ["t3"])
            dve(lambda e: e.reciprocal(out=t3[:, :], in_=t3[:, :]), ["t3"], ["t3"])
            dve(lambda e: e.tensor_scalar(out=ar[:, :], in0=lb_re, scalar1=-1.0, scalar2=None, op0=ALU.add), ["zz4", "ar"], ["ar"])
            tt(t1[:, :], ar[:, :], lr, ALU.mult, ["ar", "prm"], ["t1"])
            tt(t2[:, :], lb_im, li, ALU.mult, ["zz4", "prm"], ["t2"])
            tt(t1[:, :], t1[:, :], t2[:, :], ALU.add, ["t1", "t2"], ["t1"])
            tt(Fr[:, 0, :], t1[:, :], t3[:, :], ALU.mult, ["t1", "t3"], ["fr"])
            tt(t1[:, :], lb_im, lr, ALU.mult, ["zz4", "prm"], ["t1"])
            tt(t2[:, :], ar[:, :], li, ALU.mult, ["ar", "prm"], ["t2"])
            tt(t1[:, :], t1[:, :], t2[:, :], ALU.subtract, ["t1", "t2"], ["t1"])
            tt(Fr[:, 1, :], t1[:, :], t3[:, :], ALU.mult, ["t1", "t3"], ["fi"])

            def bc(ap2):
                return ap2.unsqueeze(2).to_broadcast([128, NPAIR, 16])

            def cmul_b(o_re, o_im, s_re, s_im, x_re, x_im, r, w):
                tt(u1[:, :, :], x_re, bc(s_re), ALU.mult, r, ["u1"])
                tt(u2[:, :, :], x_im, bc(s_im), ALU.mult, r, ["u2"])
                tt(o_re, u1[:, :, :], u2[:, :, :], ALU.subtract, ["u1", "u2"], w)
                tt(u1[:, :, :], x_im, bc(s_re), ALU.mult, r + w, ["u1"])
                tt(u2[:, :, :], x_re, bc(s_im), ALU.mult, r + w, ["u2"])
                tt(o_im, u1[:, :, :], u2[:, :, :], ALU.add, ["u1", "u2"], w)

            cmul_b(Bb[:, 0, :, :], Bb[:, 1, :, :], Fr[:, 0, :], Fr[:, 1, :], Bm[:, :, 0, :], Bm[:, :, 1, :],
                   ["fr", "fi", "Bm"], ["Bb"])
            for g2 in range(2):
                rows = slice(64 * g2, 64 * g2 + 64)
                czv = Cz[rows, 0, :].rearrange("p (q g h) -> p q g h", g=2, h=16)
                dve(lambda e, czv=czv, rows=rows, g2=g2: e.tensor_copy(out=czv[:, 0:NPAIR, g2, :], in_=Cm[rows, :, 0, :]),
                    ["Cm", "Cz"], ["Cz"])
                czv2 = Cz[rows, 1, :].rearrange("p (q g h) -> p q g h", g=2, h=16)
                dve(lambda e, czv2=czv2, rows=rows, g2=g2: e.tensor_scalar(out=czv2[:, 0:NPAIR, g2, :], in0=Cm[rows, :, 1, :],
                                                                            scalar1=-1.0, scalar2=None, op0=ALU.mult),
                    ["Cm", "Cz"], ["Cz"])
            for i in range(TC):
                cmul_b(W[0][:, 0, :, :], W[0][:, 1, :, :], PW[:, i + 1, 0, :], PW[:, i + 1, 1, :], Cm[:, :, 0, :], Cm[:, :, 1, :],
                       pwr + ["Cm"], ["W0"])
                for g2 in range(2):
                    rows = slice(64 * g2, 64 * g2 + 64)
                    dve(lambda e, rows=rows, g2=g2, i=i: e.tensor_copy(out=CE[rows, :, i, 0, 16 * g2:16 * g2 + 16], in_=W[0][rows, 0, :, :]),
                        ["W0", "CE"], ["CE"])
                    dve(lambda e, rows=rows, g2=g2, i=i: e.tensor_scalar(out=CE[rows, :, i, 1, 16 * g2:16 * g2 + 16], in0=W[0][rows, 1, :, :],
                                                                         scalar1=-1.0, scalar2=None, op0=ALU.mult),
                        ["W0", "CE"], ["CE"])
            for tau in range(TC):
                zb = tau % 2
                cmul_b(W[1][:, 0, :, :], W[1][:, 1, :, :], PW[:, tau, 0, :], PW[:, tau, 1, :], Bb[:, 0, :, :], Bb[:, 1, :, :],
                       pwr + ["Bb"], ["W1"])
                for g2 in range(2):
                    rows = slice(64 * g2, 64 * g2 + 64)
                    for ri in range(2):
                        zv = Zall[zb][rows, ri, :].rearrange("p (q g h) -> p q g h", g=2, h=16)
                        dve(lambda e, zv=zv, rows=rows, g2=g2, ri=ri: e.tensor_copy(out=zv[:, 0:NPAIR, g2, :], in_=W[1][rows, ri, :, :]),
                            ["W1", f"Z{zb}"], [f"Z{zb}"])
                for k in range(NT96):
                    npq = 3 if k < NT96 - 1 else 2
                    R = 32 * npq
                    tb = k % 4
                    kb_ = (tau * NT96 + k) % 3
                    for ri in range(2):
                        P.op("pe", lambda e, k=k, R=R, ri=ri, tb=tb, zb=zb: e.transpose(
                            out=psT[0:R, tb, ri, 0:128], in_=Zall[zb][:, ri, 96 * k:96 * k + R], identity=identb[:, :]),
                            reads=[f"Z{zb}", "identb"], writes=[f"psT{tb}"])
                    P.op("act", lambda e, k=k, R=R, tb=tb, tau=tau: e.copy(out=BE[0:R, k, TC - 1 - tau, :, :], in_=psT[0:R, tb, :, 0:128]),
                         reads=[f"psT{tb}"], writes=["BE"])
                    for q3 in range(npq):
                        c0 = 96 * k + 32 * q3
                        for ri in range(2):
                            P.op("pe", lambda e, q3=q3, c0=c0, ri=ri, kb_=kb_, zb=zb: e.matmul(
                                psK[32 * q3:32 * q3 + 32, kb_, 32 * q3:32 * q3 + 32], lhsT=Zall[zb][:, ri, c0:c0 + 32],
                                rhs=Cz[:, ri, c0:c0 + 32], start=(ri == 0), stop=(ri == 1)),
                                reads=[f"Z{zb}", "Cz"], writes=[f"psK{kb_}"])
                    P.op("dve", lambda e, R=R, k=k, kb_=kb_, tau=tau: e.tensor_copy(
                        out=Kb[0:R, k, tau, 0:R], in_=psK[0:R, kb_, 0:R]),
                        reads=[f"psK{kb_}", "Kb"], writes=["Kb"])
                for bb in range(tau * NB // TC, (tau + 1) * NB // TC):
                    n_block(bb)
            for k in range(NT96):
                R = 96 if k < NT96 - 1 else 64
                P.op("dve", lambda e, k=k, R=R: e.scalar_tensor_tensor(
                    out=Kb[0:R, k, 0, 0:R], in0=ident[0:R, 0:R], scalar=dv[0:R, k:k + 1], in1=Kb[0:R, k, 0, 0:R],
                    op0=ALU.mult, op1=ALU.add), reads=["Kb", "ident", "dv"], writes=["Kb"])
            P.barrier()

        with ExitStack() as p2:
            ut = [p2.enter_context(nc.sbuf_tensor(f"s2_ut{i}", [96, L], BF16)) for i in range(2)]
            ut8 = [p2.enter_context(nc.sbuf_tensor(f"s2_ut8{i}", [96, TC, NC], BF16)) for i in range(2)]
            go = [p2.enter_context(nc.sbuf_tensor(f"s2_go{i}", [96, L], BF16)) for i in range(1)]
            NES = 3
            POOL_EVERY = 10 ** 9
            EA = [p2.enter_context(nc.sbuf_tensor(f"s2_E{i}", [128, 2, PADC + NC], F32)) for i in range(2 * NES)]
            ptmp = p2.enter_context(nc.sbuf_tensor("s2_ptmp", [128, NC], F32))
            Sbf = [p2.enter_context(nc.sbuf_tensor(f"s2_S{i}", [128, 3, 2, 1 + NC], BF16)) for i in range(2)]
            tmp = [p2.enter_context(nc.sbuf_tensor(f"s2_tmp{i}", [96, NC], F32)) for i in range(3)]
            psE = p2.enter_context(nc.psum_tensor("s2_psE", [128, 2, 2, 512], F32))
            psY = p2.enter_context(nc.psum_tensor("s2_psY", [128, 4, 512], F32))
            for i in range(2 * NES):
                P.op("pool", lambda e, i=i: e.memset(EA[i][:, :, :], 0.0), writes=[f"E{i}r", f"E{i}i"])
            for i in range(2):
                P.op("pool", lambda e, i=i: e.memset(Sbf[i][:, :, :, :], 0.0), writes=[f"S{i}"])
            tiles = [(k, 3 if k < NT96 - 1 else 2) for k in range(NT96)]
            pairs = [(k, q3) for (k, npq) in tiles for q3 in range(npq)]
            loaded = set()

            def load_tile(k):
                if k in loaded or k >= NT96:
                    return
                loaded.add(k)
                R = 32 * tiles[k][1]
                us = k % 2
                P.dma("sp", ut[us][0:R, :], UTd[96 * k:96 * k + R, :], writes=[f"ut{us}"])
                P.op("pool", lambda e: e.tensor_copy(out=ut8[us][0:R, :, :], in_=ut[us][0:R, :].rearrange("p (c j) -> p j c", j=TC)),
                     reads=[f"ut{us}"], writes=[f"ut8{us}"])

            def emit_B(pi_):
                k, q3 = pairs[pi_]
                load_tile(k)
                us = k % 2
                es_ = pi_ % NES
                pb_ = pi_ % 2
                rows = slice(32 * q3, 32 * q3 + 32)
                for ri in range(2):
                    for j in range(TC):
                        P.op("pe", lambda e, j=j, ri=ri: e.matmul(
                            psE[:, pb_, ri, 0:NC], lhsT=BE[rows, k, j, ri, :], rhs=ut8[us][rows, j, :],
                            start=(j == 0), stop=(j == TC - 1)), reads=["BE", f"ut8{us}"], writes=[f"psE{pb_}_{ri}"])
                P.op("act", lambda e: e.copy(out=EA[2 * es_][:, :, PADC:PADC + NC], in_=psE[:, pb_, :, 0:NC]),
                     reads=[f"psE{pb_}_0", f"psE{pb_}_1"], writes=[f"E{2 * es_}r", f"E{2 * es_}i"])

            def emit_LOG(pi_):
                k, q3 = pairs[pi_]
                q = 3 * k + q3
                us = k % 2
                es_ = pi_ % NES
                src = 0
                for m in range(NLEV):
                    sh = 1 << m
                    X, Y = EA[2 * es_ + src], EA[2 * es_ + 1 - src]
                    xr, yr = f"E{2 * es_ + src}", f"E{2 * es_ + 1 - src}"
                    a_re, a_im, na_im = APW[:, m, 0, q:q + 1], APW[:, m, 1, q:q + 1], NAI[:, m, q:q + 1]
                    cur = slice(PADC, PADC + NC)
                    shf = slice(PADC - sh, PADC - sh + NC)
                    if pi_ % POOL_EVERY == POOL_EVERY - 1:
                        def stt(out, in0, sc, in1, r, w):
                            P.op("pool", lambda e: e.tensor_scalar(out=ptmp[:, :], in0=in0, scalar1=sc, scalar2=None, op0=ALU.mult),
                                 reads=r, writes=["ptmp"])
                            P.op("pool", lambda e: e.tensor_tensor(out=out, in0=ptmp[:, :], in1=in1, op=ALU.add),
                                 reads=r + ["ptmp"], writes=w)
                    else:
                        stt = lambda out, in0, sc, in1, r, w: P.op("dve", lambda e: e.scalar_tensor_tensor(
                            out=out, in0=in0, scalar=sc, in1=in1, op0=ALU.mult, op1=ALU.add), reads=r, writes=w)
                    stt(Y[:, 0, cur], X[:, 0, shf], a_re, X[:, 0, cur], [xr + "r", "apw"], [yr + "r"])
                    stt(Y[:, 1, cur], X[:, 1, shf], a_re, X[:, 1, cur], [xr + "i", "apw"], [yr + "i"])
                    cur2 = slice(PADC + sh, PADC + NC)
                    shf2 = slice(PADC, PADC + NC - sh)
                    stt(Y[:, 0, cur2], X[:, 1, shf2], na_im, Y[:, 0, cur2], [xr + "i", yr + "r", "apw"], [yr + "r"])
                    stt(Y[:, 1, cur2], X[:, 0, shf2], a_im, Y[:, 1, cur2], [xr + "r", yr + "i", "apw"], [yr + "i"])
                    src = 1 - src
                fin = 2 * es_ + src
                P.op("act", lambda e: e.copy(out=Sbf[us][:, q3, :, 1:1 + NC], in_=EA[fin][:, :, PADC:PADC + NC]),
                     reads=[f"E{fin}r", f"E{fin}i"], writes=[f"S{us}"])

            ti = [0]

            def emit_AD(k):
                npq = tiles[k][1]
                R = 32 * npq
                us = k % 2
                for i in range(TC):
                    bk = i % 4
                    for j in range(i + 1):
                        P.op("pe", lambda e, i=i, j=j, bk=bk: e.matmul(
                            psY[0:R, bk, 0:NC], lhsT=Kb[0:R, k, i - j, 0:R], rhs=ut8[us][0:R, j, :],
                            start=(j == 0), stop=False), reads=["Kb", f"ut8{us}"], writes=[f"psY{bk}"])
                    for q3 in range(npq):
                        q = 3 * k + q3
                        for ri in range(2):
                            last = (q3 == npq - 1 and ri == 1)
                            P.op("pe", lambda e, q3=q3, q=q, i=i, ri=ri, bk=bk, last=last: e.matmul(
                                psY[32 * q3:32 * q3 + 32, bk, 0:NC], lhsT=CE[:, q, i, ri, :], rhs=Sbf[us][:, q3, ri, 0:NC],
                                start=False, stop=(ri == 1)), reads=["CE", f"S{us}"], writes=[f"psY{bk}"])
                    P.op("act", lambda e, i=i, bk=bk: e.activation(
                        out=go[0][0:R, i::TC], in_=psY[0:R, bk, 0:NC], func=AF.Gelu_apprx_tanh),
                        reads=[f"psY{bk}"], writes=[f"go_{i}"])
                P.dma("pool", Gd[96 * k:96 * k + R, :], go[0][0:R, :], reads=[f"go_{i}" for i in range(TC)], writes=[f"Gd{k}"])

            emit_B(0)
            if len(pairs) > 1:
                emit_B(1)
            for pi_ in range(len(pairs)):
                if pi_ + 2 < len(pairs):
                    emit_B(pi_ + 2)
                emit_LOG(pi_)
                if after_norm is not None:
                    after_norm(3)
                k, q3 = pairs[pi_]
                if q3 == tiles[k][1] - 1:
                    emit_AD(k)
            P.barrier()

        ph.close()
        if pf is not None:
            pf.alloc()
        with ExitStack() as p3:
            wg = p3.enter_context(nc.sbuf_tensor("s3_wg", [128, KT, 2 * D], BF16))
            xb = [p3.enter_context(nc.sbuf_tensor(f"s3_xb{i}", [128, KT, TB], F32)) for i in range(2)]
            gb = [p3.enter_context(nc.sbuf_tensor(f"s3_gb{i}", [128, KT, TB], BF16)) for i in range(2)]
            sg = [p3.enter_context(nc.sbuf_tensor(f"s3_sg{i}", [128, TB], F32)) for i in range(2)]
            psA = p3.enter_context(nc.psum_tensor("s3_psA", [128, 2, 512], F32))
            psG = p3.enter_context(nc.psum_tensor("s3_psG", [128, 2, 512], F32))
            P.dma("sp", wg[:, :, :], w_glu_d[layer].rearrange("(k p) c -> p k c", p=128), reads=WC[f"glu{layer}"], writes=["wg"])
            wgr = ["wg"]
            for b in range(NB):
                s = b % 2
                t0 = b * TB
                P.dma("sp", xb[s][:, :, :], HT[:, :, t0:t0 + TB].rearrange("k p t -> p k t"), writes=[f"gx{s}"])
                P.dma("sp", gb[s][:, :, :], Gd[:, t0:t0 + TB].rearrange("(k p) t -> p k t", p=128), writes=[f"gb{s}"])
                if pf is not None and b % 2 == 1:
                    pf.load_some(1)
                for m in range(KT):
                    bk = m % 2
                    for kt in range(KT):
                        P.op("pe", lambda e, m=m, kt=kt, bk=bk, s=s: e.matmul(
                            psA[:, bk, :], lhsT=wg[:, kt, m * 128:(m + 1) * 128], rhs=gb[s][:, kt, :],
                            start=(kt == 0), stop=(kt == KT - 1)), reads=wgr + [f"gb{s}"], writes=[f"psA{bk}"])
                    for kt in range(KT):
                        P.op("pe", lambda e, m=m, kt=kt, bk=bk, s=s: e.matmul(
                            psG[:, bk, :], lhsT=wg[:, kt, D + m * 128:D + (m + 1) * 128], rhs=gb[s][:, kt, :],
                            start=(kt == 0), stop=(kt == KT - 1)), reads=wgr + [f"gb{s}"], writes=[f"psG{bk}"])
                    P.op("act", lambda e, bk=bk: e.activation(out=sg[bk][:, :], in_=psG[:, bk, :], func=AF.Sigmoid),
                         reads=[f"psG{bk}"], writes=[f"sg{bk}"])
                    P.op("dve", lambda e, bk=bk: e.tensor_tensor(out=sg[bk][:, :], in0=sg[bk][:, :], in1=psA[:, bk, :], op=ALU.mult),
                         reads=[f"psA{bk}", f"sg{bk}"], writes=[f"sg{bk}"])
                    P.op("dve", lambda e, m=m, bk=bk, s=s: e.tensor_tensor(out=xb[s][:, m, :], in0=xb[s][:, m, :], in1=sg[bk][:, :], op=ALU.add),
                         reads=[f"sg{bk}", f"gx{s}"], writes=[f"gx{s}o{m}"])
                P.dma("pool", HT[:, :, t0:t0 + TB].rearrange("k p t -> p k t"), xb[s][:, :, :],
                      reads=[f"gx{s}o{m}" for m in range(KT)] + [f"gx{s}"], writes=[f"HTs{b}"])
            P.barrier()


def build(L, phases=("in", "final")):
    nc = bass.Bass("TRN2", target_bir_lowering=False)
    NB = L // TB
    x_d = nc.dram_tensor("x", [L, D], F32, kind="ExternalInput").ap()
    ident_d = nc.dram_tensor("ident", [128, 128], F32, kind="ExternalInput").ap()
    gvec_d = nc.dram_tensor("gvec", [128, 10, KT], F32, kind="ExternalInput").ap()
    out_d = nc.dram_tensor("out", [L, D], F32, kind="ExternalOutput").ap()
    w_in_f = nc.dram_tensor("w_ffn_in", [4, D, 2 * FF], F32, kind="ExternalInput").ap()
    w_out_f = nc.dram_tensor("w_ffn_out", [4, FF, D], F32, kind="ExternalInput").ap()
    w_in_d = nc.dram_tensor("w_in_b", [4, D, 2 * FF], BF16, kind="Internal").ap()
    w_out_d = nc.dram_tensor("w_out_b", [4, FF, D], BF16, kind="Internal").ap()
    cw_d = nc.dram_tensor("ffn_cw", [4, 128, 2 * NFT, 4], F32, kind="ExternalInput").ap()
    w_kvf_f = nc.dram_tensor("w_kvf", [D, 2064], F32, kind="ExternalInput").ap()
    w_kvf_d = nc.dram_tensor("w_kvf_b", [D, 2064], BF16, kind="Internal").ap()
    bf_d = nc.dram_tensor("b_f", [NH, 1], F32, kind="ExternalInput").ap()
    w_q_f = nc.dram_tensor("w_q", [2, D, D], F32, kind="ExternalInput").ap()
    w_o_f = nc.dram_tensor("w_o", [2, D, D], F32, kind="ExternalInput").ap()
    w_q_d = nc.dram_tensor("w_q_b", [2, D, D], BF16, kind="Internal").ap()
    w_o_d = nc.dram_tensor("w_o_b", [2, D, D], BF16, kind="Internal").ap()
    tri_d = nc.dram_tensor("tri", [128, 128], F32, kind="ExternalInput").ap()
    s5p_d = nc.dram_tensor("s5p", [2, 128, NPAIR, 3], F32, kind="ExternalInput").ap()
    s5b_d = nc.dram_tensor("s5b", [2, 128, NPAIR, 2, 16], F32, kind="ExternalInput").ap()
    s5c_d = nc.dram_tensor("s5c", [2, 128, NPAIR, 2, 16], F32, kind="ExternalInput").ap()
    dv_d = nc.dram_tensor("s5dv", [2, 96, NT96], F32, kind="ExternalInput").ap()
    w_glu_f = nc.dram_tensor("w_glu", [2, D, 2 * D], F32, kind="ExternalInput").ap()
    w_glu_d = nc.dram_tensor("w_glu_b", [2, D, 2 * D], BF16, kind="Internal").ap()
    UTd = nc.dram_tensor("UTd", [D, L], BF16, kind="Internal").ap()
    Gd = nc.dram_tensor("Gd", [D, L], BF16, kind="Internal").ap()
    KTd = nc.dram_tensor("KTd", [NH, KA, L], BF16, kind="Internal").ap()
    QAd = nc.dram_tensor("QAd", [NH, 6, L], BF16, kind="Internal").ap()
    VTd = nc.dram_tensor("VTd", [L // 128, 128, NH, HD + 1], BF16, kind="Internal").ap()
    OTd = nc.dram_tensor("OTd", [NH, HD, L], BF16, kind="Internal").ap()
    QTd = nc.dram_tensor("QTd", [NH, HD, L], BF16, kind="Internal").ap()
    HT = nc.dram_tensor("HT", [KT, 128, L], F32, kind="Internal").ap()

    nc_raw = nc
    with ExitStack() as es:
        P = Prog(nc, es)
        nc = NCW(nc_raw)
        ident = es.enter_context(nc.sbuf_tensor("ident_sb", [128, 128], F32))
        ones = es.enter_context(nc.sbuf_tensor("ones_sb", [128, 128], BF16))
        gs = es.enter_context(nc.sbuf_tensor("gs_sb", [128, 10, KT], F32))
        P.dma("sp", ident[:, :], ident_d[:, :], writes=["ident"])
        P.dma("sp", gs[:, :, :], gvec_d[:, :, :], writes=["gs"])
        P.op("dve", lambda e: e.memset(ones[:, :], 1.0), writes=["ones"])
        epsb = es.enter_context(nc.sbuf_tensor("epsb_sb", [128, 1], F32))
        P.op("dve", lambda e: e.memset(epsb[:, :], EPS), writes=["epsb"])

        if "in" in phases:
            with ExitStack() as ph:
                xt = [ph.enter_context(nc.sbuf_tensor(f"in_xt{i}", [128, 4, D], F32)) for i in range(2)]
                ht = [ph.enter_context(nc.sbuf_tensor(f"in_ht{i}", [128, KT, TB], F32)) for i in range(2)]
                ps = ph.enter_context(nc.psum_tensor("in_ps", [128, 4, TB], F32))
                for b in range(NB):
                    s = b % 2
                    P.dma("sp", xt[s][:, :, :], x_d[b * TB:(b + 1) * TB, :].rearrange("(j p) d -> p j d", p=128),
                          writes=[f"xt{s}"])
                    for kt in range(KT):
                        bk = kt % 4
                        for j in range(4):
                            P.op("pe", lambda e, j=j, kt=kt, bk=bk, s=s: e.transpose(
                                out=ps[:, bk, j * 128:(j + 1) * 128], in_=xt[s][:, j, kt * 128:(kt + 1) * 128],
                                identity=ident[:, :]), reads=[f"xt{s}", "ident"], writes=[f"ps{bk}"])
                        ev = "act" if kt % 2 == 0 else "dve"
                        if ev == "act":
                            P.op("act", lambda e, kt=kt, bk=bk, s=s: e.copy(out=ht[s][:, kt, :], in_=ps[:, bk, :]),
                                 reads=[f"ps{bk}"], writes=[f"ht{s}_{kt}"])
                        else:
                            P.op("dve", lambda e, kt=kt, bk=bk, s=s: e.tensor_copy(out=ht[s][:, kt, :], in_=ps[:, bk, :]),
                                 reads=[f"ps{bk}"], writes=[f"ht{s}_{kt}"])
                    P.dma("pool", HT[:, :, b * TB:(b + 1) * TB].rearrange("k p t -> p k t"), ht[s][:, :, :],
                          reads=[f"ht{s}_{kt}" for kt in range(KT)], writes=[f"HT{b}"])
                P.barrier()

        cast_items = []

        def wcast(dst, src, name, nchunk, split=None):
            rows = dst.shape[0]
            step = rows // nchunk
            names = []
            for c in range(nchunk):
                d_, s_ = dst[c * step:(c + 1) * step], src[c * step:(c + 1) * step]
                if split:
                    d_ = d_.rearrange("r (c e) -> r c e", e=split)
                    s_ = s_.rearrange("r (c e) -> r c e", e=split)
                nm = f"WC:{name}:{c}"
                names.append(nm)
                cast_items.append(lambda d_=d_, s_=s_, nm=nm: P.dma("pool", d_, s_, writes=[nm], ring="wc"))
            WC[name] = names
        for l in range(4):
            if l < 2:
                wcast(w_glu_d[l], w_glu_f[l], f"glu{l}", 4, 1024)
            else:
                if l == 2:
                    wcast(w_kvf_d, w_kvf_f, "kvf", 4, 1032)
                wcast(w_q_d[l - 2], w_q_f[l - 2], f"q{l - 2}", 2)
                wcast(w_o_d[l - 2], w_o_f[l - 2], f"o{l - 2}", 2)
            wcast(w_in_d[l], w_in_f[l], f"in{l}", 8, 1408)
            wcast(w_out_d[l], w_out_f[l], f"out{l}", 4)

        def issue_casts(n=None):
            k = len(cast_items) if n is None else min(n, len(cast_items))
            for _ in range(k):
                cast_items.pop(0)()

        for layer in range(4):
            lstack = es.enter_context(ExitStack())
            pf = WinPrefetch(P, nc, w_in_d, layer, lstack) if f"ffn{layer}" in phases else None
            if f"s5{layer}" in phases:
                s5_phase(P, nc, L, layer, HT, UTd, Gd, s5p_d, s5b_d, s5c_d, dv_d, w_glu_d, ones, epsb, gs, ident, issue_casts, pf)
            issue_casts()
            if layer == 2 and "kv" in phases:
                kv_phase(P, nc, L, HT, w_kvf_d, bf_d, KTd, QAd, VTd, ones, epsb, gs)
            if f"fox{layer}" in phases:
                fox_phase(P, nc, L, layer, HT, w_q_d, w_o_d, KTd, QAd, VTd, OTd, QTd, tri_d, ones, epsb, gs, ident, pf)
            if f"ffn{layer}" in phases:
                ffn_phase(P, nc, L, layer, HT, w_in_d, w_out_d, cw_d, ones, epsb, gs, pf)
            lstack.close()

        if "final" in phases:
            with ExitStack() as ph:
                xb = [ph.enter_context(nc.sbuf_tensor(f"fx{i}", [128, KT, TB], F32)) for i in range(2)]
                sq = ph.enter_context(nc.sbuf_tensor("fsq", [128, KT, TB], BF16))
                rr = ph.enter_context(nc.sbuf_tensor("frr", [128, TB], F32))
                xn = ph.enter_context(nc.sbuf_tensor("fxn", [128, KT, TB], F32))
                ot = [ph.enter_context(nc.sbuf_tensor(f"fot{i}", [128, 4, D], F32)) for i in range(2)]
                pss = ph.enter_context(nc.psum_tensor("f_pss", [128, TB], F32))
                ps = ph.enter_context(nc.psum_tensor("f_ps", [128, 4, TB], F32))
                for b in range(NB):
                    s = b % 2
                    P.dma("sp", xb[s][:, :, :], HT[:, :, b * TB:(b + 1) * TB].rearrange("k p t -> p k t"),
                          reads=[f"HT{b}"], writes=[f"fx{s}x"])
                    rms_block(P, xb[s], f"fx{s}x", sq, rr, pss, ones, epsb, gs[:, 9, :], xn, "fxn", "fin_", TB)
                    for j in range(4):
                        for half in range(2):
                            bk = (j * 2 + half) % 4
                            for k4 in range(4):
                                kt = half * 4 + k4
                                P.op("pe", lambda e, j=j, kt=kt, bk=bk, k4=k4: e.transpose(
                                    out=ps[:, bk, k4 * 128:(k4 + 1) * 128], in_=xn[:, kt, j * 128:(j + 1) * 128],
                                    identity=ident[:, :]), reads=["fxn", "ident"], writes=[f"fps{bk}"])
                            if half == 0:
                                P.op("act", lambda e, j=j, bk=bk, s=s: e.copy(out=ot[s][:, j, 0:512], in_=ps[:, bk, :]),
                                     reads=[f"fps{bk}"], writes=[f"fot{s}_{j}_0"])
                            else:
                                P.op("dve", lambda e, j=j, bk=bk, s=s: e.tensor_copy(out=ot[s][:, j, 512:1024], in_=ps[:, bk, :]),
                                     reads=[f"fps{bk}"], writes=[f"fot{s}_{j}_1"])
                    P.dma("pool", out_d[b * TB:(b + 1) * TB, :].rearrange("(j p) d -> p j d", p=128), ot[s][:, :, :],
                          reads=[f"fot{s}_{j}_{h}" for j in range(4) for h in range(2)], writes=[f"out{b}"])
                P.barrier()
        print("instructions:", P.nins)
    return nc_raw


def make_gvec(g_mix, g_ffn, g_kv, g_final):
    allg = np.concatenate([g_mix, g_ffn, g_kv[None], g_final[None]], axis=0).astype(np.float32)
    return np.ascontiguousarray(allg.reshape(10, KT, 128).transpose(2, 0, 1))


def host_inputs(inp):
    f32 = lambda a: np.ascontiguousarray(np.asarray(a, dtype=np.float32))
    cwt = np.concatenate([np.asarray(inp["ffn_conv_w"], np.float32),
                          np.asarray(inp["ffn_conv_b"], np.float32)[:, None, :]], axis=1)
    cw = cwt.reshape(4, 4, 2 * NFT, 128).transpose(0, 3, 2, 1)
    def pairlay(a):
        a = np.asarray(a, np.float32)
        rest = a.shape[3:]
        a = a.reshape((2, NPAIR, 2, 64) + rest)
        a = np.moveaxis(a, 1, 3)
        return a.reshape((2, 128, NPAIR) + rest)
    ldt = np.broadcast_to(np.asarray(inp["log_dt"], np.float32)[:, :, None], (2, 64, 64))
    s5p = np.stack([pairlay(inp["lam_re"]), pairlay(inp["lam_im"]), pairlay(ldt)], axis=-1)
    s5b = np.stack([pairlay(inp["ssm_b_re"]), pairlay(inp["ssm_b_im"])], axis=3)
    ct = lambda a: np.asarray(a, np.float32).transpose(0, 1, 3, 2)
    s5c = np.stack([pairlay(ct(inp["ssm_c_re"])), pairlay(ct(inp["ssm_c_im"]))], axis=3)
    dpad = np.zeros((2, NT96 * 96), np.float32)
    dpad[:, :D] = np.asarray(inp["ssm_d"], np.float32)
    dv96 = dpad.reshape(2, NT96, 96).transpose(0, 2, 1)
    return {
        "ident": np.eye(128, dtype=np.float32),
        "gvec": make_gvec(np.asarray(inp["g_mix"]), np.asarray(inp["g_ffn"]), np.asarray(inp["g_kv"]),
                          np.asarray(inp["g_final"])),
        "w_ffn_in": f32(inp["w_ffn_in"]), "w_ffn_out": f32(inp["w_ffn_out"]), "ffn_cw": f32(cw),
        "w_kvf": f32(inp["w_kvf"]), "b_f": f32(np.asarray(inp["b_f"]).reshape(NH, 1)),
        "w_q": f32(inp["w_q"]), "w_o": f32(inp["w_o"]),
        "tri": np.triu(np.ones((128, 128), np.float32)),
        "s5p": f32(s5p), "s5b": f32(s5b), "s5c": f32(s5c), "s5dv": f32(dv96), "w_glu": f32(inp["w_glu"]),
    }


PHASES = ("in", "s50", "ffn0", "s51", "ffn1", "kv", "fox2", "ffn2", "fox3", "ffn3", "final")


def kernel(**inp):
    x = np.asarray(inp["x"], np.float32)
    B, L, _ = x.shape
    nc = build(L, phases=inp.get("_phases", PHASES))
    shared = host_inputs(inp)
    in_maps = [dict(shared, x=np.ascontiguousarray(x[c])) for c in range(B)]
    res = run_bass_kernel_spmd(nc, in_maps, core_ids=list(range(B)))
    return np.stack([r["out"] for r in res.results], axis=0)
```
